# Optimizing a Trainium2 kernel written in Bass

```python
import math
import jax, jax.numpy as jnp
from jax import lax
import numpy as np


D_MODEL = 1024
BATCH = 8
SEQ = 2048
DEPTH = 2
DEC_BATCH = 32
DEC_SEQ = 8
PAST_LEN = 8192
PAGE_SIZE = 128

N_A_LAYERS = DEPTH // 2
N_B_LAYERS = DEPTH - N_A_LAYERS
EPS = 1e-6
FFN_DIM = 2688
GLA_HEADS = 4
GLA_DK = D_MODEL // 2 // GLA_HEADS
GLA_DV = D_MODEL // GLA_HEADS
GLA_RANK = 16
GLA_TAU = 16.0
GLA_CHUNK = 64
GLA_QK = GLA_HEADS * GLA_DK
GLA_V = GLA_HEADS * GLA_DV
GLA_IN = 2 * GLA_QK + GLA_V + GLA_RANK + GLA_V
HEAD_DIM = 64
N_KV_HEADS = 4
HEADS_PER_GROUP = 4
DIL_GROUPS = ((128, 1), (512, 4), (2048, 16))
N_GROUPS = len(DIL_GROUPS)
N_Q_HEADS = N_GROUPS * HEADS_PER_GROUP
MAX_WINDOW = max(w for w, _ in DIL_GROUPS)
Q_BLOCK = 128
ROPE_THETA = 10000.0

kernel_name = 'yoco_gla_dilated_window_decoder_step'


def rmsnorm(x, g):
    xf = x.astype(jnp.float32)
    y = xf * lax.rsqrt(jnp.mean(xf * xf, axis=-1, keepdims=True) + EPS)
    return (y * g.astype(jnp.float32)).astype(x.dtype)


def swiglu_ffn(x, w_in, w_out):
    g, u = jnp.split(x @ w_in, 2, axis=-1)
    return (jax.nn.silu(g) * u) @ w_out


def rope(x, pos):
    half = HEAD_DIM // 2
    inv = ROPE_THETA ** (-jnp.arange(half, dtype=jnp.float32) / half)
    ang = pos.astype(jnp.float32)[:, None] * inv[None, :]
    cos = jnp.cos(ang)[None, :, None, :]
    sin = jnp.sin(ang)[None, :, None, :]
    xf = x.astype(jnp.float32)
    x1, x2 = xf[..., :half], xf[..., half:]
    return jnp.concatenate([x1 * cos - x2 * sin, x1 * sin + x2 * cos], axis=-1).astype(x.dtype)


def gla_recurrence(q, k, v, log_a, s0):
    b, n = q.shape[:2]
    c = math.gcd(n, GLA_CHUNK)
    nc = n // c

    def to_chunks(t):
        return jnp.moveaxis(t.reshape(b, nc, c, *t.shape[2:]), 1, 0)

    causal = jnp.tril(jnp.ones((c, c), dtype=bool))

    def step(state, inp):
        qc, kc, vc, gc = inp
        qf, kf, vf = qc.astype(jnp.float32), kc.astype(jnp.float32), vc.astype(jnp.float32)
        cum = jnp.cumsum(gc.astype(jnp.float32), axis=1)
        diff = cum[:, :, None] - cum[:, None, :]
        decay = jnp.exp(jnp.where(causal[None, :, :, None, None], diff, -jnp.inf))
        attn = jnp.einsum('bihd,bjhd,bijhd->bhij', qf, kf, decay)
        o = (jnp.einsum('bhij,bjhv->bihv', attn, vf)
             + jnp.einsum('bihd,bhdv->bihv', qf * jnp.exp(cum), state))
        last = cum[:, -1]
        k_dec = kf * jnp.exp(last[:, None] - cum)
        state = jnp.exp(last)[..., None] * state + jnp.einsum('bjhd,bjhv->bhdv', k_dec, vf)
        return state, o

    s_fin, o = lax.scan(step, s0.astype(jnp.float32),
                        (to_chunks(q), to_chunks(k), to_chunks(v), to_chunks(log_a)))
    o = jnp.moveaxis(o, 0, 1).reshape(b, n, GLA_HEADS, GLA_DV)
    return o, s_fin


def gla_mixer(h, w_in, w_g2, b_g, out_norm, w_out, s0):
    b, n, _ = h.shape
    proj = h @ w_in
    q, k, v, g_lr, r = jnp.split(proj, [GLA_QK, 2 * GLA_QK, 2 * GLA_QK + GLA_V,
                                        2 * GLA_QK + GLA_V + GLA_RANK], axis=-1)
    q = q.reshape(b, n, GLA_HEADS, GLA_DK) * (GLA_DK ** -0.5)
    k = k.reshape(b, n, GLA_HEADS, GLA_DK)
    v = v.reshape(b, n, GLA_HEADS, GLA_DV)
    log_a = jax.nn.log_sigmoid((g_lr @ w_g2 + b_g).astype(jnp.float32)) / GLA_TAU
    log_a = log_a.reshape(b, n, GLA_HEADS, GLA_DK)
    o, s_fin = gla_recurrence(q, k, v, log_a, s0)
    o = rmsnorm(o, out_norm).reshape(b, n, GLA_V).astype(h.dtype)
    y = (o * jax.nn.silu(r)) @ w_out
    return y, s_fin.astype(h.dtype)


def shared_kv(x, kv_norm, kv_w, k_norm, pos):
    b, n, _ = x.shape
    k, v = jnp.split(rmsnorm(x, kv_norm) @ kv_w, 2, axis=-1)
    k = rope(rmsnorm(k.reshape(b, n, N_KV_HEADS, HEAD_DIM), k_norm), pos)
    v = v.reshape(b, n, N_KV_HEADS, HEAD_DIM)
    return k, v


def dilated_window_attention(q, k_src, v_src, n_pad):
    b, n = q.shape[:2]
    qb = math.gcd(n, Q_BLOCK)
    nb = n // qb
    scale = HEAD_DIM ** -0.5
    q_blocks = jnp.moveaxis(q.reshape(b, nb, qb, N_GROUPS, HEADS_PER_GROUP, HEAD_DIM), 1, 0)

    def block(args):
        qblk, start = args
        rows = MAX_WINDOW + start + jnp.arange(qb)
        outs, lses = [], []
        for g, (win, dil) in enumerate(DIL_GROUPS):
            nk = win // dil + 1
            idx = rows[:, None] - dil * jnp.arange(nk)[None, :]
            valid = idx >= n_pad
            kg = jnp.take(k_src, idx, axis=1).astype(jnp.float32)
            vg = jnp.take(v_src, idx, axis=1).astype(jnp.float32)
            s = jnp.einsum('bqjd,bqkjd->bqjk', qblk[:, :, g].astype(jnp.float32), kg) * scale
            s = jnp.where(valid[None, :, None, :], s, -jnp.inf)
            m = jnp.max(s, axis=-1, keepdims=True)
            p = jnp.exp(s - m)
            den = jnp.sum(p, axis=-1, keepdims=True)
            outs.append(jnp.einsum('bqjk,bqkjd->bqjd', p, vg) / den)
            lses.append((m + jnp.log(den))[..., 0])
        o = jnp.stack(outs, axis=2)
        w = jax.nn.softmax(jnp.stack(lses, axis=2), axis=2)
        return o * w[..., None]

    out = lax.map(block, (q_blocks, jnp.arange(nb, dtype=jnp.int32) * qb))
    return jnp.moveaxis(out, 0, 1).reshape(b, n, N_Q_HEADS * HEAD_DIM)


def setup_inputs(seed: int = 0) -> dict:
    key = jax.random.key(seed)
    ks = jax.random.split(key, 19)

    def nrm(k, shape, scale):
        return jax.random.normal(k, shape, jnp.float32) * scale

    win_len = min(MAX_WINDOW, PAST_LEN)
    return {
        'x_prompt': nrm(ks[0], (BATCH, SEQ, D_MODEL), 1.0),
        'x_sample': nrm(ks[1], (DEC_BATCH, DEC_SEQ, D_MODEL), 1.0),
        'state_gla': nrm(ks[2], (N_A_LAYERS, DEC_BATCH, GLA_HEADS, GLA_DK, GLA_DV), 0.5),
        'cache_k_win': nrm(ks[3], (DEC_BATCH, win_len, N_KV_HEADS, HEAD_DIM), 1.0),
        'cache_v_win': nrm(ks[4], (DEC_BATCH, win_len, N_KV_HEADS, HEAD_DIM), 1.0),
        'norm_gains': 1.0 + nrm(ks[5], (DEPTH, 3, D_MODEL), 0.05),
        'ffn_w_in': nrm(ks[6], (DEPTH, 2, D_MODEL, 2 * FFN_DIM), D_MODEL ** -0.5),
        'ffn_w_out': nrm(ks[7], (DEPTH, 2, FFN_DIM, D_MODEL), FFN_DIM ** -0.5),
        'gla_w_in': nrm(ks[8], (N_A_LAYERS, D_MODEL, GLA_IN), D_MODEL ** -0.5),
        'gla_w_gate2': nrm(ks[9], (N_A_LAYERS, GLA_RANK, GLA_QK), GLA_RANK ** -0.5),
        'gla_b_gate': 2.0 + nrm(ks[10], (N_A_LAYERS, GLA_QK), 0.5),
        'gla_out_norm': 1.0 + nrm(ks[11], (N_A_LAYERS, GLA_DV), 0.05),
        'gla_w_out': nrm(ks[12], (N_A_LAYERS, GLA_V, D_MODEL), GLA_V ** -0.5),
        'kv_norm': 1.0 + nrm(ks[13], (D_MODEL,), 0.05),
        'kv_w': nrm(ks[14], (D_MODEL, 2 * N_KV_HEADS * HEAD_DIM), D_MODEL ** -0.5),
        'k_norm': 1.0 + nrm(ks[15], (HEAD_DIM,), 0.05),
        'attn_w_q': nrm(ks[16], (N_B_LAYERS, D_MODEL, N_Q_HEADS * HEAD_DIM), D_MODEL ** -0.5),
        'q_norm': 1.0 + nrm(ks[17], (N_B_LAYERS, HEAD_DIM), 0.05),
        'attn_w_out': nrm(ks[18], (N_B_LAYERS, N_Q_HEADS * HEAD_DIM, D_MODEL), (N_Q_HEADS * HEAD_DIM) ** -0.5),
    }


def reference(x_prompt, x_sample, state_gla, cache_k_win, cache_v_win, norm_gains, ffn_w_in, ffn_w_out,
              gla_w_in, gla_w_gate2, gla_b_gate, gla_out_norm, gla_w_out, kv_norm, kv_w, k_norm,
              attn_w_q, q_norm, attn_w_out):

    def run(x, pos0, s0_all, k_buf, v_buf):
        b, n, _ = x.shape
        pos = pos0 + jnp.arange(n, dtype=jnp.int32)
        new_states = []
        k_new = v_new = k_src = v_src = None
        n_pad = 0
        for layer in range(DEPTH):
            x = x + 0.5 * swiglu_ffn(rmsnorm(x, norm_gains[layer, 0]), ffn_w_in[layer, 0], ffn_w_out[layer, 0])
            h = rmsnorm(x, norm_gains[layer, 1])
            if layer < N_A_LAYERS:
                y, s_fin = gla_mixer(h, gla_w_in[layer], gla_w_gate2[layer], gla_b_gate[layer],
                                     gla_out_norm[layer], gla_w_out[layer], s0_all[layer])
                new_states.append(s_fin)
            else:
                bl = layer - N_A_LAYERS
                q = (h @ attn_w_q[bl]).reshape(b, n, N_Q_HEADS, HEAD_DIM)
                q = rope(rmsnorm(q, q_norm[bl]), pos).reshape(b, n, N_GROUPS, HEADS_PER_GROUP, HEAD_DIM)
                y = dilated_window_attention(q, k_src, v_src, n_pad).astype(x.dtype) @ attn_w_out[bl]
            x = x + y
            x = x + 0.5 * swiglu_ffn(rmsnorm(x, norm_gains[layer, 2]), ffn_w_in[layer, 1], ffn_w_out[layer, 1])
            if layer == N_A_LAYERS - 1:
                k_new, v_new = shared_kv(x, kv_norm, kv_w, k_norm, pos)
                n_pad = MAX_WINDOW - k_buf.shape[1]
                pad = jnp.zeros((b, n_pad, N_KV_HEADS, HEAD_DIM), x.dtype)
                k_src = jnp.concatenate([pad, k_buf.astype(x.dtype), k_new], axis=1)
                v_src = jnp.concatenate([pad, v_buf.astype(x.dtype), v_new], axis=1)
        return x, jnp.stack(new_states, axis=0), k_new, v_new

    s0_prompt = jnp.zeros((N_A_LAYERS, BATCH, GLA_HEADS, GLA_DK, GLA_DV), x_prompt.dtype)
    empty = jnp.zeros((BATCH, 0, N_KV_HEADS, HEAD_DIM), x_prompt.dtype)
    y_prompt, state_gla_prompt, k_p, v_p = run(x_prompt, 0, s0_prompt, empty, empty)
    y_sample, state_gla_sample, k_s, v_s = run(x_sample, PAST_LEN, state_gla, cache_k_win, cache_v_win)
    keep = min(MAX_WINDOW, SEQ)
    k_win_prompt = k_p[:, SEQ - keep:]
    v_win_prompt = v_p[:, SEQ - keep:]
    return (y_prompt, y_sample, state_gla_prompt, state_gla_sample, k_win_prompt, v_win_prompt, k_s, v_s)
```

```python
import os
import numpy as np
import concourse.bass as bass
import concourse.mybir as mybir
from concourse.bass_utils import run_bass_kernel_spmd

F32 = mybir.dt.float32
BF16 = mybir.dt.bfloat16
AF = mybir.ActivationFunctionType
ALU = mybir.AluOpType
AX = mybir.AxisListType
ENGS = ("pe", "act", "dve", "pool", "sp")
EPS = 1e-6
ds = bass.ds


def sl_(start, n, step=1):
    return slice(start, start + (n - 1) * step + 1, step)


class Sched:
    def __init__(self, nc):
        self.nc = nc
        self.sem = {e: nc.alloc_semaphore("prog_" + e) for e in ENGS}
        self.cnt = {e: 0 for e in ENGS}
        self.waited = {e: {} for e in ENGS}
        self.ops = {e: [] for e in ENGS}
        self.lastw = {}
        self.readers = {}
        self.semobj = {self.sem[e].name: self.sem[e] for e in ENGS}
        self.dma_sems = {}

    def _need(self, eng, tok, waits):
        if tok is None:
            return
        sname, val, peng = tok
        if peng == eng and peng == "pe":
            return
        if self.waited[eng].get(sname, 0) >= val:
            return
        waits[sname] = max(waits.get(sname, 0), val)

    def op(self, eng, fn, reads=(), writes=(), dma=None):
        psr = [r for r in reads if isinstance(r, tuple) and r[0] in ("ps", "psbf")]
        if psr:
            reads = [r for r in reads if r not in psr]
            writes = list(writes) + psr
        waits = {}
        for r in reads:
            self._need(eng, self.lastw.get(r), waits)
        for w in writes:
            self._need(eng, self.lastw.get(w), waits)
            for t in self.readers.get(w, ()):
                self._need(eng, t, waits)
        for s, v in waits.items():
            self.waited[eng][s] = v
        if dma is None:
            self.cnt[eng] += 1
            tok = (self.sem[eng].name, self.cnt[eng], eng)
            inc = (self.sem[eng], 1)
        else:
            if dma not in self.dma_sems:
                s = self.nc.alloc_semaphore("dma_" + dma)
                self.dma_sems[dma] = [s, 0]
                self.semobj[s.name] = s
            ent = self.dma_sems[dma]
            ent[1] += 16
            tok = (ent[0].name, ent[1], "dma:" + dma)
            inc = (ent[0], 16)
        self.ops[eng].append((list(waits.items()), fn, inc))
        for r in reads:
            self.readers.setdefault(r, []).append(tok)
        for w in writes:
            self.lastw[w] = tok
            self.readers[w] = []
        return tok

    def barrier(self, skip=("ws", "vg")):
        for e in ENGS:
            waits = []
            for o in ENGS:
                if o != e and self.cnt[o] > self.waited[e].get(self.sem[o].name, 0):
                    waits.append((self.sem[o].name, self.cnt[o]))
                    self.waited[e][self.sem[o].name] = self.cnt[o]
            for name, (sm, c) in self.dma_sems.items():
                if name.startswith(skip):
                    continue
                if c > self.waited[e].get(sm.name, 0):
                    waits.append((sm.name, c))
                    self.waited[e][sm.name] = c
            if waits:
                self.ops[e].append((waits, None, None))

    def final_wait(self, eng="sp"):
        waits = []
        for e in ENGS:
            if e != eng and self.cnt[e] > 0:
                waits.append((self.sem[e].name, self.cnt[e]))
        for name, (s, c) in self.dma_sems.items():
            if c > 0:
                waits.append((s.name, c))
        self.ops[eng].append((waits, None, None))

    def emit(self):
        nc = self.nc
        with nc.Block() as block:
            def mk(ename):
                def body(e):
                    for waits, fn, inc in self.ops[ename]:
                        for s, v in waits:
                            e.wait_ge(self.semobj[s], v)
                        if fn is not None:
                            ins = fn(e)
                            ins.then_inc(inc[0], inc[1])
                return body
            block.tensor(mk("pe"))
            block.scalar(mk("act"))
            block.vector(mk("dve"))
            block.gpsimd(mk("pool"))
            block.sync(mk("sp"))


C_ID, C_LE, C_GE, C_SCP, C_SCS, C_MS, C_SQ, C_SM, C_MN, C_ONE, C_BD, C_RM, C_END = 0, 128, 256, 384, 640, 672, 704, 708, 1116, 1212, 1340, 1468, 1596
DIL = ((128, 1), (512, 4), (2048, 16))


def build_consts():
    c = np.zeros((128, C_END), np.float32)
    r = np.arange(128)
    c[:, C_ID:C_ID + 128] = np.eye(128)
    c[:, C_LE:C_LE + 128] = (r[:, None] <= r[None, :])
    c[:, C_GE:C_GE + 128] = (r[:, None] >= r[None, :])
    c[:, C_SCP:C_SCP + 256] = 1.0
    c[:, C_SCP] = 0.0
    c[:, C_SCP + 128] = 0.0
    c[:, C_SCS:C_SCS + 32] = 1.0
    c[:, C_SCS:C_SCS + 32:8] = 0.0
    t = np.arange(32)
    c[:32, C_MS:C_MS + 32] = ((t[:, None] // 8 == t[None, :] // 8) & (t[:, None] <= t[None, :]))
    for s in range(4):
        c[:32, C_SQ + s] = (t // 8 == s)
    for rt in range(17):
        for g, (win, dil) in enumerate(DIL):
            for tq in range(8):
                col = C_SM + rt * 24 + g * 8 + tq
                if rt < 16:
                    R = rt * 128 + r
                    d = 2048 + tq - R
                    c[:, col] = ((d % dil == 0) & (d >= 0) & (d <= win))
    for s in range(4):
        for g, (win, dil) in enumerate(DIL):
            for tq in range(8):
                col = C_MN + s * 24 + g * 8 + tq
                d = tq - (t % 8)
                c[:32, col] = ((t // 8 == s) & (d >= 0) & (d % dil == 0))
    c[:, C_ONE:C_ONE + 128] = 1.0
    c[:, C_BD:C_BD + 128] = (r[:, None] // 64 == r[None, :] // 64)
    for m in range(128):
        if m % 64 < 32:
            c[m + 32, C_RM + m] = -1.0
        else:
            c[m - 32, C_RM + m] = 1.0
    return c


def build_rope():
    half = 32
    inv = (10000.0 ** (-np.arange(half, dtype=np.float64) / half)).astype(np.float32)
    pos = np.concatenate([np.arange(2048), np.tile(8192 + np.arange(8), 4)]).astype(np.float32)
    ang = (inv[np.arange(128) % 32][:, None] * pos[None, :]).astype(np.float32).astype(np.float64)
    return np.stack([np.cos(ang), np.sin(ang)], axis=0).astype(np.float32)


def build(stage=99, phases=None):
    nc = bass.Bass("TRN2", target_bir_lowering=False)

    def din(name, shape):
        return nc.dram_tensor(name, list(shape), F32, kind="ExternalInput").ap()

    def dout(name, shape):
        return nc.dram_tensor(name, list(shape), F32, kind="ExternalOutput").ap()

    xp = din("xp", [2048, 1024]); xs = din("xs", [32, 1024]); st_in = din("st", [16, 128, 256])
    ck = din("ck", [4, 2048, 256]); cv = din("cv", [4, 2048, 256])
    small = din("small", [64, 128])
    knq = din("knq", [2, 64])
    ffn_in = din("ffn_w_in", [2, 2, 1024, 5376]); ffn_out = din("ffn_w_out", [2, 2, 2688, 1024])
    gla_in = din("gla_w_in", [1024, 3088]); gla_g2 = din("gla_w_gate2", [16, 512]); gla_out = din("gla_w_out", [1024, 1024])
    kv_w = din("kv_w", [1024, 512]); wq = din("attn_w_q", [1024, 768]); wo = din("attn_w_out", [768, 1024])
    consts = din("consts", [128, C_END]); rope = din("rope", [2, 128, 2080])
    y_p = dout("y_p", [2048, 1024]); y_s = dout("y_s", [32, 1024])
    stp_o = dout("stp", [4, 128, 256]); sts_o = dout("sts", [16, 128, 256])
    kp_o = dout("kp", [2048, 256]); vp_o = dout("vp", [2048, 256]); ks_o = dout("ks", [32, 256]); vs_o = dout("vs", [32, 256])

    S = Sched(nc)
    A = nc.alloc_sbuf_tensor
    TPM = 1056
    xT = A("xT", [128, 8, TPM], F32)
    hT = A("hT", [128, 8, TPM], BF16)
    NSLOT = 4
    wsl = A("wsl", [128, NSLOT, 4096], BF16)
    CF = A("CF", [128, C_END], F32)
    CB = A("CB", [128, C_END], BF16)
    G = A("G", [128, 64], F32)
    negb = A("negb", [128, 4], F32)
    KNQ = A("KNQ", [128, 2, 64], F32)
    wg2 = A("wg2", [16, 512], BF16)
    Sst = A("Sst", [128, 4, 256], F32)
    Sbf = A("Sbf", [128, 4, 256], BF16)
    KT = A("KT", [128, 2, 2048], BF16)
    KTs = A("KTs", [128, 2, 32], BF16)
    Vg = A("Vg", [128, 3, 16, 256], BF16)
    Vsn = A("Vsn", [32, 256], BF16)
    zer = A("zer", [128, 512], BF16)
    SCRW = 19100
    scr = A("scr", [128, SCRW], F32)
    PS = [nc.alloc_psum_tensor(f"ps{b}", [128, 512], F32) for b in range(8)]
    NB = 5
    LL = 5
    PBFS = [PS[6][:, :].bitcast(BF16), PS[7][:, :].bitcast(BF16)]
    bank_ctr = [0]

    bank_ring = [list(range(NB))]

    def newbank():
        ring = bank_ring[0]
        b = ring[bank_ctr[0] % len(ring)]
        bank_ctr[0] += 1
        return b

    class Carver:
        def __init__(self):
            self.off = 0

        def f32(self, shape):
            n = int(np.prod(shape[1:]))
            v = scr[:shape[0], self.off:self.off + n]
            self.off += n
            assert self.off <= SCRW, (self.off, SCRW)
            return v if len(shape) == 2 else v.rearrange(_pat(len(shape)), **_dims(shape))

        def bf(self, shape):
            n = int(np.prod(shape[1:]))
            w = (n + 1) // 2
            v = scr[:shape[0], self.off:self.off + w].bitcast(BF16)[:, 0:n]
            self.off += w
            assert self.off <= SCRW, (self.off, SCRW)
            return v if len(shape) == 2 else v.rearrange(_pat(len(shape)), **_dims(shape))

    def _pat(nd):
        names = "abcd"[:nd - 1]
        return "p (" + " ".join(names) + ") -> p " + " ".join(names)

    def _dims(shape):
        names = "abcd"[:len(shape) - 1]
        return {names[i]: int(shape[i + 1]) for i in range(len(shape) - 2)}

    SP, PL, ACT, DVE, PE = "sp", "pool", "act", "dve", "pe"
    S.op(SP, lambda e: e.dma_start(out=CF[:], in_=consts[:, :]), writes=["CF"], dma="c0")
    S.op(PL, lambda e: e.dma_start(out=CB[:], in_=consts[:, :]), writes=["CB"], dma="c1")
    S.op(PL, lambda e: e.dma_start(out=wg2[:], in_=gla_g2[:, :]), writes=["wg2"], dma="c2")
    S.op(SP, lambda e: e.dma_start(out=KNQ[:, 0, :], in_=knq[0, :].partition_broadcast(128)), writes=["KNQ0"], dma="c3")
    S.op(SP, lambda e: e.dma_start(out=KNQ[:, 1, :], in_=knq[1, :].partition_broadcast(128)), writes=["KNQ1"], dma="c3b")
    S.op(PL, lambda e: e.memset(zer[:], 0.0), writes=["zer"])
    S.op(PL, lambda e: e.memset(Sst[:], 0.0), writes=[("Sst", 0), ("Sst", 1)])
    S.op(PL, lambda e: e.memset(Sbf[:], 0.0), writes=[("Sbf", 0), ("Sbf", 1)])
    cv0 = Carver()
    smt = cv0.f32([64, 128])
    S.op(SP, lambda e: e.dma_start(out=smt, in_=small[:, :]), writes=["smt"], dma="c4")
    S.op(PE, lambda e: e.transpose(out=PS[0][:, 0:64], in_=smt, identity=CF[0:64, C_ID:C_ID + 64]),
         reads=["smt", "CF"], writes=[("ps", 0)])
    S.op(DVE, lambda e: e.tensor_copy(out=G[:], in_=PS[0][:, 0:64]), reads=[("ps", 0)], writes=["G"])
    S.op(DVE, lambda e: e.tensor_scalar(out=negb[:], in0=G[:, 56:60], scalar1=-1.0, scalar2=None, op0=ALU.mult),
         reads=["G"], writes=["negb"])
    ident = CF[:, C_ID:C_ID + 128]
    identb = CB[:, C_ID:C_ID + 128]
    onesb = CB[:, C_ONE:C_ONE + 128]
    S.barrier()

    steps = []

    def step(pieces, fn, hold=0):
        steps.append((pieces, fn, hold))

    rs_state = {}

    def _rs_init():
        if "load_idx" in rs_state:
            return
        rs_state["load_idx"] = [i for i, (p, _, _) in enumerate(steps) if p is not None]
        rs_state["holds"] = [steps[i][2] for i in rs_state["load_idx"]]
        rs_state["issued"] = 0

    def _issue_for(q):
        load_idx, holds = rs_state["load_idx"], rs_state["holds"]
        while rs_state["issued"] < len(load_idx):
            k = rs_state["issued"]
            if k > q + NSLOT - 1:
                break
            if k >= NSLOT and (k - NSLOT + holds[k - NSLOT]) >= q:
                break
            sl = (slot_base[0] + k) % NSLOT
            pieces = steps[load_idx[k]][0]
            for pi, (osl, src) in enumerate(pieces(wsl[:, sl, :])):
                S.op(PL, (lambda e, o=osl, s_=src: e.dma_start(out=o, in_=s_)),
                     writes=[("ws", sl)], dma=f"ws{sl}_{pi}")
            rs_state["issued"] += 1

    def prefetch_steps():
        _rs_init()
        _issue_for(0)

    def run_steps():
        _rs_init()
        q = 0
        for i, (p, fn, hold) in enumerate(steps):
            if p is not None:
                _issue_for(q)
                assert rs_state["issued"] > q
                sl = (slot_base[0] + q) % NSLOT
                fn(wsl[:, sl, :], ("ws", sl))
                q += 1
            else:
                fn(None, None)
        slot_base[0] = (slot_base[0] + q) % NSLOT
        steps.clear()
        rs_state.clear()

    slot_base = [0]

    evac_ctr = [0]

    def evac_eng():
        evac_ctr[0] += 1
        return ACT if evac_ctr[0] % 2 else DVE

    def copy_op(eng, out, in_, reads, writes):
        if eng == ACT:
            S.op(ACT, lambda e: e.copy(out=out, in_=in_), reads=reads, writes=writes)
        else:
            S.op(DVE, lambda e: e.tensor_copy(out=out, in_=in_), reads=reads, writes=writes)

    def mm_group(out, pairs, reads, writes, skip=False, start=True, stop=True):
        def fn(e):
            ins = None
            n = len(pairs)
            for i, (l, r) in enumerate(pairs):
                ins = e.matmul(out, lhsT=l, rhs=r, start=(start and i == 0), stop=(stop and i == n - 1),
                               skip_group_check=skip)
            return ins
        S.op(PE, fn, reads=reads, writes=writes)

    def subtiles(p):
        return [(0, 512), (512, 512), (1024, 32)] if p == 0 else [(0, 512), (512, 512)]

    def toktiles(p):
        tl = [(tt * 128, 128, "p", p * 1024 + tt * 128) for tt in range(8)]
        if p == 0:
            tl.append((1024, 32, "s", 0))
        return tl

    def load_x(p):
        cvx = Carver()
        xin = [cvx.f32([128, 1024]) for _ in range(2)]
        for ti, (c0, n, kind, r0) in enumerate(toktiles(p)):
            sl = ti % 2
            src = xp[r0:r0 + n, :] if kind == "p" else xs[:, :]
            S.op(SP, (lambda e, sl=sl, n=n, src=src: e.dma_start(out=xin[sl][0:n, :], in_=src)),
                 writes=[("xin", sl)], dma=f"xin{sl}")
            for hb in range(2):
                b = newbank()
                pv = PS[b][:, :].rearrange("p (a t) -> p a t", a=4)

                def fn(e, sl=sl, n=n, hb=hb, pv=pv):
                    ins = None
                    for a in range(4):
                        fc = hb * 4 + a
                        ins = e.transpose(out=pv[:, a, 0:n], in_=xin[sl][0:n, fc * 128:(fc + 1) * 128], identity=ident[0:n, 0:n])
                    return ins
                S.op(PE, fn, reads=[("xin", sl), "CF"], writes=[("ps", b)])
                copy_op(evac_eng(), xT[:, hb * 4:hb * 4 + 4, c0:c0 + n], pv[:, :, 0:n], [("ps", b)], [("xT", c0)])

    def store_y(p):
        cvx = Carver()
        yo = [cvx.f32([128, 1024]) for _ in range(2)]
        for ti, (c0, n, kind, r0) in enumerate(toktiles(p)):
            sl = ti % 2
            for hb in range(2):
                b = newbank()
                pv = PS[b][:, :].rearrange("p (a t) -> p a t", a=4)

                def fn(e, n=n, hb=hb, pv=pv, c0=c0):
                    ins = None
                    for a in range(4):
                        fc = hb * 4 + a
                        ins = e.transpose(out=pv[0:n, a, :], in_=xT[:, fc, c0:c0 + n], identity=ident)
                    return ins
                S.op(PE, fn, reads=[("xT", c0 // 128 * 128 if n == 128 else c0), "CF"], writes=[("ps", b)])
                copy_op(evac_eng(), yo[sl][0:n, hb * 512:(hb + 1) * 512].rearrange("p (a t) -> p a t", a=4), pv[0:n, :, :],
                        [("ps", b)], [("yo", sl, hb)])
            dst = y_p[r0:r0 + n, :] if kind == "p" else y_s[:, :]
            S.op(SP, (lambda e, sl=sl, n=n, dst=dst: e.dma_start(out=dst, in_=yo[sl][0:n, :])),
                 reads=[("yo", sl, 0), ("yo", sl, 1)], dma=f"yo{sl}")

    def xkeys(c0, n):
        return [("xT", c) for c in range(c0, c0 + n, 128)] if n >= 128 else [("xT", c0)]

    def hkeys(c0, n):
        return [("hT", c) for c in range(c0, c0 + n, 128)] if n >= 128 else [("hT", c0)]

    def norm_to_hT(p, gcol, cvn, piece=512):
        sqb = cvn.bf([128, 8, piece])
        rs = cvn.f32([128, piece])
        pieces = []
        for (c0, n) in subtiles(p):
            for o in range(0, n, piece):
                pieces.append((c0 + o, min(piece, n - o)))

        def emit():
            for (c0, n) in pieces:
                S.op(ACT, (lambda e, c0=c0, n=n: e.activation(out=sqb[:, :, 0:n], in_=xT[:, :, c0:c0 + n], func=AF.Square)),
                     reads=xkeys(c0, n), writes=["sqb"])
                b = newbank()
                mm_group(PS[b][:, 0:n], [(onesb, sqb[:, fc, 0:n]) for fc in range(8)], reads=["sqb", "CB"], writes=[("ps", b)])
                S.op(ACT, (lambda e, b=b, n=n: e.activation(out=rs[:, 0:n], in_=PS[b][:, 0:n], func=AF.Ln, scale=1.0 / 1024, bias=EPS)),
                     reads=[("ps", b)], writes=["rs"])
                S.op(ACT, (lambda e, n=n: e.activation(out=rs[:, 0:n], in_=rs[:, 0:n], func=AF.Exp, scale=-0.5)), reads=["rs"], writes=["rs"])
                for fc in range(8):
                    S.op(DVE, (lambda e, fc=fc, c0=c0, n=n: e.scalar_tensor_tensor(
                        out=hT[:, fc, c0:c0 + n], in0=xT[:, fc, c0:c0 + n], scalar=G[:, gcol + fc:gcol + fc + 1], in1=rs[:, 0:n],
                        op0=ALU.mult, op1=ALU.mult)), reads=xkeys(c0, n) + ["rs", "G"], writes=hkeys(c0, n))
        return emit

    def ffn(p, l, f):
        cvf = Carver()
        emit_norm = norm_to_hT(p, (l * 3 + (0 if f == 0 else 2)) * 8, cvf)
        act = cvf.bf([128, 21, TPM])
        sg = [cvf.f32([128, 512]) for _ in range(2)]
        w_in = ffn_in[l, f].rearrange("(kt p) c -> p kt c", p=128)
        w_out = ffn_out[l, f].rearrange("(kt p) c -> p kt c", p=128)
        subs = subtiles(p)
        for j in range(21):
            def pieces(slot, j=j):
                v = slot[:, 0:2048].rearrange("p (kt c) -> p kt c", kt=8)
                return [(v[:, :, 0:128], w_in[:, :, j * 128:(j + 1) * 128]),
                        (v[:, :, 128:256], w_in[:, :, 2688 + j * 128:2688 + (j + 1) * 128])]

            def fn(slot, key, j=j):
                v = slot[:, 0:2048].rearrange("p (kt c) -> p kt c", kt=8)
                for si, (c0, n) in enumerate(subs):
                    bg, bu = newbank(), newbank()
                    mm_group(PS[bg][:, 0:n], [(v[:, kt, 0:128], hT[:, kt, c0:c0 + n]) for kt in range(8)],
                             reads=[key] + hkeys(c0, n), writes=[("ps", bg)])
                    mm_group(PS[bu][:, 0:n], [(v[:, kt, 128:256], hT[:, kt, c0:c0 + n]) for kt in range(8)],
                             reads=[key] + hkeys(c0, n), writes=[("ps", bu)])
                    sl = (j * 3 + si) % 2
                    S.op(ACT, (lambda e, bg=bg, n=n, sl=sl: e.activation(out=sg[sl][:, 0:n], in_=PS[bg][:, 0:n], func=AF.Silu)),
                         reads=[("ps", bg)], writes=[("sg", sl)])
                    S.op(DVE, (lambda e, bu=bu, n=n, sl=sl, j=j, c0=c0: e.tensor_tensor(
                        out=act[:, j, c0:c0 + n], in0=sg[sl][:, 0:n], in1=PS[bu][:, 0:n], op=ALU.mult)),
                        reads=[("sg", sl), ("ps", bu)], writes=[("act", j, c0)])
            step(pieces, fn)
        for m in range(8):
            def pieces(slot, m=m):
                v = slot[:, 0:2688].rearrange("p (kt c) -> p kt c", kt=21)
                return [(v, w_out[:, :, m * 128:(m + 1) * 128])]

            def fn(slot, key, m=m):
                v = slot[:, 0:2688].rearrange("p (kt c) -> p kt c", kt=21)
                for (c0, n) in subs:
                    b = newbank()
                    mm_group(PS[b][:, 0:n], [(v[:, kt, :], act[:, kt, c0:c0 + n]) for kt in range(21)],
                             reads=[key] + [("act", kt, c0) for kt in range(21)], writes=[("ps", b)])
                    S.op(DVE, (lambda e, b=b, n=n, m=m, c0=c0: e.scalar_tensor_tensor(
                        out=xT[:, m, c0:c0 + n], in0=PS[b][:, 0:n], scalar=0.5, in1=xT[:, m, c0:c0 + n],
                        op0=ALU.mult, op1=ALU.add)), reads=[("ps", b)] + xkeys(c0, n), writes=xkeys(c0, n))
            step(pieces, fn)
        prefetch_steps()
        emit_norm()
        run_steps()
        S.barrier()

    QSC = float(128 ** -0.5)

    def gla(p):
        cvn = Carver()
        emit_norm = norm_to_hT(p, 1 * 8, cvn, piece=256)
        NOFF = cvn.off
        gin = gla_in.rearrange("(kt p) c -> p kt c", p=128)
        gout = gla_out.rearrange("(kt p) c -> p kt c", p=128)
        tiles = [(0, 512, "p"), (512, 512, "p")]
        if p == 0:
            tiles.append((1024, 32, "s"))

        def w8(slot, ncol):
            return slot[:, 0:8 * ncol].rearrange("p (kt c) -> p kt c", kt=8)

        def ld(col0, ncol, src=None):
            src = gin if src is None else src
            return lambda slot: [(w8(slot, ncol), src[:, :, col0:col0 + ncol])]

        for (c0, n, kind) in tiles:
            ntt = n // 128 if kind == "p" else 1
            ntok = 128 if kind == "p" else 32
            hk = hkeys(c0, n)
            T = n
            cv = Carver()
            cv.off = NOFF
            oT = cv.f32([128, 8, T])
            qT = cv.bf([128, 4, T]); kT = cv.bf([128, 4, T]); kdT = cv.bf([128, 4, T])
            EQ = cv.f32([128, 4, T])
            tA = cv.f32([128, 4, T]); tB = cv.f32([128, 4, T])
            EK = tB
            glr = cv.bf([16, T])
            vv = cv.bf([128, ntt, 1024])
            ktok = cv.bf([128, ntt, 4, 128])
            Am = cv.bf([128, ntt, 4, 128])
            if kind == "s":
                kms = cv.bf([32, 4, 128])
                S0 = [cv.f32([128, 4, 256]) for _ in range(4)]
                S0b = [cv.bf([128, 4, 256]) for _ in range(4)]
                Snew = cv.f32([128, 4, 256])

                def prefetch_states(S0=S0, S0b=S0b):
                    for s_ in range(4):
                        S.op(SP, (lambda e, s_=s_: e.dma_start(out=S0[s_][:, :, :], in_=st_in[s_ * 4:(s_ + 1) * 4].rearrange("h p d -> p h d"))),
                             writes=[("S0", s_)], dma=f"s0_{s_}")
                        S.op(PL, (lambda e, s_=s_: e.dma_start(out=S0b[s_][:, :, :], in_=st_in[s_ * 4:(s_ + 1) * 4].rearrange("h p d -> p h d"))),
                             writes=[("S0b", s_)], dma=f"s0b_{s_}")
            else:
                prefetch_states = None

            def fn_g(slot, key, c0=c0, n=n, hk=hk, kind=kind, glr=glr, tA=tA, tB=tB, EQ=EQ, EK=EK, prefetch_states=prefetch_states):
                if prefetch_states is not None:
                    prefetch_states()
                w = w8(slot, 16)
                b = newbank()
                mm_group(PS[b][0:16, 0:n], [(w[:, kt, :], hT[:, kt, c0:c0 + n]) for kt in range(8)], reads=[key] + hk, writes=[("ps", b)])
                S.op(ACT, lambda e: e.copy(out=glr[0:16, 0:n], in_=PS[b][0:16, 0:n]), reads=[("ps", b)], writes=["glr"])
                for h in range(4):
                    b2 = newbank()
                    mm_group(PS[b2][:, 0:n], [(wg2[0:16, h * 128:(h + 1) * 128], glr[0:16, 0:n])], reads=["glr", "wg2"], writes=[("ps", b2)])
                    S.op(ACT, (lambda e, b2=b2, h=h: e.activation(out=tA[:, h, 0:n], in_=PS[b2][:, 0:n], func=AF.Exp, bias=negb[:, h:h + 1], scale=-1.0)),
                         reads=[("ps", b2), "negb"], writes=[("tA", h)])
                S.op(ACT, lambda e: e.activation(out=tA[:, :, 0:n], in_=tA[:, :, 0:n], func=AF.Ln, bias=1.0),
                     reads=[("tA", h) for h in range(4)], writes=[("tA", h) for h in range(4)])
                for h in range(4):
                    if kind == "p":
                        for q2 in range(n // 256):
                            cs2 = slice(q2 * 256, (q2 + 1) * 256)
                            S.op(DVE, (lambda e, h=h, cs2=cs2: e.tensor_tensor_scan(out=tB[:, h, cs2], data0=CF[:, C_SCP:C_SCP + 256], data1=tA[:, h, cs2], initial=0.0, op0=ALU.mult, op1=ALU.add)),
                                 reads=[("tA", h), "CF"], writes=[("tB", h)])
                    else:
                        S.op(DVE, (lambda e, h=h: e.tensor_tensor_scan(out=tB[:, h, 0:n], data0=CF[:, C_SCS:C_SCS + 32], data1=tA[:, h, 0:n], initial=0.0, op0=ALU.mult, op1=ALU.add)),
                             reads=[("tA", h), "CF"], writes=[("tB", h)])
                S.op(ACT, lambda e: e.activation(out=EQ[:, :, 0:n], in_=tB[:, :, 0:n], func=AF.Exp, scale=-1.0 / 16),
                     reads=[("tB", h) for h in range(4)], writes=["EQ"])
                S.op(ACT, lambda e: e.activation(out=EK[:, :, 0:n], in_=tB[:, :, 0:n], func=AF.Exp, scale=1.0 / 16),
                     reads=["EQ"], writes=["EK"] + [("tB", h) for h in range(4)])
            step(ld(2048, 16), fn_g)

            for half in range(2):
                def fn_v(slot, key, half=half, c0=c0, hk=hk, ntt=ntt, ntok=ntok, vv=vv):
                    w = w8(slot, 512)
                    for tt in range(ntt):
                        b = newbank()
                        mm_group(PS[b][0:ntok, :], [(hT[:, kt, c0 + tt * 128:c0 + tt * 128 + ntok], w[:, kt, :]) for kt in range(8)], reads=[key] + hk, writes=[("ps", b)])
                        copy_op(evac_eng(), vv[0:ntok, tt, half * 512:(half + 1) * 512], PS[b][0:ntok, :], [("ps", b)], [("vv", tt, half)])
                step(ld(1024 + half * 512, 512), fn_v)

            def fn_q(slot, key, c0=c0, n=n, hk=hk, qT=qT, EQ=EQ):
                w = w8(slot, 512)
                for h in range(4):
                    b = newbank()
                    mm_group(PS[b][:, 0:n], [(w[:, kt, h * 128:(h + 1) * 128], hT[:, kt, c0:c0 + n]) for kt in range(8)], reads=[key] + hk, writes=[("ps", b)])
                    S.op(DVE, (lambda e, b=b, h=h: e.scalar_tensor_tensor(out=qT[:, h, 0:n], in0=PS[b][:, 0:n], scalar=QSC, in1=EQ[:, h, 0:n], op0=ALU.mult, op1=ALU.mult)),
                         reads=[("ps", b), "EQ"], writes=[("qT", h)])
            step(ld(0, 512), fn_q)

            def fn_k(slot, key, c0=c0, n=n, hk=hk, ntt=ntt, ntok=ntok, kind=kind, kT=kT, kdT=kdT, EK=EK, EQ=EQ, ktok=ktok):
                w = w8(slot, 512)
                for h in range(4):
                    b = newbank()
                    mm_group(PS[b][:, 0:n], [(w[:, kt, h * 128:(h + 1) * 128], hT[:, kt, c0:c0 + n]) for kt in range(8)], reads=[key] + hk, writes=[("ps", b)])
                    S.op(DVE, (lambda e, b=b, h=h: e.tensor_tensor(out=kT[:, h, 0:n], in0=PS[b][:, 0:n], in1=EK[:, h, 0:n], op=ALU.mult)),
                         reads=[("ps", b), "EK"], writes=[("kT", h)])
                segw = 128 if kind == "p" else 8
                nseg = n // segw
                S.op(DVE, lambda e: e.tensor_tensor(out=kdT[:, :, 0:n].rearrange("p h (s w) -> p h s w", w=segw),
                                                    in0=kT[:, :, 0:n].rearrange("p h (s w) -> p h s w", w=segw),
                                                    in1=EQ[:, :, segw - 1:n:segw].unsqueeze(3).to_broadcast([128, 4, nseg, segw]), op=ALU.mult),
                     reads=[("kT", h) for h in range(4)] + ["EQ"], writes=[("kdT", h) for h in range(4)])
                for tt in range(ntt):
                    pv = PBFS[tt % 2][:, 0:512].rearrange("p (h d) -> p h d", h=4)

                    def fnt(e, tt=tt, pv=pv):
                        ins = None
                        for h in range(4):
                            ins = e.transpose(out=pv[0:ntok, h, :], in_=kdT[:, h, tt * 128:tt * 128 + ntok], identity=identb)
                        return ins
                    S.op(PE, fnt, reads=[("kdT", h) for h in range(4)] + ["CB"], writes=[("psbf", tt % 2)])
                    copy_op(evac_eng(), ktok[0:ntok, tt, :, :], pv[0:ntok, :, :], [("psbf", tt % 2)], [("ktok", tt)])
            step(ld(512, 512), fn_k)

            def fn_rec(slot, key, c0=c0, n=n, kind=kind, ntt=ntt, qT=qT, kT=kT, vv=vv, ktok=ktok, Am=Am, oT=oT, EQ=EQ):
                qk_keys = [("kT", h) for h in range(4)] + [("qT", h) for h in range(4)]
                if kind == "p":
                    for tt in range(ntt):
                        cs = slice(tt * 128, (tt + 1) * 128)
                        ba = newbank()
                        pa = PS[ba][:, :].rearrange("p (h t) -> p h t", h=4)

                        def fa(e, cs=cs, pa=pa):
                            ins = None
                            for h in range(4):
                                ins = e.matmul(pa[:, h, :], lhsT=kT[:, h, cs], rhs=qT[:, h, cs], start=True, stop=True)
                            return ins
                        S.op(PE, fa, reads=qk_keys, writes=[("ps", ba)])
                        S.op(DVE, (lambda e, pa=pa, tt=tt: e.tensor_tensor(out=Am[:, tt, :, :], in0=pa, in1=CB[:, C_LE:C_LE + 128].unsqueeze(1).to_broadcast([128, 4, 128]), op=ALU.mult)),
                             reads=[("ps", ba), "CB"], writes=[("Am", tt)])
                    for tt in range(ntt):
                        cs = slice(tt * 128, (tt + 1) * 128)
                        bks = []
                        for hp in range(2):
                            bk = newbank()
                            pk = PS[bk][:, :].rearrange("p (a d) -> p a d", a=2)
                            bks.append((bk, pk))

                            def fk(e, hp=hp, pk=pk, tt=tt):
                                ins = None
                                for hh in range(2):
                                    h = hp * 2 + hh
                                    ins = e.matmul(pk[:, hh, :], lhsT=ktok[:, tt, h, :], rhs=vv[:, tt, h * 256:(h + 1) * 256], start=True, stop=True)
                                return ins
                            S.op(PE, fk, reads=[("ktok", tt), ("vv", tt, 0), ("vv", tt, 1)], writes=[("ps", bk)])
                        for hp in range(2):
                            bo = newbank()
                            po = PS[bo][:, :].rearrange("p (a t) -> p a t", a=4)

                            def fo(e, hp=hp, po=po, tt=tt, cs=cs):
                                ins = None
                                for hh in range(2):
                                    h = hp * 2 + hh
                                    for half in range(2):
                                        o_ = po[:, hh * 2 + half, :]
                                        e.matmul(o_, lhsT=vv[:, tt, h * 256 + half * 128:h * 256 + (half + 1) * 128], rhs=Am[:, tt, h, :], start=True, stop=False)
                                        ins = e.matmul(o_, lhsT=Sbf[:, h, half * 128:(half + 1) * 128], rhs=qT[:, h, cs], start=False, stop=True)
                                return ins
                            S.op(PE, fo, reads=[("Am", tt), ("Sbf", hp), ("vv", tt, 0), ("vv", tt, 1)] + [("qT", h) for h in range(4)], writes=[("ps", bo)])
                            copy_op(ACT, oT[:, hp * 4:(hp + 1) * 4, cs], po, [("ps", bo)], [("oT", hp)])
                        col = tt * 128 + 127
                        for hp in range(2):
                            bk, pk = bks[hp]
                            for hh in range(2):
                                h = hp * 2 + hh
                                S.op(DVE, (lambda e, h=h, hh=hh, pk=pk, col=col: e.scalar_tensor_tensor(out=Sst[:, h, :], in0=Sst[:, h, :], scalar=EQ[:, h, col:col + 1], in1=pk[:, hh, :], op0=ALU.mult, op1=ALU.add)),
                                     reads=[("ps", bk), "EQ"], writes=[("Sst", hp)])
                            S.op(ACT, (lambda e, hp=hp: e.copy(out=Sbf[:, hp * 2:hp * 2 + 2, :], in_=Sst[:, hp * 2:hp * 2 + 2, :])), reads=[("Sst", hp)], writes=[("Sbf", hp)])
                else:
                    ba = newbank()
                    pa = PS[ba][:, :].rearrange("p (h t) -> p h t", h=4)

                    def fa(e, pa=pa):
                        ins = None
                        for h in range(4):
                            ins = e.matmul(pa[0:32, h, 0:32], lhsT=kT[:, h, 0:32], rhs=qT[:, h, 0:32], start=True, stop=True)
                        return ins
                    S.op(PE, fa, reads=qk_keys, writes=[("ps", ba)])
                    S.op(DVE, (lambda e, pa=pa: e.tensor_tensor(out=Am[0:32, 0, :, 0:32], in0=pa[0:32, :, 0:32], in1=CB[0:32, C_MS:C_MS + 32].unsqueeze(1).to_broadcast([32, 4, 32]), op=ALU.mult)),
                         reads=[("ps", ba), "CB"], writes=[("Am", 0)])
                    bo = LL
                    po = PS[bo][:, 0:256].rearrange("p (a t) -> p a t", a=8)
                    for s_ in range(4):
                        sl = s_
                        c8 = slice(s_ * 8, s_ * 8 + 8)

                        def fo(e, s_=s_, sl=sl, c8=c8):
                            ins = None
                            for h in range(4):
                                for half in range(2):
                                    o_ = po[:, h * 2 + half, c8]
                                    e.matmul(o_, lhsT=vv[0:32, 0, h * 256 + half * 128:h * 256 + (half + 1) * 128], rhs=Am[0:32, 0, h, c8], start=True, stop=False, skip_group_check=True)
                                    ins = e.matmul(o_, lhsT=S0b[sl][:, h, half * 128:(half + 1) * 128], rhs=qT[:, h, c8], start=False, stop=True, skip_group_check=True)
                            return ins
                        S.op(PE, fo, reads=[("Am", 0), ("S0b", sl), ("vv", 0, 0), ("vv", 0, 1)] + [("qT", h) for h in range(4)], writes=[("ps", bo)])
                        S.op(DVE, (lambda e, s_=s_: e.tensor_scalar(out=kms[:, :, :], in0=ktok[0:32, 0, :, :], scalar1=CF[0:32, C_SQ + s_:C_SQ + s_ + 1], scalar2=None, op0=ALU.mult)),
                             reads=[("ktok", 0), "CF"], writes=["kms"])
                        col = s_ * 8 + 7
                        for hp in range(2):
                            bk = newbank()
                            pk = PS[bk][:, :].rearrange("p (a d) -> p a d", a=2)

                            def fk(e, hp=hp, pk=pk):
                                ins = None
                                for hh in range(2):
                                    h = hp * 2 + hh
                                    ins = e.matmul(pk[:, hh, :], lhsT=kms[0:32, h, :], rhs=vv[0:32, 0, h * 256:(h + 1) * 256], start=True, stop=True)
                                return ins
                            S.op(PE, fk, reads=["kms", ("vv", 0, 0), ("vv", 0, 1)], writes=[("ps", bk)])
                            for hh in range(2):
                                h = hp * 2 + hh
                                S.op(DVE, (lambda e, h=h, hh=hh, pk=pk, sl=sl, col=col: e.scalar_tensor_tensor(out=Snew[:, h, :], in0=S0[sl][:, h, :], scalar=EQ[:, h, col:col + 1], in1=pk[:, hh, :], op0=ALU.mult, op1=ALU.add)),
                                     reads=[("ps", bk), ("S0", sl), "EQ"], writes=[("Snew", hp)])
                        S.op(SP, (lambda e, s_=s_: e.dma_start(out=sts_o[s_ * 4:(s_ + 1) * 4].rearrange("h p d -> p h d"), in_=Snew[:, :, :])),
                             reads=[("Snew", 0), ("Snew", 1)], dma="sts")
                    copy_op(ACT, oT[:, :, 0:32], po, [("ps", bo)], [("oT", 0), ("oT", 1)])
                S.barrier()
            step(None, fn_rec)

            cvB = Carver()
            cvB.off = NOFF
            oT_B = cvB.f32([128, 8, T])
            sqo = [cvB.bf([128, 2, T]) for _ in range(2)]
            RS = cvB.f32([128, 4, T])
            sr = [cvB.f32([128, T]) for _ in range(2)]; t1 = [cvB.f32([128, T]) for _ in range(2)]
            uT = cvB.bf([128, 8, T])

            for half in range(2):
                def fn_r(slot, key, half=half, c0=c0, n=n, hk=hk, oT=oT_B, sqo=sqo, RS=RS, sr=sr, t1=t1, uT=uT):
                    w = w8(slot, 512)
                    if half == 0:
                        for h in range(4):
                            S.op(ACT, (lambda e, h=h: e.activation(out=sqo[h % 2][:, :, 0:n], in_=oT[:, 2 * h:2 * h + 2, 0:n], func=AF.Square)),
                                 reads=[("oT", h // 2)], writes=[("sqo", h % 2)])
                            b = newbank()
                            mm_group(PS[b][:, 0:n], [(onesb, sqo[h % 2][:, 0, 0:n]), (onesb, sqo[h % 2][:, 1, 0:n])], reads=[("sqo", h % 2), "CB"], writes=[("ps", b)])
                            S.op(ACT, (lambda e, b=b, h=h: e.activation(out=RS[:, h, 0:n], in_=PS[b][:, 0:n], func=AF.Ln, scale=1.0 / 256, bias=EPS)),
                                 reads=[("ps", b)], writes=[("RS", h)])
                            S.op(ACT, (lambda e, h=h: e.activation(out=RS[:, h, 0:n], in_=RS[:, h, 0:n], func=AF.Exp, scale=-0.5)), reads=[("RS", h)], writes=[("RS", h)])
                    for cc in range(4):
                        c = half * 4 + cc
                        h = c // 2
                        b = newbank()
                        mm_group(PS[b][:, 0:n], [(w[:, kt, cc * 128:(cc + 1) * 128], hT[:, kt, c0:c0 + n]) for kt in range(8)], reads=[key] + hk, writes=[("ps", b)])
                        sl = c % 2
                        S.op(ACT, (lambda e, b=b, sl=sl: e.activation(out=sr[sl][:, 0:n], in_=PS[b][:, 0:n], func=AF.Silu)), reads=[("ps", b)], writes=[("sr", sl)])
                        S.op(DVE, (lambda e, c=c, h=h, sl=sl: e.tensor_tensor(out=t1[sl][:, 0:n], in0=oT[:, c, 0:n], in1=RS[:, h, 0:n], op=ALU.mult)),
                             reads=[("oT", c // 4), ("RS", h)], writes=[("t1", sl)])
                        S.op(DVE, (lambda e, c=c, sl=sl: e.scalar_tensor_tensor(out=uT[:, c, 0:n], in0=t1[sl][:, 0:n], scalar=G[:, 60 + (c % 2):61 + (c % 2)], in1=sr[sl][:, 0:n], op0=ALU.mult, op1=ALU.mult)),
                             reads=[("t1", sl), ("sr", sl), "G"], writes=[("uT", c)])
                step(ld(2064 + half * 512, 512), fn_r)

            for half in range(2):
                def fn_o(slot, key, half=half, c0=c0, n=n, uT=uT):
                    w = w8(slot, 512)
                    for mm in range(4):
                        m = half * 4 + mm
                        b = newbank()
                        mm_group(PS[b][:, 0:n], [(w[:, kt, mm * 128:(mm + 1) * 128], uT[:, kt, 0:n]) for kt in range(8)], reads=[key] + [("uT", c) for c in range(8)], writes=[("ps", b)])
                        S.op(DVE, (lambda e, b=b, m=m: e.tensor_tensor(out=xT[:, m, c0:c0 + n], in0=xT[:, m, c0:c0 + n], in1=PS[b][:, 0:n], op=ALU.add)),
                             reads=[("ps", b)] + xkeys(c0, n), writes=xkeys(c0, n))
                    if half == 1:
                        S.barrier()
                step(ld(half * 512, 512, gout), fn_o)
        prefetch_steps()
        emit_norm()
        run_steps()
        if p == 1:
            S.op(SP, lambda e: e.dma_start(out=stp_o.rearrange("h p d -> p h d"), in_=Sst[:, :, :]), reads=[("Sst", 0), ("Sst", 1)], dma="stp")
        S.barrier()

    def w8g(slot, ncol, kt=8):
        return slot[:, 0:kt * ncol].rearrange("p (kt c) -> p kt c", kt=kt)

    def rope_ops(src3, dst3, cst, n, nh, ta, tb, rk, wk):
        cos = cst[0:n, 0:32].unsqueeze(1).to_broadcast([n, nh, 32])
        sin = cst[0:n, 32:64].unsqueeze(1).to_broadcast([n, nh, 32])
        a = src3[:, :, 0:32]; b_ = src3[:, :, 32:64]
        S.op(DVE, lambda e: e.tensor_tensor(out=ta[0:n], in0=a, in1=cos, op=ALU.mult), reads=rk, writes=["ta"])
        S.op(DVE, lambda e: e.tensor_tensor(out=tb[0:n], in0=b_, in1=sin, op=ALU.mult), reads=rk, writes=["tb"])
        S.op(DVE, lambda e: e.tensor_tensor(out=dst3[:, :, 0:32], in0=ta[0:n], in1=tb[0:n], op=ALU.subtract), reads=["ta", "tb"], writes=wk)
        S.op(DVE, lambda e: e.tensor_tensor(out=ta[0:n], in0=a, in1=sin, op=ALU.mult), reads=rk, writes=["ta"])
        S.op(DVE, lambda e: e.tensor_tensor(out=tb[0:n], in0=b_, in1=cos, op=ALU.mult), reads=rk, writes=["tb"])
        S.op(DVE, lambda e: e.tensor_tensor(out=dst3[:, :, 32:64], in0=ta[0:n], in1=tb[0:n], op=ALU.add), reads=["ta", "tb"], writes=wk)

    def head_norm(src, dst, n, nh, gain, sq, ss, rk, wk):
        W_ = nh * 64
        S.op(DVE, lambda e: e.tensor_tensor(out=sq[0:n, 0:W_], in0=src[0:n, 0:W_], in1=src[0:n, 0:W_], op=ALU.mult), reads=rk, writes=["sq"])
        S.op(DVE, lambda e: e.tensor_reduce(out=ss[0:n, 0:nh], in_=sq[0:n, 0:W_].rearrange("p (h d) -> p h d", h=nh), axis=AX.X, op=ALU.add), reads=["sq"], writes=["ss"])
        S.op(ACT, lambda e: e.activation(out=ss[0:n, 0:nh], in_=ss[0:n, 0:nh], func=AF.Sqrt, scale=1.0 / 64, bias=EPS), reads=["ss"], writes=["ss"])
        S.op(DVE, lambda e: e.reciprocal(out=ss[0:n, 0:nh], in_=ss[0:n, 0:nh]), reads=["ss"], writes=["ss"])
        s3 = src[0:n, 0:W_].rearrange("p (h d) -> p h d", h=nh)
        d3 = dst[0:n, 0:W_].rearrange("p (h d) -> p h d", h=nh)
        S.op(DVE, lambda e: e.tensor_tensor(out=d3, in0=s3, in1=ss[0:n, 0:nh].unsqueeze(2).to_broadcast([n, nh, 64]), op=ALU.mult), reads=rk + ["ss"], writes=wk)
        S.op(DVE, lambda e: e.tensor_tensor(out=d3, in0=d3, in1=gain[0:n, :].unsqueeze(1).to_broadcast([n, nh, 64]), op=ALU.mult), reads=wk + ["KNQ0", "KNQ1"], writes=wk)

    def fm_norm_rope(ps_b, n, gcol, tab, bufs, idx, writer):
        sqb, rsb, qn32, qnb, tt, tt2 = bufs
        sl = idx % NFM
        S.op(ACT, lambda e: e.activation(out=sqb[sl][:, 0:n], in_=PS[ps_b][:, 0:n], func=AF.Square), reads=[("ps", ps_b)], writes=[("f_sqb", sl)])
        S.op(ACT, lambda e: e.activation(out=qnb[sl][:, 0:n], in_=PS[ps_b][:, 0:n], func=AF.Copy, scale=G[:, gcol:gcol + 1]), reads=[("ps", ps_b), "G"], writes=[("f_qnb", sl)])
        S.op(DVE, lambda e: e.scalar_tensor_tensor(out=tt[sl][:, 0:n], in0=PS[ps_b][:, 0:n], scalar=G[:, gcol:gcol + 1], in1=tab[:, 0, 0:n], op0=ALU.mult, op1=ALU.mult),
             reads=[("ps", ps_b), "f_tab", "G"], writes=[("f_t", sl)])
        b2 = newbank()
        mm_group(PS[b2][:, 0:n], [(CB[:, C_BD:C_BD + 128], sqb[sl][:, 0:n])], reads=[("f_sqb", sl), "CB"], writes=[("ps", b2)])
        b3 = newbank()
        mm_group(PS[b3][:, 0:n], [(CB[:, C_RM:C_RM + 128], qnb[sl][:, 0:n])], reads=[("f_qnb", sl), "CB"], writes=[("ps", b3)])
        S.op(ACT, lambda e: e.activation(out=rsb[sl][:, 0:n], in_=PS[b2][:, 0:n], func=AF.Ln, scale=1.0 / 64, bias=EPS), reads=[("ps", b2)], writes=[("f_rs", sl)])
        S.op(ACT, lambda e: e.activation(out=rsb[sl][:, 0:n], in_=rsb[sl][:, 0:n], func=AF.Exp, scale=-0.5), reads=[("f_rs", sl)], writes=[("f_rs", sl)])
        S.op(DVE, lambda e: e.tensor_tensor(out=tt2[sl][:, 0:n], in0=PS[b3][:, 0:n], in1=tab[:, 1, 0:n], op=ALU.mult), reads=[("ps", b3), "f_tab"], writes=[("f_t2", sl)])
        S.op(DVE, lambda e: e.tensor_tensor(out=tt[sl][:, 0:n], in0=tt[sl][:, 0:n], in1=tt2[sl][:, 0:n], op=ALU.add), reads=[("f_t", sl), ("f_t2", sl)], writes=[("f_t", sl)])
        writer(tt[sl], rsb[sl], [("f_t", sl), ("f_rs", sl)])

    NFM = 3

    def fm_bufs(cv):
        return ([cv.bf([128, 512]) for _ in range(NFM)], [cv.f32([128, 512]) for _ in range(NFM)], None,
                [cv.bf([128, 512]) for _ in range(NFM)], [cv.f32([128, 512]) for _ in range(NFM)], [cv.f32([128, 512]) for _ in range(NFM)])

    def tabcols(p, c0, n):
        return (p * 1024 + c0) if c0 < 1024 else 2048


    def kvproj(p):
        cvk = Carver()
        emit_norm = norm_to_hT(p, 48, cvk, piece=256)
        bufs = fm_bufs(cvk)
        tab = cvk.f32([128, 2, 512])
        kfm = [cvk.f32([128, 512]) for _ in range(2)]
        vf = [cvk.f32([128, 256]) for _ in range(2)]
        ko2 = [cvk.f32([128, 4, 256]) for _ in range(2)]
        kvw = kv_w.rearrange("(kt p) c -> p kt c", p=128)
        allh = hkeys(0, 1024)
        ropev = rope.rearrange("t p n -> p t n")
        cnt = [0]

        def fn(slot, key):
            w = w8g(slot, 512)
            for (c0, n) in subtiles(p):
                tc0 = tabcols(p, c0, n)
                S.op(SP, (lambda e, tc0=tc0, n=n: e.dma_start(out=tab[:, :, 0:n], in_=ropev[:, :, tc0:tc0 + n])), writes=["f_tab"], dma="ftab")
                for kc in range(2):
                    b = newbank()
                    mm_group(PS[b][:, 0:n], [(w[:, kt, kc * 128:(kc + 1) * 128], hT[:, kt, c0:c0 + n]) for kt in range(8)], reads=[key] + hkeys(c0, n), writes=[("ps", b)])
                    ci = cnt[0]
                    cnt[0] += 1

                    def writer(t_, t2_, rk, kc=kc, c0=c0, n=n, ci=ci):
                        S.op(DVE, lambda e: e.tensor_tensor(out=kfm[ci % 2][:, 0:n], in0=t_[:, 0:n], in1=t2_[:, 0:n], op=ALU.mult), reads=rk, writes=[("kfm", ci % 2)])
                        if c0 < 1024:
                            S.op(ACT, lambda e: e.copy(out=KT[:, kc, p * 1024 + c0:p * 1024 + c0 + n], in_=kfm[ci % 2][:, 0:n]), reads=[("kfm", ci % 2)], writes=[("KT", p)])
                        else:
                            S.op(ACT, lambda e: e.copy(out=KTs[:, kc, 0:n], in_=kfm[ci % 2][:, 0:n]), reads=[("kfm", ci % 2)], writes=["KTs"])
                        ntl = (n + 127) // 128
                        for t4 in range(0, ntl, 4):
                            bt = newbank()
                            nt4 = min(4, ntl - t4)
                            pv = PS[bt][:, :].rearrange("p (a t) -> p a t", a=4)
                            rows = min(128, n)

                            def ft(e, t4=t4, nt4=nt4, pv=pv, rows=rows):
                                ins = None
                                for a in range(nt4):
                                    ins = e.transpose(out=pv[0:rows, a, :], in_=kfm[ci % 2][:, (t4 + a) * 128:(t4 + a) * 128 + rows], identity=ident)
                                return ins
                            S.op(PE, ft, reads=[("kfm", ci % 2), "CF"], writes=[("ps", bt)])
                            sb = (c0 // 512) % 2
                            for a in range(nt4):
                                S.op(DVE if a % 2 else ACT, (lambda e, a=a, sb=sb, pv=pv, rows=rows, kc=kc, t4=t4: (e.tensor_copy if a % 2 else e.copy)(out=ko2[sb][0:rows, t4 + a, kc * 128:(kc + 1) * 128], in_=pv[0:rows, a, :])),
                                     reads=[("ps", bt)], writes=[("ko2", sb, kc)])
                            if kc == 1:
                                if c0 < 1024:
                                    r0 = p * 1024 + c0
                                    dstk = kp_o[r0:r0 + 512, :].rearrange("(t q) c -> q t c", q=128)
                                    S.op(SP, (lambda e, sb=sb, dstk=dstk: e.dma_start(out=dstk, in_=ko2[sb][:, 0:4, :])), reads=[("ko2", sb, 0), ("ko2", sb, 1)], dma=f"ko{sb}")
                                else:
                                    S.op(SP, (lambda e, sb=sb: e.dma_start(out=ks_o[:, :], in_=ko2[sb][0:32, 0, :])), reads=[("ko2", sb, 0), ("ko2", sb, 1)], dma=f"ko{sb}")
                    fm_norm_rope(b, n, 62, tab, bufs, ci, writer)
            for ti, (c0, n, kind, r0) in enumerate(toktiles(p)):
                sl = ti % 2
                b = newbank()
                mm_group(PS[b][0:n, 0:256], [(hT[:, kt, c0:c0 + n], w[:, kt, 256:512]) for kt in range(8)], reads=[key] + hkeys(c0, n), writes=[("ps", b)])
                S.op(ACT, (lambda e, sl=sl, n=n, b=b: e.copy(out=vf[sl][0:n, :], in_=PS[b][0:n, 0:256])), reads=[("ps", b)], writes=[("vf", sl)])
                dstv = vp_o[r0:r0 + n, :] if kind == "p" else vs_o[:, :]
                S.op(SP, (lambda e, sl=sl, n=n, dstv=dstv: e.dma_start(out=dstv, in_=vf[sl][0:n, :])), reads=[("vf", sl)],
                     writes=([("vdram", r0 // 128)] if kind == "p" else []), dma=f"vf{sl}")
                if kind == "p":
                    u = r0 // 128
                    S.op(DVE, (lambda e, u=u, b=b: e.tensor_copy(out=Vg[:, 0, u, :], in_=PS[b][:, 0:256])), reads=[("ps", b)], writes=[("Vg", 0, u)])
                else:
                    S.op(DVE, (lambda e, b=b: e.tensor_copy(out=Vsn[0:32, :], in_=PS[b][0:32, 0:256])), reads=[("ps", b)], writes=["Vsn"])
            for r in range(4):
                for bl in range(2):
                    b_ = 2 * p + bl
                    u = r * 4 + b_
                    row0 = r + 512 * b_
                    tiles_ = [("vdram", t_) for t_ in range(4 * b_, 4 * b_ + 4)]
                    S.op(PL, (lambda e, u=u, row0=row0: e.dma_start(out=Vg[:, 1, u, :], in_=vp_o[sl_(row0, 128, 4), :])),
                         reads=tiles_, writes=[("Vg", 1, u)], dma="vg1")
            for r in range(16):
                row0 = r + 1024 * p
                rows = slice(64 * p, 64 * p + 64)
                tiles_ = [("vdram", t_) for t_ in range(8 * p, 8 * p + 8)]
                S.op(PL, (lambda e, r=r, row0=row0, rows=rows: e.dma_start(out=Vg[rows, 2, r, :], in_=vp_o[sl_(row0, 64, 16), :])),
                     reads=tiles_, writes=[("Vg", 2, r)], dma="vg2")
        step(lambda slot: [(w8g(slot, 512), kvw[:, :, :])], fn)
        prefetch_steps()
        emit_norm()
        run_steps()
        S.barrier()

    def attn(p):
        cvn = Carver()
        emit_norm = norm_to_hT(p, 32, cvn)
        cva = Carver()
        TP = TPM if p == 0 else 1024
        QT = cva.bf([128, 6, TPM])
        On = cva.f32([128, 6, TPM])
        Zacc = cva.f32([128, 2, TPM])
        off_mark = cva.off
        bufs = fm_bufs(cva)
        tab = cva.f32([128, 2, 512])
        ropev = rope.rearrange("t p n -> p t n")
        cvb = Carver()
        cvb.off = off_mark
        Pe = [cvb.bf([128, 4, 128]) for _ in range(3)]
        KcH = [cvb.bf([128, 8, 256]) for _ in range(2)]
        VcH = [cvb.bf([128, 8, 256]) for _ in range(2)]
        KTc = [cvb.bf([128, 2, 128]) for _ in range(4)]
        wqv = wq.rearrange("(kt p) c -> p kt c", p=128)
        wov = wo.rearrange("(kt p) c -> p kt c", p=128)
        held = {}

        def fnA(slot, key):
            held["wA"] = w8g(slot, 512); held["kA"] = key

        def fnB(slot, key):
            wA, kA = held["wA"], held["kA"]
            wB = w8g(slot, 256)
            ci = 0
            for (c0, n) in subtiles(p):
                tc0 = tabcols(p, c0, n)
                S.op(SP, (lambda e, tc0=tc0, n=n: e.dma_start(out=tab[:, :, 0:n], in_=ropev[:, :, tc0:tc0 + n])), writes=["f_tab"], dma="ftab")
                for c in range(6):
                    wsrc, wkey, cc = (wA, kA, c) if c < 4 else (wB, key, c - 4)
                    b = newbank()
                    mm_group(PS[b][:, 0:n], [(wsrc[:, kt, cc * 128:(cc + 1) * 128], hT[:, kt, c0:c0 + n]) for kt in range(8)], reads=[wkey] + hkeys(c0, n), writes=[("ps", b)])

                    def writer(t_, t2_, rk, c=c, c0=c0, n=n):
                        S.op(DVE, lambda e: e.tensor_tensor(out=QT[:, c, c0:c0 + n], in0=t_[:, 0:n], in1=t2_[:, 0:n], op=ALU.mult), reads=rk, writes=["QT"])
                    fm_norm_rope(b, n, 63, tab, bufs, ci, writer)
                    ci += 1
        step(lambda slot: [(w8g(slot, 512), wqv[:, :, 0:512])], fnA, hold=1)
        step(lambda slot: [(w8g(slot, 256), wqv[:, :, 512:768])], fnB)

        ACCB = [5, 7]
        NPE = 3

        def acc_views(u):
            bnk = ACCB[u % 2]
            return (bnk, PS[bnk][:, 0:256].rearrange("p (a t) -> p a t", a=2), PS[bnk][:, 256:512].rearrange("p (a t) -> p a t", a=2))

        pe_ctr = [0]

        def stage1(B):
            r0_, r1_ = B["rows"]
            g, qsl, nq, qdims, ktile = B["g"], B["qsl"], B["nq"], B["qdims"], B["ktile"]
            bsx = [newbank(), newbank()]
            sl = pe_ctr[0] % NPE
            pe_ctr[0] += 1
            B["sl"] = sl
            nqq = nq if qdims is None else 24
            for hp in range(2):
                bs = bsx[hp]
                if qdims is None:
                    psS = PS[bs][:, 0:256].rearrange("p (h t) -> p h t", h=2)
                else:
                    psS = PS[bs][:, 0:48].rearrange("p (h g t) -> p h g t", h=2, g=3)

                def fs(e, hp=hp, psS=psS):
                    ins = None
                    for jh in range(2):
                        j = jh * 2 + hp
                        if qdims is None:
                            o_ = psS[r0_:r1_, jh, 0:nq]
                            rhs = QT[hp * 64:hp * 64 + 64, (g * 4 + j) // 2, qsl]
                        else:
                            o_ = psS[r0_:r1_, jh, :, :]
                            rhs = QT[hp * 64:hp * 64 + 64, jh:6:2, qsl]
                        ins = e.matmul(o_, lhsT=ktile(j), rhs=rhs, start=True, stop=True)
                    return ins
                S.op(PE, fs, reads=["QT"] + B["keys"], writes=[("ps", bs)])
            for hp in range(2):
                bs = bsx[hp]
                if qdims is None:
                    sview = PS[bs][r0_:r1_, 0:256].rearrange("p (h t) -> p h t", h=2)[:, :, 0:nq]
                else:
                    sview = PS[bs][r0_:r1_, 0:48].rearrange("p (h t) -> p h t", h=2)
                pview = Pe[sl][r0_:r1_, hp:4:2, 0:nqq]
                S.op(ACT, (lambda e, pview=pview, sview=sview: e.activation(out=pview, in_=sview, func=AF.Exp, scale=0.125)),
                     reads=[("ps", bs)], writes=[("Pe", sl, hp)])
            mask = B["mask"]
            if mask is not None:
                pall = Pe[sl][r0_:r1_, :, 0:nqq]
                S.op(DVE, lambda e: e.tensor_tensor(out=pall, in0=pall, in1=mask.unsqueeze(1).to_broadcast([r1_ - r0_, 4, nqq]), op=ALU.mult),
                     reads=["CB"], writes=[("Pe", sl, 0), ("Pe", sl, 1)])

        def stage2(B):
            r0_, r1_ = B["rows"]
            sl = B["sl"]
            nqq = B["nq"] if B["qdims"] is None else 24
            bnk, accO, accD = acc_views(B["u"])
            Vt = B["Vt"]
            if B["first"]:
                mm_group(PS[bnk][:, :], [(zer[:, 0:128], zer[:, 0:512])], reads=["zer"], writes=[("ps", bnk)])

            def fp(e):
                ins = None
                for j in range(4):
                    hp = j % 2
                    rhs = Pe[sl][r0_:r1_, j, 0:nqq]
                    e.matmul(accO[hp * 64:hp * 64 + 64, j // 2, 0:nqq], lhsT=Vt(j), rhs=rhs, start=False, stop=False, skip_group_check=True)
                    ins = e.matmul(accD[hp * 64:hp * 64 + 64, j // 2, 0:nqq], lhsT=CB[r0_:r1_, C_ONE:C_ONE + 64], rhs=rhs, start=False, stop=False, skip_group_check=True)
                return ins
            S.op(PE, fp, reads=[("Pe", sl, 0), ("Pe", sl, 1), "CB"] + B["keys"], writes=[("ps", bnk)])
            if B["last"]:
                B["evac"](bnk, accO, accD)

        def fn_units(slot, key):
            blocks = []
            units = []
            for b in range(8):
                Bk = 8 * p + b
                kbs = []
                if Bk >= 1:
                    kbs.append((128 * (Bk - 1), 128, 1, (0, Bk - 1), (0, 128), "ge"))
                kbs.append((128 * Bk, 128, 1, (0, Bk), (0, 128), "le"))
                units.append((0, (128 * b, 128, 1), kbs))
            for r in range(4):
                for bl in range(2):
                    b = 2 * p + bl
                    kbs = []
                    if b >= 1:
                        kbs.append((r + 512 * (b - 1), 128, 4, (1, r * 4 + b - 1), (0, 128), "ge"))
                    kbs.append((r + 512 * b, 128, 4, (1, r * 4 + b), (0, 128), "le"))
                    units.append((1, (r + 512 * bl, 128, 4), kbs))
            for r in range(16):
                if p == 0:
                    kbs = [(r, 64, 16, (2, r), (0, 64), "le")]
                else:
                    kbs = [(r, 128, 16, (2, r), (0, 128), "le64")]
                units.append((2, (r, 64, 16), kbs))
            ucount = [0]
            for (g, (q0, nq, qst), kbs) in units:
                qsl = sl_(q0, nq, qst)
                u = ucount[0]
                ucount[0] += 1

                def evac(bnk, accO, accD, g=g, qsl=qsl, nq=nq):
                    S.op(ACT, lambda e: e.copy(out=On[:, 2 * g:2 * g + 2, qsl], in_=accO[:, :, 0:nq]), reads=[("ps", bnk)], writes=["On"])
                    S.op(DVE, lambda e: e.tensor_tensor(out=Zacc[:, :, qsl], in0=Zacc[:, :, qsl], in1=accD[:, :, 0:nq], op=ALU.add), reads=[("ps", bnk), "Zacc"], writes=["Zacc"])
                for bi, (k0, nk, kst, (vg, vu), rows, mk) in enumerate(kbs):
                    ksl = sl_(k0, nk, kst)
                    r0_, r1_ = rows
                    if mk is None:
                        mask = None
                    elif mk == "le":
                        mask = CB[r0_:r1_, C_LE:C_LE + nq]
                    elif mk == "ge":
                        mask = CB[r0_:r1_, C_GE:C_GE + nq]
                    else:
                        mask = CB[:, C_LE + 64:C_LE + 128]
                    blocks.append(dict(g=g, qsl=qsl, nq=nq, qdims=None, rows=rows, mask=mask, u=u,
                                       ktile=(lambda j, ksl=ksl: KT[(j % 2) * 64:(j % 2) * 64 + 64, j // 2, ksl]),
                                       Vt=(lambda j, vg=vg, vu=vu, r0_=r0_, r1_=r1_: Vg[r0_:r1_, vg, vu, j * 64:(j + 1) * 64]),
                                       keys=[("KT", 0), ("KT", 1), ("Vg", vg, vu)],
                                       first=(bi == 0), last=(bi == len(kbs) - 1), evac=evac))
            if p == 0:
                for s_ in range(4):
                    qsl = slice(1024 + 8 * s_, 1024 + 8 * s_ + 8)
                    u = ucount[0]
                    ucount[0] += 1

                    def evac(bnk, accO, accD, qsl=qsl):
                        for jj in range(2):
                            S.op(ACT, (lambda e, jj=jj: e.copy(out=On[:, jj:6:2, qsl], in_=accO[:, jj, 0:24].rearrange("p (g t) -> p g t", g=3))),
                                 reads=[("ps", bnk)], writes=["On"])
                        S.op(DVE, lambda e: e.tensor_reduce(out=Zacc[:, :, qsl], in_=accD[:, :, 0:24].rearrange("p a (g t) -> p a t g", g=3), axis=AX.X, op=ALU.add),
                             reads=[("ps", bnk)], writes=["Zacc"])
                    for rt in range(16):
                        sl = rt % 4
                        hh, ri = rt // 8, rt % 8

                        def pre(sl=sl, hh=hh, ri=ri, s_=s_):
                            if ri == 0:
                                S.op(PL, lambda e: e.dma_start(out=KcH[hh][:, :, :], in_=ck[s_, hh * 1024:(hh + 1) * 1024, :].rearrange("(t p) c -> p t c", p=128)),
                                     writes=[("KcH", hh)], dma=f"kc{hh}")
                                S.op(PL, lambda e: e.dma_start(out=VcH[hh][:, :, :], in_=cv[s_, hh * 1024:(hh + 1) * 1024, :].rearrange("(t p) c -> p t c", p=128)),
                                     writes=[("VcH", hh)], dma=f"vc{hh}")
                            bt = newbank()
                            pv = PS[bt][:, 0:128].bitcast(BF16).rearrange("p (a t) -> p a t", a=2)

                            def ft(e):
                                ins = None
                                for kc in range(2):
                                    ins = e.transpose(out=pv[:, kc, :], in_=KcH[hh][:, ri, kc * 128:(kc + 1) * 128], identity=identb)
                                return ins
                            S.op(PE, ft, reads=[("KcH", hh), "CB"], writes=[("ps", bt)])
                            S.op(DVE, lambda e: e.tensor_copy(out=KTc[sl][:, :, :], in_=pv), reads=[("ps", bt)], writes=[("KTc", sl)])
                        blocks.append(dict(g=0, qsl=qsl, nq=8, qdims=3, rows=(0, 128), mask=CB[:, C_SM + rt * 24:C_SM + rt * 24 + 24], u=u,
                                           ktile=(lambda j, sl=sl: KTc[sl][(j % 2) * 64:(j % 2) * 64 + 64, j // 2, :]),
                                           Vt=(lambda j, hh=hh, ri=ri: VcH[hh][:, ri, j * 64:(j + 1) * 64]),
                                           keys=[("KTc", sl), ("VcH", hh)], first=(rt == 0), last=False, evac=None, pre=pre))
                    blocks.append(dict(g=0, qsl=qsl, nq=8, qdims=3, rows=(0, 32), mask=CB[0:32, C_MN + s_ * 24:C_MN + s_ * 24 + 24], u=u,
                                       ktile=(lambda j: KTs[(j % 2) * 64:(j % 2) * 64 + 64, j // 2, :]),
                                       Vt=(lambda j: Vsn[0:32, j * 64:(j + 1) * 64]),
                                       keys=["KTs", "Vsn"], first=False, last=True, evac=evac))
            D0, D = 2, 2
            nb = len(blocks)
            for i in range(nb + D0 + D):
                if i < nb and blocks[i].get("pre") is not None:
                    blocks[i]["pre"]()
                if 0 <= i - D0 < nb:
                    stage1(blocks[i - D0])
                if 0 <= i - D0 - D < nb:
                    stage2(blocks[i - D0 - D])
            for si, (c0, n) in enumerate(subtiles(p)):
                S.op(ACT, (lambda e, c0=c0, n=n: e.activation(out=Zacc[:, :, c0:c0 + n], in_=Zacc[:, :, c0:c0 + n], func=AF.Ln)), reads=["Zacc"], writes=[("Zr", si)])
                S.op(ACT, (lambda e, c0=c0, n=n: e.activation(out=Zacc[:, :, c0:c0 + n], in_=Zacc[:, :, c0:c0 + n], func=AF.Exp, scale=-1.0)), reads=[("Zr", si)], writes=[("Zr", si)])
                for c in range(6):
                    S.op(DVE, (lambda e, c=c, c0=c0, n=n: e.tensor_tensor(out=QT[:, c, c0:c0 + n], in0=On[:, c, c0:c0 + n], in1=Zacc[:, c % 2, c0:c0 + n], op=ALU.mult)),
                         reads=["On", ("Zr", si)], writes=["QT", ("QTn", si)])
        def fn_units_ring(slot, key):
            S.barrier()
            bank_ring[0] = [0, 1, 2, 3, 4, 6]
            fn_units(slot, key)
            bank_ring[0] = list(range(NB))
        step(None, fn_units_ring)

        for half in range(2):
            def fn_o(slot, key, half=half):
                w = w8g(slot, 512, kt=6)
                for mm in range(4):
                    m = half * 4 + mm
                    for (c0, n) in subtiles(p):
                        b = newbank()
                        mm_group(PS[b][:, 0:n], [(w[:, kt, mm * 128:(mm + 1) * 128], QT[:, kt, c0:c0 + n]) for kt in range(6)], reads=[key, ("QTn", c0 // 512)], writes=[("ps", b)])
                        S.op(DVE, (lambda e, b=b, m=m, c0=c0, n=n: e.tensor_tensor(out=xT[:, m, c0:c0 + n], in0=xT[:, m, c0:c0 + n], in1=PS[b][:, 0:n], op=ALU.add)),
                             reads=[("ps", b)] + xkeys(c0, n), writes=xkeys(c0, n))
            step((lambda slot, half=half: [(w8g(slot, 512, kt=6), wov[:, :, half * 512:(half + 1) * 512])]), fn_o)
        prefetch_steps()
        emit_norm()
        S.barrier(skip=("ws",))
        S.op(DVE, lambda e: e.memset(Zacc[:, :, :], 0.0), writes=["Zacc"])
        run_steps()
        S.barrier()

    for p in range(2):
        load_x(p)
        S.barrier()
        ph = phases if phases is not None else ["f1", "gla", "f2", "kv", "f3", "att", "f4"][:stage]
        if "f1" in ph:
            ffn(p, 0, 0)
        if "gla" in ph:
            gla(p)
        if "f2" in ph:
            ffn(p, 0, 1)
        if "kv" in ph:
            kvproj(p)
        if "f3" in ph:
            ffn(p, 1, 0)
        if "att" in ph:
            attn(p)
        if "f4" in ph:
            ffn(p, 1, 1)
        store_y(p)
        S.barrier()
    S.final_wait(SP)
    S.emit()
    return nc


_CACHE = {}


def make_in_maps(inp, ncores=8):
    f = lambda a: np.ascontiguousarray(np.asarray(a, dtype=np.float32))
    small = np.concatenate([f(inp["norm_gains"]).reshape(48, 128), f(inp["kv_norm"]).reshape(8, 128),
                            f(inp["gla_b_gate"]).reshape(4, 128), f(inp["gla_out_norm"]).reshape(2, 128),
                            np.tile(f(inp["k_norm"]).reshape(64), 2)[None, :], np.tile(f(inp["q_norm"]).reshape(64), 2)[None, :]], axis=0)
    knq = np.stack([f(inp["k_norm"]).reshape(64), f(inp["q_norm"]).reshape(64)], axis=0)
    shared = {
        "small": f(small), "knq": f(knq),
        "ffn_w_in": f(inp["ffn_w_in"]), "ffn_w_out": f(inp["ffn_w_out"]),
        "gla_w_in": f(inp["gla_w_in"])[0], "gla_w_gate2": f(inp["gla_w_gate2"])[0], "gla_w_out": f(inp["gla_w_out"])[0],
        "kv_w": f(inp["kv_w"]), "attn_w_q": f(inp["attn_w_q"])[0], "attn_w_out": f(inp["attn_w_out"])[0],
        "consts": build_consts(), "rope": build_rope(),
    }
    maps = []
    for i in range(ncores):
        m = dict(shared)
        m["xp"] = f(inp["x_prompt"][i])
        m["xs"] = f(inp["x_sample"][4 * i:4 * i + 4]).reshape(32, 1024)
        m["st"] = f(inp["state_gla"][0, 4 * i:4 * i + 4]).reshape(16, 128, 256)
        m["ck"] = f(inp["cache_k_win"][4 * i:4 * i + 4]).reshape(4, 2048, 256)
        m["cv"] = f(inp["cache_v_win"][4 * i:4 * i + 4]).reshape(4, 2048, 256)
        maps.append(m)
    return maps


def assemble(results, ncores=8):
    y_p = np.stack([r["y_p"] for r in results], 0)
    y_s = np.concatenate([r["y_s"].reshape(4, 8, 1024) for r in results], 0)
    stp = np.stack([r["stp"] for r in results], 0)[None]
    sts = np.concatenate([r["sts"].reshape(4, 4, 128, 256) for r in results], 0)[None]
    kp = np.stack([r["kp"].reshape(2048, 4, 64) for r in results], 0)
    vp = np.stack([r["vp"].reshape(2048, 4, 64) for r in results], 0)
    ks = np.concatenate([r["ks"].reshape(4, 8, 4, 64) for r in results], 0)
    vs = np.concatenate([r["vs"].reshape(4, 8, 4, 64) for r in results], 0)
    return tuple(np.ascontiguousarray(a.astype(np.float32)) for a in (y_p, y_s, stp, sts, kp, vp, ks, vs))


def kernel(**inputs):
    if "nc" not in _CACHE:
        _CACHE["nc"] = build()
    nc = _CACHE["nc"]
    maps = make_in_maps(inputs, 8)
    res = run_bass_kernel_spmd(nc, maps, core_ids=list(range(8)))
    return assemble(res.results, 8)
```

```python
import os
import numpy as np
import concourse.bass as bass
import concourse.mybir as mybir
from concourse.bass_utils import run_bass_kernel_spmd

F32 = mybir.dt.float32
BF16 = mybir.dt.bfloat16
AF = mybir.ActivationFunctionType
ALU = mybir.AluOpType
AX = mybir.AxisListType
ENGS = ("pe", "act", "dve", "pool", "sp")
EPS = 1e-6
ds = bass.ds


def sl_(start, n, step=1):
    return slice(start, start + (n - 1) * step + 1, step)


class Sched:
    def __init__(self, nc):
        self.nc = nc
        self.sem = {e: nc.alloc_semaphore("prog_" + e) for e in ENGS}
        self.cnt = {e: 0 for e in ENGS}
        self.waited = {e: {} for e in ENGS}
        self.ops = {e: [] for e in ENGS}
        self.lastw = {}
        self.readers = {}
        self.semobj = {self.sem[e].name: self.sem[e] for e in ENGS}
        self.dma_sems = {}

    def _need(self, eng, tok, waits):
        if tok is None:
            return
        sname, val, peng = tok
        if peng == eng and peng == "pe":
            return
        if self.waited[eng].get(sname, 0) >= val:
            return
        waits[sname] = max(waits.get(sname, 0), val)

    def op(self, eng, fn, reads=(), writes=(), dma=None):
        psr = [r for r in reads if isinstance(r, tuple) and r[0] in ("ps", "psbf")]
        if psr:
            reads = [r for r in reads if r not in psr]
            writes = list(writes) + psr
        waits = {}
        for r in reads:
            self._need(eng, self.lastw.get(r), waits)
        for w in writes:
            self._need(eng, self.lastw.get(w), waits)
            for t in self.readers.get(w, ()):
                self._need(eng, t, waits)
        for s, v in waits.items():
            self.waited[eng][s] = v
        if dma is None:
            self.cnt[eng] += 1
            tok = (self.sem[eng].name, self.cnt[eng], eng)
            inc = (self.sem[eng], 1)
        else:
            if dma not in self.dma_sems:
                s = self.nc.alloc_semaphore("dma_" + dma)
                self.dma_sems[dma] = [s, 0]
                self.semobj[s.name] = s
            ent = self.dma_sems[dma]
            ent[1] += 16
            tok = (ent[0].name, ent[1], "dma:" + dma)
            inc = (ent[0], 16)
        self.ops[eng].append((list(waits.items()), fn, inc))
        for r in reads:
            self.readers.setdefault(r, []).append(tok)
        for w in writes:
            self.lastw[w] = tok
            self.readers[w] = []
        return tok

    def barrier(self, skip=("ws", "vg")):
        for e in ENGS:
            waits = []
            for o in ENGS:
                if o != e and self.cnt[o] > self.waited[e].get(self.sem[o].name, 0):
                    waits.append((self.sem[o].name, self.cnt[o]))
                    self.waited[e][self.sem[o].name] = self.cnt[o]
            for name, (sm, c) in self.dma_sems.items():
                if name.startswith(skip):
                    continue
                if c > self.waited[e].get(sm.name, 0):
                    waits.append((sm.name, c))
                    self.waited[e][sm.name] = c
            if waits:
                self.ops[e].append((waits, None, None))

    def final_wait(self, eng="sp"):
        waits = []
        for e in ENGS:
            if e != eng and self.cnt[e] > 0:
                waits.append((self.sem[e].name, self.cnt[e]))
        for name, (s, c) in self.dma_sems.items():
            if c > 0:
                waits.append((s.name, c))
        self.ops[eng].append((waits, None, None))

    def emit(self):
        nc = self.nc
        with nc.Block() as block:
            def mk(ename):
                def body(e):
                    for waits, fn, inc in self.ops[ename]:
                        for s, v in waits:
                            e.wait_ge(self.semobj[s], v)
                        if fn is not None:
                            ins = fn(e)
                            ins.then_inc(inc[0], inc[1])
                return body
            block.tensor(mk("pe"))
            block.scalar(mk("act"))
            block.vector(mk("dve"))
            block.gpsimd(mk("pool"))
            block.sync(mk("sp"))


C_ID, C_LE, C_GE, C_SCP, C_SCS, C_MS, C_SQ, C_SM, C_MN, C_ONE, C_BD, C_RM, C_END = 0, 128, 256, 384, 640, 672, 704, 708, 1116, 1212, 1340, 1468, 1596
DIL = ((128, 1), (512, 4), (2048, 16))


def build_consts():
    c = np.zeros((128, C_END), np.float32)
    r = np.arange(128)
    c[:, C_ID:C_ID + 128] = np.eye(128)
    c[:, C_LE:C_LE + 128] = (r[:, None] <= r[None, :])
    c[:, C_GE:C_GE + 128] = (r[:, None] >= r[None, :])
    c[:, C_SCP:C_SCP + 256] = 1.0
    c[:, C_SCP] = 0.0
    c[:, C_SCP + 128] = 0.0
    c[:, C_SCS:C_SCS + 32] = 1.0
    c[:, C_SCS:C_SCS + 32:8] = 0.0
    t = np.arange(32)
    c[:32, C_MS:C_MS + 32] = ((t[:, None] // 8 == t[None, :] // 8) & (t[:, None] <= t[None, :]))
    for s in range(4):
        c[:32, C_SQ + s] = (t // 8 == s)
    for rt in range(17):
        for g, (win, dil) in enumerate(DIL):
            for tq in range(8):
                col = C_SM + rt * 24 + g * 8 + tq
                if rt < 16:
                    R = rt * 128 + r
                    d = 2048 + tq - R
                    c[:, col] = ((d % dil == 0) & (d >= 0) & (d <= win))
    for s in range(4):
        for g, (win, dil) in enumerate(DIL):
            for tq in range(8):
                col = C_MN + s * 24 + g * 8 + tq
                d = tq - (t % 8)
                c[:32, col] = ((t // 8 == s) & (d >= 0) & (d % dil == 0))
    c[:, C_ONE:C_ONE + 128] = 1.0
    c[:, C_BD:C_BD + 128] = (r[:, None] // 64 == r[None, :] // 64)
    for m in range(128):
        if m % 64 < 32:
            c[m + 32, C_RM + m] = -1.0
        else:
            c[m - 32, C_RM + m] = 1.0
    return c


def build_rope():
    half = 32
    inv = (10000.0 ** (-np.arange(half, dtype=np.float64) / half)).astype(np.float32)
    pos = np.concatenate([np.arange(2048), np.tile(8192 + np.arange(8), 4)]).astype(np.float32)
    ang = (inv[np.arange(128) % 32][:, None] * pos[None, :]).astype(np.float32).astype(np.float64)
    return np.stack([np.cos(ang), np.sin(ang)], axis=0).astype(np.float32)


def build(stage=99, phases=None):
    nc = bass.Bass("TRN2", target_bir_lowering=False)

    def din(name, shape):
        return nc.dram_tensor(name, list(shape), F32, kind="ExternalInput").ap()

    def dout(name, shape):
        return nc.dram_tensor(name, list(shape), F32, kind="ExternalOutput").ap()

    xp = din("xp", [2048, 1024]); xs = din("xs", [32, 1024]); st_in = din("st", [16, 128, 256])
    ck = din("ck", [4, 2048, 256]); cv = din("cv", [4, 2048, 256])
    small = din("small", [64, 128])
    knq = din("knq", [2, 64])
    ffn_in = din("ffn_w_in", [2, 2, 1024, 5376]); ffn_out = din("ffn_w_out", [2, 2, 2688, 1024])
    gla_in = din("gla_w_in", [1024, 3088]); gla_g2 = din("gla_w_gate2", [16, 512]); gla_out = din("gla_w_out", [1024, 1024])
    kv_w = din("kv_w", [1024, 512]); wq = din("attn_w_q", [1024, 768]); wo = din("attn_w_out", [768, 1024])
    consts = din("consts", [128, C_END]); rope = din("rope", [2, 128, 2080])
    y_p = dout("y_p", [2048, 1024]); y_s = dout("y_s", [32, 1024])
    stp_o = dout("stp", [4, 128, 256]); sts_o = dout("sts", [16, 128, 256])
    kp_o = dout("kp", [2048, 256]); vp_o = dout("vp", [2048, 256]); ks_o = dout("ks", [32, 256]); vs_o = dout("vs", [32, 256])

    S = Sched(nc)
    A = nc.alloc_sbuf_tensor
    TPM = 1056
    xT = A("xT", [128, 8, TPM], F32)
    hT = A("hT", [128, 8, TPM], BF16)
    NSLOT = 4
    wsl = A("wsl", [128, NSLOT, 4096], BF16)
    CF = A("CF", [128, C_END], F32)
    CB = A("CB", [128, C_END], BF16)
    G = A("G", [128, 64], F32)
    negb = A("negb", [128, 4], F32)
    KNQ = A("KNQ", [128, 2, 64], F32)
    wg2 = A("wg2", [16, 512], BF16)
    Sst = A("Sst", [128, 4, 256], F32)
    Sbf = A("Sbf", [128, 4, 256], BF16)
    KT = A("KT", [128, 2, 2048], BF16)
    KTs = A("KTs", [128, 2, 32], BF16)
    Vg = A("Vg", [128, 3, 16, 256], BF16)
    Vsn = A("Vsn", [32, 256], BF16)
    zer = A("zer", [128, 512], BF16)
    SCRW = 19100
    scr = A("scr", [128, SCRW], F32)
    PS = [nc.alloc_psum_tensor(f"ps{b}", [128, 512], F32) for b in range(8)]
    NB = 5
    LL = 5
    PBFS = [PS[6][:, :].bitcast(BF16), PS[7][:, :].bitcast(BF16)]
    bank_ctr = [0]

    bank_ring = [list(range(NB))]

    def newbank():
        ring = bank_ring[0]
        b = ring[bank_ctr[0] % len(ring)]
        bank_ctr[0] += 1
        return b

    class Carver:
        def __init__(self):
            self.off = 0

        def f32(self, shape):
            n = int(np.prod(shape[1:]))
            v = scr[:shape[0], self.off:self.off + n]
            self.off += n
            assert self.off <= SCRW, (self.off, SCRW)
            return v if len(shape) == 2 else v.rearrange(_pat(len(shape)), **_dims(shape))

        def bf(self, shape):
            n = int(np.prod(shape[1:]))
            w = (n + 1) // 2
            v = scr[:shape[0], self.off:self.off + w].bitcast(BF16)[:, 0:n]
            self.off += w
            assert self.off <= SCRW, (self.off, SCRW)
            return v if len(shape) == 2 else v.rearrange(_pat(len(shape)), **_dims(shape))

    def _pat(nd):
        names = "abcd"[:nd - 1]
        return "p (" + " ".join(names) + ") -> p " + " ".join(names)

    def _dims(shape):
        names = "abcd"[:len(shape) - 1]
        return {names[i]: int(shape[i + 1]) for i in range(len(shape) - 2)}

    SP, PL, ACT, DVE, PE = "sp", "pool", "act", "dve", "pe"
    S.op(SP, lambda e: e.dma_start(out=CF[:], in_=consts[:, :]), writes=["CF"], dma="c0")
    S.op(PL, lambda e: e.dma_start(out=CB[:], in_=consts[:, :]), writes=["CB"], dma="c1")
    S.op(PL, lambda e: e.dma_start(out=wg2[:], in_=gla_g2[:, :]), writes=["wg2"], dma="c2")
    S.op(SP, lambda e: e.dma_start(out=KNQ[:, 0, :], in_=knq[0, :].partition_broadcast(128)), writes=["KNQ0"], dma="c3")
    S.op(SP, lambda e: e.dma_start(out=KNQ[:, 1, :], in_=knq[1, :].partition_broadcast(128)), writes=["KNQ1"], dma="c3b")
    S.op(PL, lambda e: e.memset(zer[:], 0.0), writes=["zer"])
    S.op(PL, lambda e: e.memset(Sst[:], 0.0), writes=[("Sst", 0), ("Sst", 1)])
    S.op(PL, lambda e: e.memset(Sbf[:], 0.0), writes=[("Sbf", 0), ("Sbf", 1)])
    cv0 = Carver()
    smt = cv0.f32([64, 128])
    S.op(SP, lambda e: e.dma_start(out=smt, in_=small[:, :]), writes=["smt"], dma="c4")
    S.op(PE, lambda e: e.transpose(out=PS[0][:, 0:64], in_=smt, identity=CF[0:64, C_ID:C_ID + 64]),
         reads=["smt", "CF"], writes=[("ps", 0)])
    S.op(DVE, lambda e: e.tensor_copy(out=G[:], in_=PS[0][:, 0:64]), reads=[("ps", 0)], writes=["G"])
    S.op(DVE, lambda e: e.tensor_scalar(out=negb[:], in0=G[:, 56:60], scalar1=-1.0, scalar2=None, op0=ALU.mult),
         reads=["G"], writes=["negb"])
    ident = CF[:, C_ID:C_ID + 128]
    identb = CB[:, C_ID:C_ID + 128]
    onesb = CB[:, C_ONE:C_ONE + 128]
    S.barrier()

    steps = []

    def step(pieces, fn, hold=0):
        steps.append((pieces, fn, hold))

    rs_state = {}

    def _rs_init():
        if "load_idx" in rs_state:
            return
        rs_state["load_idx"] = [i for i, (p, _, _) in enumerate(steps) if p is not None]
        rs_state["holds"] = [steps[i][2] for i in rs_state["load_idx"]]
        rs_state["issued"] = 0

    def _issue_for(q):
        load_idx, holds = rs_state["load_idx"], rs_state["holds"]
        while rs_state["issued"] < len(load_idx):
            k = rs_state["issued"]
            if k > q + NSLOT - 1:
                break
            if k >= NSLOT and (k - NSLOT + holds[k - NSLOT]) >= q:
                break
            sl = (slot_base[0] + k) % NSLOT
            pieces = steps[load_idx[k]][0]
            for pi, (osl, src) in enumerate(pieces(wsl[:, sl, :])):
                S.op(PL, (lambda e, o=osl, s_=src: e.dma_start(out=o, in_=s_)),
                     writes=[("ws", sl)], dma=f"ws{sl}_{pi}")
            rs_state["issued"] += 1

    def prefetch_steps():
        _rs_init()
        _issue_for(0)

    def run_steps():
        _rs_init()
        q = 0
        for i, (p, fn, hold) in enumerate(steps):
            if p is not None:
                _issue_for(q)
                assert rs_state["issued"] > q
                sl = (slot_base[0] + q) % NSLOT
                fn(wsl[:, sl, :], ("ws", sl))
                q += 1
            else:
                fn(None, None)
        slot_base[0] = (slot_base[0] + q) % NSLOT
        steps.clear()
        rs_state.clear()

    slot_base = [0]

    evac_ctr = [0]

    def evac_eng():
        evac_ctr[0] += 1
        return ACT if evac_ctr[0] % 2 else DVE

    def copy_op(eng, out, in_, reads, writes):
        if eng == ACT:
            S.op(ACT, lambda e: e.copy(out=out, in_=in_), reads=reads, writes=writes)
        else:
            S.op(DVE, lambda e: e.tensor_copy(out=out, in_=in_), reads=reads, writes=writes)

    def mm_group(out, pairs, reads, writes, skip=False, start=True, stop=True):
        def fn(e):
            ins = None
            n = len(pairs)
            for i, (l, r) in enumerate(pairs):
                ins = e.matmul(out, lhsT=l, rhs=r, start=(start and i == 0), stop=(stop and i == n - 1),
                               skip_group_check=skip)
            return ins
        S.op(PE, fn, reads=reads, writes=writes)

    def subtiles(p):
        return [(0, 512), (512, 512), (1024, 32)] if p == 0 else [(0, 512), (512, 512)]

    def toktiles(p):
        tl = [(tt * 128, 128, "p", p * 1024 + tt * 128) for tt in range(8)]
        if p == 0:
            tl.append((1024, 32, "s", 0))
        return tl

    XOFF = 14800
    xin = [scr[:, XOFF + i * 1024:XOFF + (i + 1) * 1024] for i in range(4)]
    x_issued = {}

    def x_dma(p, ti):
        if (p, ti) in x_issued:
            return
        tl = toktiles(p)
        if ti >= len(tl):
            return
        x_issued[(p, ti)] = True
        c0, n, kind, r0 = tl[ti]
        sl = ti % 4
        src = xp[r0:r0 + n, :] if kind == "p" else xs[:, :]
        S.op(SP, (lambda e, sl=sl, n=n, src=src: e.dma_start(out=xin[sl][0:n, :], in_=src)),
             writes=[("xin", sl)], dma=f"xin{sl}")

    def x_prefetch(p):
        for ti in range(4):
            x_dma(p, ti)

    def load_x(p):
        for ti, (c0, n, kind, r0) in enumerate(toktiles(p)):
            sl = ti % 4
            x_dma(p, ti)
            for hb in range(2):
                b = newbank()
                pv = PS[b][:, :].rearrange("p (a t) -> p a t", a=4)

                def fn(e, sl=sl, n=n, hb=hb, pv=pv):
                    ins = None
                    for a in range(4):
                        fc = hb * 4 + a
                        ins = e.transpose(out=pv[:, a, 0:n], in_=xin[sl][0:n, fc * 128:(fc + 1) * 128], identity=ident[0:n, 0:n])
                    return ins
                S.op(PE, fn, reads=[("xin", sl), "CF"], writes=[("ps", b)])
                copy_op(evac_eng(), xT[:, hb * 4:hb * 4 + 4, c0:c0 + n], pv[:, :, 0:n], [("ps", b)], [("xT", c0)])
            x_dma(p, ti + 4)

    def store_y(p):
        cvx = Carver()
        yo = [cvx.f32([128, 1024]) for _ in range(4)]
        for ti, (c0, n, kind, r0) in enumerate(toktiles(p)):
            sl = ti % 4
            for hb in range(2):
                b = newbank()
                pv = PS[b][:, :].rearrange("p (a t) -> p a t", a=4)

                def fn(e, n=n, hb=hb, pv=pv, c0=c0):
                    ins = None
                    for a in range(4):
                        fc = hb * 4 + a
                        ins = e.transpose(out=pv[0:n, a, :], in_=xT[:, fc, c0:c0 + n], identity=ident)
                    return ins
                S.op(PE, fn, reads=[("xT", c0 // 128 * 128 if n == 128 else c0), "CF"], writes=[("ps", b)])
                copy_op(evac_eng(), yo[sl][0:n, hb * 512:(hb + 1) * 512].rearrange("p (a t) -> p a t", a=4), pv[0:n, :, :],
                        [("ps", b)], [("yo", sl, hb)])
            dst = y_p[r0:r0 + n, :] if kind == "p" else y_s[:, :]
            S.op(SP, (lambda e, sl=sl, n=n, dst=dst: e.dma_start(out=dst, in_=yo[sl][0:n, :])),
                 reads=[("yo", sl, 0), ("yo", sl, 1)], dma=f"yo{sl}")

    def xkeys(c0, n):
        return [("xT", c) for c in range(c0, c0 + n, 128)] if n >= 128 else [("xT", c0)]

    def hkeys(c0, n):
        return [("hT", c) for c in range(c0, c0 + n, 128)] if n >= 128 else [("hT", c0)]

    def norm_to_hT(p, gcol, cvn, piece=512):
        sqb = cvn.bf([128, 8, piece])
        rs = cvn.f32([128, piece])
        pieces = []
        for (c0, n) in subtiles(p):
            for o in range(0, n, piece):
                pieces.append((c0 + o, min(piece, n - o)))

        def emit():
            for (c0, n) in pieces:
                S.op(ACT, (lambda e, c0=c0, n=n: e.activation(out=sqb[:, :, 0:n], in_=xT[:, :, c0:c0 + n], func=AF.Square)),
                     reads=xkeys(c0, n), writes=["sqb"])
                b = newbank()
                mm_group(PS[b][:, 0:n], [(onesb, sqb[:, fc, 0:n]) for fc in range(8)], reads=["sqb", "CB"], writes=[("ps", b)])
                S.op(ACT, (lambda e, b=b, n=n: e.activation(out=rs[:, 0:n], in_=PS[b][:, 0:n], func=AF.Ln, scale=1.0 / 1024, bias=EPS)),
                     reads=[("ps", b)], writes=["rs"])
                S.op(ACT, (lambda e, n=n: e.activation(out=rs[:, 0:n], in_=rs[:, 0:n], func=AF.Exp, scale=-0.5)), reads=["rs"], writes=["rs"])
                for fc in range(8):
                    S.op(DVE, (lambda e, fc=fc, c0=c0, n=n: e.scalar_tensor_tensor(
                        out=hT[:, fc, c0:c0 + n], in0=xT[:, fc, c0:c0 + n], scalar=G[:, gcol + fc:gcol + fc + 1], in1=rs[:, 0:n],
                        op0=ALU.mult, op1=ALU.mult)), reads=xkeys(c0, n) + ["rs", "G"], writes=hkeys(c0, n))
        return emit

    def ffn(p, l, f):
        cvf = Carver()
        emit_norm = norm_to_hT(p, (l * 3 + (0 if f == 0 else 2)) * 8, cvf)
        act = cvf.bf([128, 21, TPM])
        sg = [cvf.f32([128, 512]) for _ in range(2)]
        w_in = ffn_in[l, f].rearrange("(kt p) c -> p kt c", p=128)
        w_out = ffn_out[l, f].rearrange("(kt p) c -> p kt c", p=128)
        subs = subtiles(p)
        for j in range(21):
            def pieces(slot, j=j):
                v = slot[:, 0:2048].rearrange("p (kt c) -> p kt c", kt=8)
                return [(v[:, :, 0:128], w_in[:, :, j * 128:(j + 1) * 128]),
                        (v[:, :, 128:256], w_in[:, :, 2688 + j * 128:2688 + (j + 1) * 128])]

            def fn(slot, key, j=j):
                v = slot[:, 0:2048].rearrange("p (kt c) -> p kt c", kt=8)
                for si, (c0, n) in enumerate(subs):
                    bg, bu = newbank(), newbank()
                    mm_group(PS[bg][:, 0:n], [(v[:, kt, 0:128], hT[:, kt, c0:c0 + n]) for kt in range(8)],
                             reads=[key] + hkeys(c0, n), writes=[("ps", bg)])
                    mm_group(PS[bu][:, 0:n], [(v[:, kt, 128:256], hT[:, kt, c0:c0 + n]) for kt in range(8)],
                             reads=[key] + hkeys(c0, n), writes=[("ps", bu)])
                    sl = (j * 3 + si) % 2
                    S.op(ACT, (lambda e, bg=bg, n=n, sl=sl: e.activation(out=sg[sl][:, 0:n], in_=PS[bg][:, 0:n], func=AF.Silu)),
                         reads=[("ps", bg)], writes=[("sg", sl)])
                    S.op(DVE, (lambda e, bu=bu, n=n, sl=sl, j=j, c0=c0: e.tensor_tensor(
                        out=act[:, j, c0:c0 + n], in0=sg[sl][:, 0:n], in1=PS[bu][:, 0:n], op=ALU.mult)),
                        reads=[("sg", sl), ("ps", bu)], writes=[("act", j, c0)])
            step(pieces, fn)
        for m in range(8):
            def pieces(slot, m=m):
                v = slot[:, 0:2688].rearrange("p (kt c) -> p kt c", kt=21)
                return [(v, w_out[:, :, m * 128:(m + 1) * 128])]

            def fn(slot, key, m=m):
                v = slot[:, 0:2688].rearrange("p (kt c) -> p kt c", kt=21)
                for (c0, n) in subs:
                    b = newbank()
                    mm_group(PS[b][:, 0:n], [(v[:, kt, :], act[:, kt, c0:c0 + n]) for kt in range(21)],
                             reads=[key] + [("act", kt, c0) for kt in range(21)], writes=[("ps", b)])
                    S.op(DVE, (lambda e, b=b, n=n, m=m, c0=c0: e.scalar_tensor_tensor(
                        out=xT[:, m, c0:c0 + n], in0=PS[b][:, 0:n], scalar=0.5, in1=xT[:, m, c0:c0 + n],
                        op0=ALU.mult, op1=ALU.add)), reads=[("ps", b)] + xkeys(c0, n), writes=xkeys(c0, n))
            step(pieces, fn)
        prefetch_steps()
        emit_norm()
        run_steps()
        S.barrier()

    QSC = float(128 ** -0.5)

    def gla(p):
        cvn = Carver()
        emit_norm = norm_to_hT(p, 1 * 8, cvn, piece=256)
        NOFF = cvn.off
        gin = gla_in.rearrange("(kt p) c -> p kt c", p=128)
        gout = gla_out.rearrange("(kt p) c -> p kt c", p=128)
        tiles = [(0, 512, "p"), (512, 512, "p")]
        if p == 0:
            tiles.append((1024, 32, "s"))

        def w8(slot, ncol):
            return slot[:, 0:8 * ncol].rearrange("p (kt c) -> p kt c", kt=8)

        def ld(col0, ncol, src=None):
            src = gin if src is None else src
            return lambda slot: [(w8(slot, ncol), src[:, :, col0:col0 + ncol])]

        for (c0, n, kind) in tiles:
            ntt = n // 128 if kind == "p" else 1
            ntok = 128 if kind == "p" else 32
            hk = hkeys(c0, n)
            T = n
            cv = Carver()
            cv.off = NOFF
            oT = cv.f32([128, 8, T])
            qT = cv.bf([128, 4, T]); kT = cv.bf([128, 4, T]); kdT = cv.bf([128, 4, T])
            EQ = cv.f32([128, 4, T])
            tA = cv.f32([128, 4, T]); tB = cv.f32([128, 4, T])
            EK = tB
            glr = cv.bf([16, T])
            vv = cv.bf([128, ntt, 1024])
            ktok = cv.bf([128, ntt, 4, 128])
            Am = cv.bf([128, ntt, 4, 128])
            if kind == "s":
                kms = cv.bf([32, 4, 128])
                S0 = [cv.f32([128, 4, 256]) for _ in range(4)]
                S0b = [cv.bf([128, 4, 256]) for _ in range(4)]
                Snew = cv.f32([128, 4, 256])

                def prefetch_states(S0=S0, S0b=S0b):
                    for s_ in range(4):
                        S.op(SP, (lambda e, s_=s_: e.dma_start(out=S0[s_][:, :, :], in_=st_in[s_ * 4:(s_ + 1) * 4].rearrange("h p d -> p h d"))),
                             writes=[("S0", s_)], dma=f"s0_{s_}")
                        S.op(PL, (lambda e, s_=s_: e.dma_start(out=S0b[s_][:, :, :], in_=st_in[s_ * 4:(s_ + 1) * 4].rearrange("h p d -> p h d"))),
                             writes=[("S0b", s_)], dma=f"s0b_{s_}")
            else:
                prefetch_states = None

            def fn_g(slot, key, c0=c0, n=n, hk=hk, kind=kind, glr=glr, tA=tA, tB=tB, EQ=EQ, EK=EK, prefetch_states=prefetch_states):
                if prefetch_states is not None:
                    prefetch_states()
                w = w8(slot, 16)
                b = newbank()
                mm_group(PS[b][0:16, 0:n], [(w[:, kt, :], hT[:, kt, c0:c0 + n]) for kt in range(8)], reads=[key] + hk, writes=[("ps", b)])
                S.op(ACT, lambda e: e.copy(out=glr[0:16, 0:n], in_=PS[b][0:16, 0:n]), reads=[("ps", b)], writes=["glr"])
                for h in range(4):
                    b2 = newbank()
                    mm_group(PS[b2][:, 0:n], [(wg2[0:16, h * 128:(h + 1) * 128], glr[0:16, 0:n])], reads=["glr", "wg2"], writes=[("ps", b2)])
                    S.op(ACT, (lambda e, b2=b2, h=h: e.activation(out=tA[:, h, 0:n], in_=PS[b2][:, 0:n], func=AF.Exp, bias=negb[:, h:h + 1], scale=-1.0)),
                         reads=[("ps", b2), "negb"], writes=[("tA", h)])
                S.op(ACT, lambda e: e.activation(out=tA[:, :, 0:n], in_=tA[:, :, 0:n], func=AF.Ln, bias=1.0),
                     reads=[("tA", h) for h in range(4)], writes=[("tA", h) for h in range(4)])
                for h in range(4):
                    if kind == "p":
                        for q2 in range(n // 256):
                            cs2 = slice(q2 * 256, (q2 + 1) * 256)
                            S.op(DVE, (lambda e, h=h, cs2=cs2: e.tensor_tensor_scan(out=tB[:, h, cs2], data0=CF[:, C_SCP:C_SCP + 256], data1=tA[:, h, cs2], initial=0.0, op0=ALU.mult, op1=ALU.add)),
                                 reads=[("tA", h), "CF"], writes=[("tB", h)])
                    else:
                        S.op(DVE, (lambda e, h=h: e.tensor_tensor_scan(out=tB[:, h, 0:n], data0=CF[:, C_SCS:C_SCS + 32], data1=tA[:, h, 0:n], initial=0.0, op0=ALU.mult, op1=ALU.add)),
                             reads=[("tA", h), "CF"], writes=[("tB", h)])
                S.op(ACT, lambda e: e.activation(out=EQ[:, :, 0:n], in_=tB[:, :, 0:n], func=AF.Exp, scale=-1.0 / 16),
                     reads=[("tB", h) for h in range(4)], writes=["EQ"])
                S.op(ACT, lambda e: e.activation(out=EK[:, :, 0:n], in_=tB[:, :, 0:n], func=AF.Exp, scale=1.0 / 16),
                     reads=["EQ"], writes=["EK"] + [("tB", h) for h in range(4)])
            step(ld(2048, 16), fn_g)

            for half in range(2):
                def fn_v(slot, key, half=half, c0=c0, hk=hk, ntt=ntt, ntok=ntok, vv=vv):
                    w = w8(slot, 512)
                    for tt in range(ntt):
                        b = newbank()
                        mm_group(PS[b][0:ntok, :], [(hT[:, kt, c0 + tt * 128:c0 + tt * 128 + ntok], w[:, kt, :]) for kt in range(8)], reads=[key] + hk, writes=[("ps", b)])
                        copy_op(evac_eng(), vv[0:ntok, tt, half * 512:(half + 1) * 512], PS[b][0:ntok, :], [("ps", b)], [("vv", tt, half)])
                step(ld(1024 + half * 512, 512), fn_v)

            def fn_q(slot, key, c0=c0, n=n, hk=hk, qT=qT, EQ=EQ):
                w = w8(slot, 512)
                for h in range(4):
                    b = newbank()
                    mm_group(PS[b][:, 0:n], [(w[:, kt, h * 128:(h + 1) * 128], hT[:, kt, c0:c0 + n]) for kt in range(8)], reads=[key] + hk, writes=[("ps", b)])
                    S.op(DVE, (lambda e, b=b, h=h: e.scalar_tensor_tensor(out=qT[:, h, 0:n], in0=PS[b][:, 0:n], scalar=QSC, in1=EQ[:, h, 0:n], op0=ALU.mult, op1=ALU.mult)),
                         reads=[("ps", b), "EQ"], writes=[("qT", h)])
            step(ld(0, 512), fn_q)

            def fn_k(slot, key, c0=c0, n=n, hk=hk, ntt=ntt, ntok=ntok, kind=kind, kT=kT, kdT=kdT, EK=EK, EQ=EQ, ktok=ktok):
                w = w8(slot, 512)
                for h in range(4):
                    b = newbank()
                    mm_group(PS[b][:, 0:n], [(w[:, kt, h * 128:(h + 1) * 128], hT[:, kt, c0:c0 + n]) for kt in range(8)], reads=[key] + hk, writes=[("ps", b)])
                    S.op(DVE, (lambda e, b=b, h=h: e.tensor_tensor(out=kT[:, h, 0:n], in0=PS[b][:, 0:n], in1=EK[:, h, 0:n], op=ALU.mult)),
                         reads=[("ps", b), "EK"], writes=[("kT", h)])
                segw = 128 if kind == "p" else 8
                nseg = n // segw
                S.op(DVE, lambda e: e.tensor_tensor(out=kdT[:, :, 0:n].rearrange("p h (s w) -> p h s w", w=segw),
                                                    in0=kT[:, :, 0:n].rearrange("p h (s w) -> p h s w", w=segw),
                                                    in1=EQ[:, :, segw - 1:n:segw].unsqueeze(3).to_broadcast([128, 4, nseg, segw]), op=ALU.mult),
                     reads=[("kT", h) for h in range(4)] + ["EQ"], writes=[("kdT", h) for h in range(4)])
                for tt in range(ntt):
                    pv = PBFS[tt % 2][:, 0:512].rearrange("p (h d) -> p h d", h=4)

                    def fnt(e, tt=tt, pv=pv):
                        ins = None
                        for h in range(4):
                            ins = e.transpose(out=pv[0:ntok, h, :], in_=kdT[:, h, tt * 128:tt * 128 + ntok], identity=identb)
                        return ins
                    S.op(PE, fnt, reads=[("kdT", h) for h in range(4)] + ["CB"], writes=[("psbf", tt % 2)])
                    copy_op(evac_eng(), ktok[0:ntok, tt, :, :], pv[0:ntok, :, :], [("psbf", tt % 2)], [("ktok", tt)])
            step(ld(512, 512), fn_k)

            def fn_rec(slot, key, c0=c0, n=n, kind=kind, ntt=ntt, qT=qT, kT=kT, vv=vv, ktok=ktok, Am=Am, oT=oT, EQ=EQ):
                qk_keys = [("kT", h) for h in range(4)] + [("qT", h) for h in range(4)]
                if kind == "p":
                    for tt in range(ntt):
                        cs = slice(tt * 128, (tt + 1) * 128)
                        ba = newbank()
                        pa = PS[ba][:, :].rearrange("p (h t) -> p h t", h=4)

                        def fa(e, cs=cs, pa=pa):
                            ins = None
                            for h in range(4):
                                ins = e.matmul(pa[:, h, :], lhsT=kT[:, h, cs], rhs=qT[:, h, cs], start=True, stop=True)
                            return ins
                        S.op(PE, fa, reads=qk_keys, writes=[("ps", ba)])
                        S.op(DVE, (lambda e, pa=pa, tt=tt: e.tensor_tensor(out=Am[:, tt, :, :], in0=pa, in1=CB[:, C_LE:C_LE + 128].unsqueeze(1).to_broadcast([128, 4, 128]), op=ALU.mult)),
                             reads=[("ps", ba), "CB"], writes=[("Am", tt)])
                    for tt in range(ntt):
                        cs = slice(tt * 128, (tt + 1) * 128)
                        bks = []
                        for hp in range(2):
                            bk = newbank()
                            pk = PS[bk][:, :].rearrange("p (a d) -> p a d", a=2)
                            bks.append((bk, pk))

                            def fk(e, hp=hp, pk=pk, tt=tt):
                                ins = None
                                for hh in range(2):
                                    h = hp * 2 + hh
                                    ins = e.matmul(pk[:, hh, :], lhsT=ktok[:, tt, h, :], rhs=vv[:, tt, h * 256:(h + 1) * 256], start=True, stop=True)
                                return ins
                            S.op(PE, fk, reads=[("ktok", tt), ("vv", tt, 0), ("vv", tt, 1)], writes=[("ps", bk)])
                        for hp in range(2):
                            bo = newbank()
                            po = PS[bo][:, :].rearrange("p (a t) -> p a t", a=4)

                            def fo(e, hp=hp, po=po, tt=tt, cs=cs):
                                ins = None
                                for hh in range(2):
                                    h = hp * 2 + hh
                                    for half in range(2):
                                        o_ = po[:, hh * 2 + half, :]
                                        e.matmul(o_, lhsT=vv[:, tt, h * 256 + half * 128:h * 256 + (half + 1) * 128], rhs=Am[:, tt, h, :], start=True, stop=False)
                                        ins = e.matmul(o_, lhsT=Sbf[:, h, half * 128:(half + 1) * 128], rhs=qT[:, h, cs], start=False, stop=True)
                                return ins
                            S.op(PE, fo, reads=[("Am", tt), ("Sbf", hp), ("vv", tt, 0), ("vv", tt, 1)] + [("qT", h) for h in range(4)], writes=[("ps", bo)])
                            copy_op(ACT, oT[:, hp * 4:(hp + 1) * 4, cs], po, [("ps", bo)], [("oT", hp)])
                        col = tt * 128 + 127
                        for hp in range(2):
                            bk, pk = bks[hp]
                            for hh in range(2):
                                h = hp * 2 + hh
                                S.op(DVE, (lambda e, h=h, hh=hh, pk=pk, col=col: e.scalar_tensor_tensor(out=Sst[:, h, :], in0=Sst[:, h, :], scalar=EQ[:, h, col:col + 1], in1=pk[:, hh, :], op0=ALU.mult, op1=ALU.add)),
                                     reads=[("ps", bk), "EQ"], writes=[("Sst", hp)])
                            S.op(ACT, (lambda e, hp=hp: e.copy(out=Sbf[:, hp * 2:hp * 2 + 2, :], in_=Sst[:, hp * 2:hp * 2 + 2, :])), reads=[("Sst", hp)], writes=[("Sbf", hp)])
                else:
                    ba = newbank()
                    pa = PS[ba][:, :].rearrange("p (h t) -> p h t", h=4)

                    def fa(e, pa=pa):
                        ins = None
                        for h in range(4):
                            ins = e.matmul(pa[0:32, h, 0:32], lhsT=kT[:, h, 0:32], rhs=qT[:, h, 0:32], start=True, stop=True)
                        return ins
                    S.op(PE, fa, reads=qk_keys, writes=[("ps", ba)])
                    S.op(DVE, (lambda e, pa=pa: e.tensor_tensor(out=Am[0:32, 0, :, 0:32], in0=pa[0:32, :, 0:32], in1=CB[0:32, C_MS:C_MS + 32].unsqueeze(1).to_broadcast([32, 4, 32]), op=ALU.mult)),
                         reads=[("ps", ba), "CB"], writes=[("Am", 0)])
                    bo = LL
                    po = PS[bo][:, 0:256].rearrange("p (a t) -> p a t", a=8)
                    for s_ in range(4):
                        sl = s_
                        c8 = slice(s_ * 8, s_ * 8 + 8)

                        def fo(e, s_=s_, sl=sl, c8=c8):
                            ins = None
                            for h in range(4):
                                for half in range(2):
                                    o_ = po[:, h * 2 + half, c8]
                                    e.matmul(o_, lhsT=vv[0:32, 0, h * 256 + half * 128:h * 256 + (half + 1) * 128], rhs=Am[0:32, 0, h, c8], start=True, stop=False, skip_group_check=True)
                                    ins = e.matmul(o_, lhsT=S0b[sl][:, h, half * 128:(half + 1) * 128], rhs=qT[:, h, c8], start=False, stop=True, skip_group_check=True)
                            return ins
                        S.op(PE, fo, reads=[("Am", 0), ("S0b", sl), ("vv", 0, 0), ("vv", 0, 1)] + [("qT", h) for h in range(4)], writes=[("ps", bo)])
                        S.op(DVE, (lambda e, s_=s_: e.tensor_scalar(out=kms[:, :, :], in0=ktok[0:32, 0, :, :], scalar1=CF[0:32, C_SQ + s_:C_SQ + s_ + 1], scalar2=None, op0=ALU.mult)),
                             reads=[("ktok", 0), "CF"], writes=["kms"])
                        col = s_ * 8 + 7
                        for hp in range(2):
                            bk = newbank()
                            pk = PS[bk][:, :].rearrange("p (a d) -> p a d", a=2)

                            def fk(e, hp=hp, pk=pk):
                                ins = None
                                for hh in range(2):
                                    h = hp * 2 + hh
                                    ins = e.matmul(pk[:, hh, :], lhsT=kms[0:32, h, :], rhs=vv[0:32, 0, h * 256:(h + 1) * 256], start=True, stop=True)
                                return ins
                            S.op(PE, fk, reads=["kms", ("vv", 0, 0), ("vv", 0, 1)], writes=[("ps", bk)])
                            for hh in range(2):
                                h = hp * 2 + hh
                                S.op(DVE, (lambda e, h=h, hh=hh, pk=pk, sl=sl, col=col: e.scalar_tensor_tensor(out=Snew[:, h, :], in0=S0[sl][:, h, :], scalar=EQ[:, h, col:col + 1], in1=pk[:, hh, :], op0=ALU.mult, op1=ALU.add)),
                                     reads=[("ps", bk), ("S0", sl), "EQ"], writes=[("Snew", hp)])
                        S.op(SP, (lambda e, s_=s_: e.dma_start(out=sts_o[s_ * 4:(s_ + 1) * 4].rearrange("h p d -> p h d"), in_=Snew[:, :, :])),
                             reads=[("Snew", 0), ("Snew", 1)], dma="sts")
                    copy_op(ACT, oT[:, :, 0:32], po, [("ps", bo)], [("oT", 0), ("oT", 1)])
                S.barrier()
            step(None, fn_rec)

            cvB = Carver()
            cvB.off = NOFF
            oT_B = cvB.f32([128, 8, T])
            sqo = [cvB.bf([128, 2, T]) for _ in range(2)]
            RS = cvB.f32([128, 4, T])
            sr = [cvB.f32([128, T]) for _ in range(2)]; t1 = [cvB.f32([128, T]) for _ in range(2)]
            uT = cvB.bf([128, 8, T])

            for half in range(2):
                def fn_r(slot, key, half=half, c0=c0, n=n, hk=hk, oT=oT_B, sqo=sqo, RS=RS, sr=sr, t1=t1, uT=uT):
                    w = w8(slot, 512)
                    if half == 0:
                        for h in range(4):
                            S.op(ACT, (lambda e, h=h: e.activation(out=sqo[h % 2][:, :, 0:n], in_=oT[:, 2 * h:2 * h + 2, 0:n], func=AF.Square)),
                                 reads=[("oT", h // 2)], writes=[("sqo", h % 2)])
                            b = newbank()
                            mm_group(PS[b][:, 0:n], [(onesb, sqo[h % 2][:, 0, 0:n]), (onesb, sqo[h % 2][:, 1, 0:n])], reads=[("sqo", h % 2), "CB"], writes=[("ps", b)])
                            S.op(ACT, (lambda e, b=b, h=h: e.activation(out=RS[:, h, 0:n], in_=PS[b][:, 0:n], func=AF.Ln, scale=1.0 / 256, bias=EPS)),
                                 reads=[("ps", b)], writes=[("RS", h)])
                            S.op(ACT, (lambda e, h=h: e.activation(out=RS[:, h, 0:n], in_=RS[:, h, 0:n], func=AF.Exp, scale=-0.5)), reads=[("RS", h)], writes=[("RS", h)])
                    for cc in range(4):
                        c = half * 4 + cc
                        h = c // 2
                        b = newbank()
                        mm_group(PS[b][:, 0:n], [(w[:, kt, cc * 128:(cc + 1) * 128], hT[:, kt, c0:c0 + n]) for kt in range(8)], reads=[key] + hk, writes=[("ps", b)])
                        sl = c % 2
                        S.op(ACT, (lambda e, b=b, sl=sl: e.activation(out=sr[sl][:, 0:n], in_=PS[b][:, 0:n], func=AF.Silu)), reads=[("ps", b)], writes=[("sr", sl)])
                        S.op(DVE, (lambda e, c=c, h=h, sl=sl: e.tensor_tensor(out=t1[sl][:, 0:n], in0=oT[:, c, 0:n], in1=RS[:, h, 0:n], op=ALU.mult)),
                             reads=[("oT", c // 4), ("RS", h)], writes=[("t1", sl)])
                        S.op(DVE, (lambda e, c=c, sl=sl: e.scalar_tensor_tensor(out=uT[:, c, 0:n], in0=t1[sl][:, 0:n], scalar=G[:, 60 + (c % 2):61 + (c % 2)], in1=sr[sl][:, 0:n], op0=ALU.mult, op1=ALU.mult)),
                             reads=[("t1", sl), ("sr", sl), "G"], writes=[("uT", c)])
                step(ld(2064 + half * 512, 512), fn_r)

            for half in range(2):
                def fn_o(slot, key, half=half, c0=c0, n=n, uT=uT):
                    w = w8(slot, 512)
                    for mm in range(4):
                        m = half * 4 + mm
                        b = newbank()
                        mm_group(PS[b][:, 0:n], [(w[:, kt, mm * 128:(mm + 1) * 128], uT[:, kt, 0:n]) for kt in range(8)], reads=[key] + [("uT", c) for c in range(8)], writes=[("ps", b)])
                        S.op(DVE, (lambda e, b=b, m=m: e.tensor_tensor(out=xT[:, m, c0:c0 + n], in0=xT[:, m, c0:c0 + n], in1=PS[b][:, 0:n], op=ALU.add)),
                             reads=[("ps", b)] + xkeys(c0, n), writes=xkeys(c0, n))
                    if half == 1:
                        S.barrier()
                step(ld(half * 512, 512, gout), fn_o)
        prefetch_steps()
        emit_norm()
        run_steps()
        if p == 1:
            S.op(SP, lambda e: e.dma_start(out=stp_o.rearrange("h p d -> p h d"), in_=Sst[:, :, :]), reads=[("Sst", 0), ("Sst", 1)], dma="stp")
        S.barrier()

    def w8g(slot, ncol, kt=8):
        return slot[:, 0:kt * ncol].rearrange("p (kt c) -> p kt c", kt=kt)

    def rope_ops(src3, dst3, cst, n, nh, ta, tb, rk, wk):
        cos = cst[0:n, 0:32].unsqueeze(1).to_broadcast([n, nh, 32])
        sin = cst[0:n, 32:64].unsqueeze(1).to_broadcast([n, nh, 32])
        a = src3[:, :, 0:32]; b_ = src3[:, :, 32:64]
        S.op(DVE, lambda e: e.tensor_tensor(out=ta[0:n], in0=a, in1=cos, op=ALU.mult), reads=rk, writes=["ta"])
        S.op(DVE, lambda e: e.tensor_tensor(out=tb[0:n], in0=b_, in1=sin, op=ALU.mult), reads=rk, writes=["tb"])
        S.op(DVE, lambda e: e.tensor_tensor(out=dst3[:, :, 0:32], in0=ta[0:n], in1=tb[0:n], op=ALU.subtract), reads=["ta", "tb"], writes=wk)
        S.op(DVE, lambda e: e.tensor_tensor(out=ta[0:n], in0=a, in1=sin, op=ALU.mult), reads=rk, writes=["ta"])
        S.op(DVE, lambda e: e.tensor_tensor(out=tb[0:n], in0=b_, in1=cos, op=ALU.mult), reads=rk, writes=["tb"])
        S.op(DVE, lambda e: e.tensor_tensor(out=dst3[:, :, 32:64], in0=ta[0:n], in1=tb[0:n], op=ALU.add), reads=["ta", "tb"], writes=wk)

    def head_norm(src, dst, n, nh, gain, sq, ss, rk, wk):
        W_ = nh * 64
        S.op(DVE, lambda e: e.tensor_tensor(out=sq[0:n, 0:W_], in0=src[0:n, 0:W_], in1=src[0:n, 0:W_], op=ALU.mult), reads=rk, writes=["sq"])
        S.op(DVE, lambda e: e.tensor_reduce(out=ss[0:n, 0:nh], in_=sq[0:n, 0:W_].rearrange("p (h d) -> p h d", h=nh), axis=AX.X, op=ALU.add), reads=["sq"], writes=["ss"])
        S.op(ACT, lambda e: e.activation(out=ss[0:n, 0:nh], in_=ss[0:n, 0:nh], func=AF.Sqrt, scale=1.0 / 64, bias=EPS), reads=["ss"], writes=["ss"])
        S.op(DVE, lambda e: e.reciprocal(out=ss[0:n, 0:nh], in_=ss[0:n, 0:nh]), reads=["ss"], writes=["ss"])
        s3 = src[0:n, 0:W_].rearrange("p (h d) -> p h d", h=nh)
        d3 = dst[0:n, 0:W_].rearrange("p (h d) -> p h d", h=nh)
        S.op(DVE, lambda e: e.tensor_tensor(out=d3, in0=s3, in1=ss[0:n, 0:nh].unsqueeze(2).to_broadcast([n, nh, 64]), op=ALU.mult), reads=rk + ["ss"], writes=wk)
        S.op(DVE, lambda e: e.tensor_tensor(out=d3, in0=d3, in1=gain[0:n, :].unsqueeze(1).to_broadcast([n, nh, 64]), op=ALU.mult), reads=wk + ["KNQ0", "KNQ1"], writes=wk)

    def fm_norm_rope(ps_b, n, gcol, tab, bufs, idx, writer):
        sqb, rsb, qn32, qnb, tt, tt2 = bufs
        sl = idx % NFM
        S.op(ACT, lambda e: e.activation(out=sqb[sl][:, 0:n], in_=PS[ps_b][:, 0:n], func=AF.Square), reads=[("ps", ps_b)], writes=[("f_sqb", sl)])
        S.op(ACT, lambda e: e.activation(out=qnb[sl][:, 0:n], in_=PS[ps_b][:, 0:n], func=AF.Copy, scale=G[:, gcol:gcol + 1]), reads=[("ps", ps_b), "G"], writes=[("f_qnb", sl)])
        S.op(DVE, lambda e: e.scalar_tensor_tensor(out=tt[sl][:, 0:n], in0=PS[ps_b][:, 0:n], scalar=G[:, gcol:gcol + 1], in1=tab[:, 0, 0:n], op0=ALU.mult, op1=ALU.mult),
             reads=[("ps", ps_b), "f_tab", "G"], writes=[("f_t", sl)])
        b2 = newbank()
        mm_group(PS[b2][:, 0:n], [(CB[:, C_BD:C_BD + 128], sqb[sl][:, 0:n])], reads=[("f_sqb", sl), "CB"], writes=[("ps", b2)])
        b3 = newbank()
        mm_group(PS[b3][:, 0:n], [(CB[:, C_RM:C_RM + 128], qnb[sl][:, 0:n])], reads=[("f_qnb", sl), "CB"], writes=[("ps", b3)])
        S.op(ACT, lambda e: e.activation(out=rsb[sl][:, 0:n], in_=PS[b2][:, 0:n], func=AF.Ln, scale=1.0 / 64, bias=EPS), reads=[("ps", b2)], writes=[("f_rs", sl)])
        S.op(ACT, lambda e: e.activation(out=rsb[sl][:, 0:n], in_=rsb[sl][:, 0:n], func=AF.Exp, scale=-0.5), reads=[("f_rs", sl)], writes=[("f_rs", sl)])
        S.op(DVE, lambda e: e.tensor_tensor(out=tt2[sl][:, 0:n], in0=PS[b3][:, 0:n], in1=tab[:, 1, 0:n], op=ALU.mult), reads=[("ps", b3), "f_tab"], writes=[("f_t2", sl)])
        S.op(DVE, lambda e: e.tensor_tensor(out=tt[sl][:, 0:n], in0=tt[sl][:, 0:n], in1=tt2[sl][:, 0:n], op=ALU.add), reads=[("f_t", sl), ("f_t2", sl)], writes=[("f_t", sl)])
        writer(tt[sl], rsb[sl], [("f_t", sl), ("f_rs", sl)])

    NFM = 3

    def fm_bufs(cv):
        return ([cv.bf([128, 512]) for _ in range(NFM)], [cv.f32([128, 512]) for _ in range(NFM)], None,
                [cv.bf([128, 512]) for _ in range(NFM)], [cv.f32([128, 512]) for _ in range(NFM)], [cv.f32([128, 512]) for _ in range(NFM)])

    def tabcols(p, c0, n):
        return (p * 1024 + c0) if c0 < 1024 else 2048


    def kvproj(p):
        cvk = Carver()
        emit_norm = norm_to_hT(p, 48, cvk, piece=256)
        bufs = fm_bufs(cvk)
        tab = cvk.f32([128, 2, 512])
        kfm = [cvk.f32([128, 512]) for _ in range(2)]
        vf = [cvk.f32([128, 256]) for _ in range(2)]
        ko2 = [cvk.f32([128, 4, 256]) for _ in range(2)]
        kvw = kv_w.rearrange("(kt p) c -> p kt c", p=128)
        allh = hkeys(0, 1024)
        ropev = rope.rearrange("t p n -> p t n")
        cnt = [0]

        def fn(slot, key):
            w = w8g(slot, 512)
            for (c0, n) in subtiles(p):
                tc0 = tabcols(p, c0, n)
                S.op(SP, (lambda e, tc0=tc0, n=n: e.dma_start(out=tab[:, :, 0:n], in_=ropev[:, :, tc0:tc0 + n])), writes=["f_tab"], dma="ftab")
                for kc in range(2):
                    b = newbank()
                    mm_group(PS[b][:, 0:n], [(w[:, kt, kc * 128:(kc + 1) * 128], hT[:, kt, c0:c0 + n]) for kt in range(8)], reads=[key] + hkeys(c0, n), writes=[("ps", b)])
                    ci = cnt[0]
                    cnt[0] += 1

                    def writer(t_, t2_, rk, kc=kc, c0=c0, n=n, ci=ci):
                        S.op(DVE, lambda e: e.tensor_tensor(out=kfm[ci % 2][:, 0:n], in0=t_[:, 0:n], in1=t2_[:, 0:n], op=ALU.mult), reads=rk, writes=[("kfm", ci % 2)])
                        if c0 < 1024:
                            S.op(ACT, lambda e: e.copy(out=KT[:, kc, p * 1024 + c0:p * 1024 + c0 + n], in_=kfm[ci % 2][:, 0:n]), reads=[("kfm", ci % 2)], writes=[("KT", p)])
                        else:
                            S.op(ACT, lambda e: e.copy(out=KTs[:, kc, 0:n], in_=kfm[ci % 2][:, 0:n]), reads=[("kfm", ci % 2)], writes=["KTs"])
                        ntl = (n + 127) // 128
                        for t4 in range(0, ntl, 4):
                            bt = newbank()
                            nt4 = min(4, ntl - t4)
                            pv = PS[bt][:, :].rearrange("p (a t) -> p a t", a=4)
                            rows = min(128, n)

                            def ft(e, t4=t4, nt4=nt4, pv=pv, rows=rows):
                                ins = None
                                for a in range(nt4):
                                    ins = e.transpose(out=pv[0:rows, a, :], in_=kfm[ci % 2][:, (t4 + a) * 128:(t4 + a) * 128 + rows], identity=ident)
                                return ins
                            S.op(PE, ft, reads=[("kfm", ci % 2), "CF"], writes=[("ps", bt)])
                            sb = (c0 // 512) % 2
                            for a in range(nt4):
                                S.op(DVE if a % 2 else ACT, (lambda e, a=a, sb=sb, pv=pv, rows=rows, kc=kc, t4=t4: (e.tensor_copy if a % 2 else e.copy)(out=ko2[sb][0:rows, t4 + a, kc * 128:(kc + 1) * 128], in_=pv[0:rows, a, :])),
                                     reads=[("ps", bt)], writes=[("ko2", sb, kc)])
                            if kc == 1:
                                if c0 < 1024:
                                    r0 = p * 1024 + c0
                                    dstk = kp_o[r0:r0 + 512, :].rearrange("(t q) c -> q t c", q=128)
                                    S.op(SP, (lambda e, sb=sb, dstk=dstk: e.dma_start(out=dstk, in_=ko2[sb][:, 0:4, :])), reads=[("ko2", sb, 0), ("ko2", sb, 1)], dma=f"ko{sb}")
                                else:
                                    S.op(SP, (lambda e, sb=sb: e.dma_start(out=ks_o[:, :], in_=ko2[sb][0:32, 0, :])), reads=[("ko2", sb, 0), ("ko2", sb, 1)], dma=f"ko{sb}")
                    fm_norm_rope(b, n, 62, tab, bufs, ci, writer)
            for ti, (c0, n, kind, r0) in enumerate(toktiles(p)):
                sl = ti % 2
                b = newbank()
                mm_group(PS[b][0:n, 0:256], [(hT[:, kt, c0:c0 + n], w[:, kt, 256:512]) for kt in range(8)], reads=[key] + hkeys(c0, n), writes=[("ps", b)])
                S.op(ACT, (lambda e, sl=sl, n=n, b=b: e.copy(out=vf[sl][0:n, :], in_=PS[b][0:n, 0:256])), reads=[("ps", b)], writes=[("vf", sl)])
                dstv = vp_o[r0:r0 + n, :] if kind == "p" else vs_o[:, :]
                S.op(SP, (lambda e, sl=sl, n=n, dstv=dstv: e.dma_start(out=dstv, in_=vf[sl][0:n, :])), reads=[("vf", sl)],
                     writes=([("vdram", r0 // 128)] if kind == "p" else []), dma=f"vf{sl}")
                if kind == "p":
                    u = r0 // 128
                    S.op(DVE, (lambda e, u=u, b=b: e.tensor_copy(out=Vg[:, 0, u, :], in_=PS[b][:, 0:256])), reads=[("ps", b)], writes=[("Vg", 0, u)])
                else:
                    S.op(DVE, (lambda e, b=b: e.tensor_copy(out=Vsn[0:32, :], in_=PS[b][0:32, 0:256])), reads=[("ps", b)], writes=["Vsn"])
            for r in range(4):
                for bl in range(2):
                    b_ = 2 * p + bl
                    u = r * 4 + b_
                    row0 = r + 512 * b_
                    tiles_ = [("vdram", t_) for t_ in range(4 * b_, 4 * b_ + 4)]
                    S.op(PL, (lambda e, u=u, row0=row0: e.dma_start(out=Vg[:, 1, u, :], in_=vp_o[sl_(row0, 128, 4), :])),
                         reads=tiles_, writes=[("Vg", 1, u)], dma="vg1")
            for r in range(16):
                row0 = r + 1024 * p
                rows = slice(64 * p, 64 * p + 64)
                tiles_ = [("vdram", t_) for t_ in range(8 * p, 8 * p + 8)]
                S.op(PL, (lambda e, r=r, row0=row0, rows=rows: e.dma_start(out=Vg[rows, 2, r, :], in_=vp_o[sl_(row0, 64, 16), :])),
                     reads=tiles_, writes=[("Vg", 2, r)], dma="vg2")
        step(lambda slot: [(w8g(slot, 512), kvw[:, :, :])], fn)
        prefetch_steps()
        emit_norm()
        run_steps()
        S.barrier()

    def attn(p):
        cvn = Carver()
        emit_norm = norm_to_hT(p, 32, cvn)
        cva = Carver()
        TP = TPM if p == 0 else 1024
        QT = cva.bf([128, 6, TPM])
        On = cva.f32([128, 6, TPM])
        Zacc = cva.f32([128, 2, TPM])
        off_mark = cva.off
        bufs = fm_bufs(cva)
        tab = cva.f32([128, 2, 512])
        ropev = rope.rearrange("t p n -> p t n")
        cvb = Carver()
        cvb.off = off_mark
        Pe = [cvb.bf([128, 4, 128]) for _ in range(3)]
        KcH = [cvb.bf([128, 8, 256]) for _ in range(2)]
        VcH = [cvb.bf([128, 8, 256]) for _ in range(2)]
        KTc = [cvb.bf([128, 2, 128]) for _ in range(4)]
        wqv = wq.rearrange("(kt p) c -> p kt c", p=128)
        wov = wo.rearrange("(kt p) c -> p kt c", p=128)
        held = {}

        def fnA(slot, key):
            held["wA"] = w8g(slot, 512); held["kA"] = key

        def fnB(slot, key):
            wA, kA = held["wA"], held["kA"]
            wB = w8g(slot, 256)
            ci = 0
            for (c0, n) in subtiles(p):
                tc0 = tabcols(p, c0, n)
                S.op(SP, (lambda e, tc0=tc0, n=n: e.dma_start(out=tab[:, :, 0:n], in_=ropev[:, :, tc0:tc0 + n])), writes=["f_tab"], dma="ftab")
                for c in range(6):
                    wsrc, wkey, cc = (wA, kA, c) if c < 4 else (wB, key, c - 4)
                    b = newbank()
                    mm_group(PS[b][:, 0:n], [(wsrc[:, kt, cc * 128:(cc + 1) * 128], hT[:, kt, c0:c0 + n]) for kt in range(8)], reads=[wkey] + hkeys(c0, n), writes=[("ps", b)])

                    def writer(t_, t2_, rk, c=c, c0=c0, n=n):
                        S.op(DVE, lambda e: e.tensor_tensor(out=QT[:, c, c0:c0 + n], in0=t_[:, 0:n], in1=t2_[:, 0:n], op=ALU.mult), reads=rk, writes=["QT"])
                    fm_norm_rope(b, n, 63, tab, bufs, ci, writer)
                    ci += 1
        step(lambda slot: [(w8g(slot, 512), wqv[:, :, 0:512])], fnA, hold=1)
        step(lambda slot: [(w8g(slot, 256), wqv[:, :, 512:768])], fnB)

        ACCB = [5, 7]
        NPE = 3

        def acc_views(u):
            bnk = ACCB[u % 2]
            return (bnk, PS[bnk][:, 0:256].rearrange("p (a t) -> p a t", a=2), PS[bnk][:, 256:512].rearrange("p (a t) -> p a t", a=2))

        pe_ctr = [0]

        def stage1(B):
            r0_, r1_ = B["rows"]
            g, qsl, nq, qdims, ktile = B["g"], B["qsl"], B["nq"], B["qdims"], B["ktile"]
            bsx = [newbank(), newbank()]
            sl = pe_ctr[0] % NPE
            pe_ctr[0] += 1
            B["sl"] = sl
            nqq = nq if qdims is None else 24
            for hp in range(2):
                bs = bsx[hp]
                if qdims is None:
                    psS = PS[bs][:, 0:256].rearrange("p (h t) -> p h t", h=2)
                else:
                    psS = PS[bs][:, 0:48].rearrange("p (h g t) -> p h g t", h=2, g=3)

                def fs(e, hp=hp, psS=psS):
                    ins = None
                    for jh in range(2):
                        j = jh * 2 + hp
                        if qdims is None:
                            o_ = psS[r0_:r1_, jh, 0:nq]
                            rhs = QT[hp * 64:hp * 64 + 64, (g * 4 + j) // 2, qsl]
                        else:
                            o_ = psS[r0_:r1_, jh, :, :]
                            rhs = QT[hp * 64:hp * 64 + 64, jh:6:2, qsl]
                        ins = e.matmul(o_, lhsT=ktile(j), rhs=rhs, start=True, stop=True)
                    return ins
                S.op(PE, fs, reads=["QT"] + B["keys"], writes=[("ps", bs)])
            for hp in range(2):
                bs = bsx[hp]
                if qdims is None:
                    sview = PS[bs][r0_:r1_, 0:256].rearrange("p (h t) -> p h t", h=2)[:, :, 0:nq]
                else:
                    sview = PS[bs][r0_:r1_, 0:48].rearrange("p (h t) -> p h t", h=2)
                pview = Pe[sl][r0_:r1_, hp:4:2, 0:nqq]
                S.op(ACT, (lambda e, pview=pview, sview=sview: e.activation(out=pview, in_=sview, func=AF.Exp, scale=0.125)),
                     reads=[("ps", bs)], writes=[("Pe", sl, hp)])
            mask = B["mask"]
            if mask is not None:
                pall = Pe[sl][r0_:r1_, :, 0:nqq]
                S.op(DVE, lambda e: e.tensor_tensor(out=pall, in0=pall, in1=mask.unsqueeze(1).to_broadcast([r1_ - r0_, 4, nqq]), op=ALU.mult),
                     reads=["CB"], writes=[("Pe", sl, 0), ("Pe", sl, 1)])

        def stage2(B):
            r0_, r1_ = B["rows"]
            sl = B["sl"]
            nqq = B["nq"] if B["qdims"] is None else 24
            bnk, accO, accD = acc_views(B["u"])
            Vt = B["Vt"]
            if B["first"]:
                mm_group(PS[bnk][:, :], [(zer[:, 0:128], zer[:, 0:512])], reads=["zer"], writes=[("ps", bnk)])

            def fp(e):
                ins = None
                for j in range(4):
                    hp = j % 2
                    rhs = Pe[sl][r0_:r1_, j, 0:nqq]
                    e.matmul(accO[hp * 64:hp * 64 + 64, j // 2, 0:nqq], lhsT=Vt(j), rhs=rhs, start=False, stop=False, skip_group_check=True)
                    ins = e.matmul(accD[hp * 64:hp * 64 + 64, j // 2, 0:nqq], lhsT=CB[r0_:r1_, C_ONE:C_ONE + 64], rhs=rhs, start=False, stop=False, skip_group_check=True)
                return ins
            S.op(PE, fp, reads=[("Pe", sl, 0), ("Pe", sl, 1), "CB"] + B["keys"], writes=[("ps", bnk)])
            if B["last"]:
                B["evac"](bnk, accO, accD)

        def fn_units(slot, key):
            blocks = []
            units = []
            for b in range(8):
                Bk = 8 * p + b
                kbs = []
                if Bk >= 1:
                    kbs.append((128 * (Bk - 1), 128, 1, (0, Bk - 1), (0, 128), "ge"))
                kbs.append((128 * Bk, 128, 1, (0, Bk), (0, 128), "le"))
                units.append((0, (128 * b, 128, 1), kbs))
            for r in range(4):
                for bl in range(2):
                    b = 2 * p + bl
                    kbs = []
                    if b >= 1:
                        kbs.append((r + 512 * (b - 1), 128, 4, (1, r * 4 + b - 1), (0, 128), "ge"))
                    kbs.append((r + 512 * b, 128, 4, (1, r * 4 + b), (0, 128), "le"))
                    units.append((1, (r + 512 * bl, 128, 4), kbs))
            for r in range(16):
                if p == 0:
                    kbs = [(r, 64, 16, (2, r), (0, 64), "le")]
                else:
                    kbs = [(r, 128, 16, (2, r), (0, 128), "le64")]
                units.append((2, (r, 64, 16), kbs))
            ucount = [0]
            for (g, (q0, nq, qst), kbs) in units:
                qsl = sl_(q0, nq, qst)
                u = ucount[0]
                ucount[0] += 1

                def evac(bnk, accO, accD, g=g, qsl=qsl, nq=nq):
                    S.op(ACT, lambda e: e.copy(out=On[:, 2 * g:2 * g + 2, qsl], in_=accO[:, :, 0:nq]), reads=[("ps", bnk)], writes=["On"])
                    S.op(DVE, lambda e: e.tensor_tensor(out=Zacc[:, :, qsl], in0=Zacc[:, :, qsl], in1=accD[:, :, 0:nq], op=ALU.add), reads=[("ps", bnk), "Zacc"], writes=["Zacc"])
                for bi, (k0, nk, kst, (vg, vu), rows, mk) in enumerate(kbs):
                    ksl = sl_(k0, nk, kst)
                    r0_, r1_ = rows
                    if mk is None:
                        mask = None
                    elif mk == "le":
                        mask = CB[r0_:r1_, C_LE:C_LE + nq]
                    elif mk == "ge":
                        mask = CB[r0_:r1_, C_GE:C_GE + nq]
                    else:
                        mask = CB[:, C_LE + 64:C_LE + 128]
                    blocks.append(dict(g=g, qsl=qsl, nq=nq, qdims=None, rows=rows, mask=mask, u=u,
                                       ktile=(lambda j, ksl=ksl: KT[(j % 2) * 64:(j % 2) * 64 + 64, j // 2, ksl]),
                                       Vt=(lambda j, vg=vg, vu=vu, r0_=r0_, r1_=r1_: Vg[r0_:r1_, vg, vu, j * 64:(j + 1) * 64]),
                                       keys=[("KT", 0), ("KT", 1), ("Vg", vg, vu)],
                                       first=(bi == 0), last=(bi == len(kbs) - 1), evac=evac))
            if p == 0:
                for s_ in range(4):
                    qsl = slice(1024 + 8 * s_, 1024 + 8 * s_ + 8)
                    u = ucount[0]
                    ucount[0] += 1

                    def evac(bnk, accO, accD, qsl=qsl):
                        for jj in range(2):
                            S.op(ACT, (lambda e, jj=jj: e.copy(out=On[:, jj:6:2, qsl], in_=accO[:, jj, 0:24].rearrange("p (g t) -> p g t", g=3))),
                                 reads=[("ps", bnk)], writes=["On"])
                        S.op(DVE, lambda e: e.tensor_reduce(out=Zacc[:, :, qsl], in_=accD[:, :, 0:24].rearrange("p a (g t) -> p a t g", g=3), axis=AX.X, op=ALU.add),
                             reads=[("ps", bnk)], writes=["Zacc"])
                    for rt in range(16):
                        sl = rt % 4
                        hh, ri = rt // 8, rt % 8

                        def pre(sl=sl, hh=hh, ri=ri, s_=s_):
                            if ri == 0:
                                S.op(PL, lambda e: e.dma_start(out=KcH[hh][:, :, :], in_=ck[s_, hh * 1024:(hh + 1) * 1024, :].rearrange("(t p) c -> p t c", p=128)),
                                     writes=[("KcH", hh)], dma=f"kc{hh}")
                                S.op(PL, lambda e: e.dma_start(out=VcH[hh][:, :, :], in_=cv[s_, hh * 1024:(hh + 1) * 1024, :].rearrange("(t p) c -> p t c", p=128)),
                                     writes=[("VcH", hh)], dma=f"vc{hh}")
                            bt = newbank()
                            pv = PS[bt][:, 0:128].bitcast(BF16).rearrange("p (a t) -> p a t", a=2)

                            def ft(e):
                                ins = None
                                for kc in range(2):
                                    ins = e.transpose(out=pv[:, kc, :], in_=KcH[hh][:, ri, kc * 128:(kc + 1) * 128], identity=identb)
                                return ins
                            S.op(PE, ft, reads=[("KcH", hh), "CB"], writes=[("ps", bt)])
                            S.op(DVE, lambda e: e.tensor_copy(out=KTc[sl][:, :, :], in_=pv), reads=[("ps", bt)], writes=[("KTc", sl)])
                        blocks.append(dict(g=0, qsl=qsl, nq=8, qdims=3, rows=(0, 128), mask=CB[:, C_SM + rt * 24:C_SM + rt * 24 + 24], u=u,
                                           ktile=(lambda j, sl=sl: KTc[sl][(j % 2) * 64:(j % 2) * 64 + 64, j // 2, :]),
                                           Vt=(lambda j, hh=hh, ri=ri: VcH[hh][:, ri, j * 64:(j + 1) * 64]),
                                           keys=[("KTc", sl), ("VcH", hh)], first=(rt == 0), last=False, evac=None, pre=pre))
                    blocks.append(dict(g=0, qsl=qsl, nq=8, qdims=3, rows=(0, 32), mask=CB[0:32, C_MN + s_ * 24:C_MN + s_ * 24 + 24], u=u,
                                       ktile=(lambda j: KTs[(j % 2) * 64:(j % 2) * 64 + 64, j // 2, :]),
                                       Vt=(lambda j: Vsn[0:32, j * 64:(j + 1) * 64]),
                                       keys=["KTs", "Vsn"], first=False, last=True, evac=evac))
            D0, D = 2, 2
            nb = len(blocks)
            for i in range(nb + D0 + D):
                if i < nb and blocks[i].get("pre") is not None:
                    blocks[i]["pre"]()
                if 0 <= i - D0 < nb:
                    stage1(blocks[i - D0])
                if 0 <= i - D0 - D < nb:
                    stage2(blocks[i - D0 - D])
            for si, (c0, n) in enumerate(subtiles(p)):
                S.op(ACT, (lambda e, c0=c0, n=n: e.activation(out=Zacc[:, :, c0:c0 + n], in_=Zacc[:, :, c0:c0 + n], func=AF.Ln)), reads=["Zacc"], writes=[("Zr", si)])
                S.op(ACT, (lambda e, c0=c0, n=n: e.activation(out=Zacc[:, :, c0:c0 + n], in_=Zacc[:, :, c0:c0 + n], func=AF.Exp, scale=-1.0)), reads=[("Zr", si)], writes=[("Zr", si)])
                for c in range(6):
                    S.op(DVE, (lambda e, c=c, c0=c0, n=n: e.tensor_tensor(out=QT[:, c, c0:c0 + n], in0=On[:, c, c0:c0 + n], in1=Zacc[:, c % 2, c0:c0 + n], op=ALU.mult)),
                         reads=["On", ("Zr", si)], writes=["QT", ("QTn", si)])
        def fn_units_ring(slot, key):
            S.barrier()
            bank_ring[0] = [0, 1, 2, 3, 4, 6]
            fn_units(slot, key)
            bank_ring[0] = list(range(NB))
        step(None, fn_units_ring)

        for half in range(2):
            def fn_o(slot, key, half=half):
                w = w8g(slot, 512, kt=6)
                for mm in range(4):
                    m = half * 4 + mm
                    for (c0, n) in subtiles(p):
                        b = newbank()
                        mm_group(PS[b][:, 0:n], [(w[:, kt, mm * 128:(mm + 1) * 128], QT[:, kt, c0:c0 + n]) for kt in range(6)], reads=[key, ("QTn", c0 // 512)], writes=[("ps", b)])
                        S.op(DVE, (lambda e, b=b, m=m, c0=c0, n=n: e.tensor_tensor(out=xT[:, m, c0:c0 + n], in0=xT[:, m, c0:c0 + n], in1=PS[b][:, 0:n], op=ALU.add)),
                             reads=[("ps", b)] + xkeys(c0, n), writes=xkeys(c0, n))
            step((lambda slot, half=half: [(w8g(slot, 512, kt=6), wov[:, :, half * 512:(half + 1) * 512])]), fn_o)
        prefetch_steps()
        emit_norm()
        S.barrier(skip=("ws",))
        S.op(DVE, lambda e: e.memset(Zacc[:, :, :], 0.0), writes=["Zacc"])
        run_steps()
        S.barrier()

    x_prefetch(0)
    for p in range(2):
        load_x(p)
        S.barrier()
        ph = phases if phases is not None else ["f1", "gla", "f2", "kv", "f3", "att", "f4"][:stage]
        if "f1" in ph:
            ffn(p, 0, 0)
        if "gla" in ph:
            gla(p)
        if "f2" in ph:
            ffn(p, 0, 1)
        if "kv" in ph:
            kvproj(p)
        if "f3" in ph:
            ffn(p, 1, 0)
        if "att" in ph:
            attn(p)
        if p == 0:
            x_prefetch(1)
        if "f4" in ph:
            ffn(p, 1, 1)
        store_y(p)
        if p == 1:
            S.barrier()
    S.final_wait(SP)
    S.emit()
    return nc


_CACHE = {}


def make_in_maps(inp, ncores=8):
    f = lambda a: np.ascontiguousarray(np.asarray(a, dtype=np.float32))
    small = np.concatenate([f(inp["norm_gains"]).reshape(48, 128), f(inp["kv_norm"]).reshape(8, 128),
                            f(inp["gla_b_gate"]).reshape(4, 128), f(inp["gla_out_norm"]).reshape(2, 128),
                            np.tile(f(inp["k_norm"]).reshape(64), 2)[None, :], np.tile(f(inp["q_norm"]).reshape(64), 2)[None, :]], axis=0)
    knq = np.stack([f(inp["k_norm"]).reshape(64), f(inp["q_norm"]).reshape(64)], axis=0)
    shared = {
        "small": f(small), "knq": f(knq),
        "ffn_w_in": f(inp["ffn_w_in"]), "ffn_w_out": f(inp["ffn_w_out"]),
        "gla_w_in": f(inp["gla_w_in"])[0], "gla_w_gate2": f(inp["gla_w_gate2"])[0], "gla_w_out": f(inp["gla_w_out"])[0],
        "kv_w": f(inp["kv_w"]), "attn_w_q": f(inp["attn_w_q"])[0], "attn_w_out": f(inp["attn_w_out"])[0],
        "consts": build_consts(), "rope": build_rope(),
    }
    maps = []
    for i in range(ncores):
        m = dict(shared)
        m["xp"] = f(inp["x_prompt"][i])
        m["xs"] = f(inp["x_sample"][4 * i:4 * i + 4]).reshape(32, 1024)
        m["st"] = f(inp["state_gla"][0, 4 * i:4 * i + 4]).reshape(16, 128, 256)
        m["ck"] = f(inp["cache_k_win"][4 * i:4 * i + 4]).reshape(4, 2048, 256)
        m["cv"] = f(inp["cache_v_win"][4 * i:4 * i + 4]).reshape(4, 2048, 256)
        maps.append(m)
    return maps


def assemble(results, ncores=8):
    y_p = np.stack([r["y_p"] for r in results], 0)
    y_s = np.concatenate([r["y_s"].reshape(4, 8, 1024) for r in results], 0)
    stp = np.stack([r["stp"] for r in results], 0)[None]
    sts = np.concatenate([r["sts"].reshape(4, 4, 128, 256) for r in results], 0)[None]
    kp = np.stack([r["kp"].reshape(2048, 4, 64) for r in results], 0)
    vp = np.stack([r["vp"].reshape(2048, 4, 64) for r in results], 0)
    ks = np.concatenate([r["ks"].reshape(4, 8, 4, 64) for r in results], 0)
    vs = np.concatenate([r["vs"].reshape(4, 8, 4, 64) for r in results], 0)
    return tuple(np.ascontiguousarray(a.astype(np.float32)) for a in (y_p, y_s, stp, sts, kp, vp, ks, vs))


def kernel(**inputs):
    if "nc" not in _CACHE:
        _CACHE["nc"] = build()
    nc = _CACHE["nc"]
    maps = make_in_maps(inputs, 8)
    res = run_bass_kernel_spmd(nc, maps, core_ids=list(range(8)))
    return assemble(res.results, 8)
```

```python
import os
import numpy as np
import concourse.bass as bass
import concourse.mybir as mybir
from concourse.bass_utils import run_bass_kernel_spmd

F32 = mybir.dt.float32
BF16 = mybir.dt.bfloat16
AF = mybir.ActivationFunctionType
ALU = mybir.AluOpType
AX = mybir.AxisListType
ENGS = ("pe", "act", "dve", "pool", "sp")
EPS = 1e-6
ds = bass.ds


def sl_(start, n, step=1):
    return slice(start, start + (n - 1) * step + 1, step)


class Sched:
    def __init__(self, nc):
        self.nc = nc
        self.sem = {e: nc.alloc_semaphore("prog_" + e) for e in ENGS}
        self.cnt = {e: 0 for e in ENGS}
        self.waited = {e: {} for e in ENGS}
        self.ops = {e: [] for e in ENGS}
        self.lastw = {}
        self.readers = {}
        self.semobj = {self.sem[e].name: self.sem[e] for e in ENGS}
        self.dma_sems = {}

    def _need(self, eng, tok, waits):
        if tok is None:
            return
        sname, val, peng = tok
        if peng == eng and peng == "pe":
            return
        if self.waited[eng].get(sname, 0) >= val:
            return
        waits[sname] = max(waits.get(sname, 0), val)

    def op(self, eng, fn, reads=(), writes=(), dma=None):
        psr = [r for r in reads if isinstance(r, tuple) and r[0] in ("ps", "psbf")]
        if psr:
            reads = [r for r in reads if r not in psr]
            writes = list(writes) + psr
        waits = {}
        for r in reads:
            self._need(eng, self.lastw.get(r), waits)
        for w in writes:
            self._need(eng, self.lastw.get(w), waits)
            for t in self.readers.get(w, ()):
                self._need(eng, t, waits)
        for s, v in waits.items():
            self.waited[eng][s] = v
        if dma is None:
            self.cnt[eng] += 1
            tok = (self.sem[eng].name, self.cnt[eng], eng)
            inc = (self.sem[eng], 1)
        else:
            if dma not in self.dma_sems:
                s = self.nc.alloc_semaphore("dma_" + dma)
                self.dma_sems[dma] = [s, 0]
                self.semobj[s.name] = s
            ent = self.dma_sems[dma]
            ent[1] += 16
            tok = (ent[0].name, ent[1], "dma:" + dma)
            inc = (ent[0], 16)
        self.ops[eng].append((list(waits.items()), fn, inc))
        for r in reads:
            self.readers.setdefault(r, []).append(tok)
        for w in writes:
            self.lastw[w] = tok
            self.readers[w] = []
        return tok

    def barrier(self, skip=("ws", "vg")):
        for e in ENGS:
            waits = []
            for o in ENGS:
                if o != e and self.cnt[o] > self.waited[e].get(self.sem[o].name, 0):
                    waits.append((self.sem[o].name, self.cnt[o]))
                    self.waited[e][self.sem[o].name] = self.cnt[o]
            for name, (sm, c) in self.dma_sems.items():
                if name.startswith(skip):
                    continue
                if c > self.waited[e].get(sm.name, 0):
                    waits.append((sm.name, c))
                    self.waited[e][sm.name] = c
            if waits:
                self.ops[e].append((waits, None, None))

    def final_wait(self, eng="sp"):
        waits = []
        for e in ENGS:
            if e != eng and self.cnt[e] > 0:
                waits.append((self.sem[e].name, self.cnt[e]))
        for name, (s, c) in self.dma_sems.items():
            if c > 0:
                waits.append((s.name, c))
        self.ops[eng].append((waits, None, None))

    def emit(self):
        nc = self.nc
        with nc.Block() as block:
            def mk(ename):
                def body(e):
                    for waits, fn, inc in self.ops[ename]:
                        for s, v in waits:
                            e.wait_ge(self.semobj[s], v)
                        if fn is not None:
                            ins = fn(e)
                            ins.then_inc(inc[0], inc[1])
                return body
            block.tensor(mk("pe"))
            block.scalar(mk("act"))
            block.vector(mk("dve"))
            block.gpsimd(mk("pool"))
            block.sync(mk("sp"))


C_ID, C_LE, C_GE, C_SCP, C_SCS, C_MS, C_SQ, C_SM, C_MN, C_ONE, C_BD, C_RM, C_END = 0, 128, 256, 384, 640, 672, 704, 708, 1116, 1212, 1340, 1468, 1596
DIL = ((128, 1), (512, 4), (2048, 16))


def build_consts():
    c = np.zeros((128, C_END), np.float32)
    r = np.arange(128)
    c[:, C_ID:C_ID + 128] = np.eye(128)
    c[:, C_LE:C_LE + 128] = (r[:, None] <= r[None, :])
    c[:, C_GE:C_GE + 128] = (r[:, None] >= r[None, :])
    c[:, C_SCP:C_SCP + 256] = 1.0
    c[:, C_SCP] = 0.0
    c[:, C_SCP + 128] = 0.0
    c[:, C_SCS:C_SCS + 32] = 1.0
    c[:, C_SCS:C_SCS + 32:8] = 0.0
    t = np.arange(32)
    c[:32, C_MS:C_MS + 32] = ((t[:, None] // 8 == t[None, :] // 8) & (t[:, None] <= t[None, :]))
    for s in range(4):
        c[:32, C_SQ + s] = (t // 8 == s)
    for rt in range(17):
        for g, (win, dil) in enumerate(DIL):
            for tq in range(8):
                col = C_SM + rt * 24 + g * 8 + tq
                if rt < 16:
                    R = rt * 128 + r
                    d = 2048 + tq - R
                    c[:, col] = ((d % dil == 0) & (d >= 0) & (d <= win))
    for s in range(4):
        for g, (win, dil) in enumerate(DIL):
            for tq in range(8):
                col = C_MN + s * 24 + g * 8 + tq
                d = tq - (t % 8)
                c[:32, col] = ((t // 8 == s) & (d >= 0) & (d % dil == 0))
    c[:, C_ONE:C_ONE + 128] = 1.0
    c[:, C_BD:C_BD + 128] = (r[:, None] // 64 == r[None, :] // 64)
    for m in range(128):
        if m % 64 < 32:
            c[m + 32, C_RM + m] = -1.0
        else:
            c[m - 32, C_RM + m] = 1.0
    return c


def build_rope():
    half = 32
    inv = (10000.0 ** (-np.arange(half, dtype=np.float64) / half)).astype(np.float32)
    pos = np.concatenate([np.arange(2048), np.tile(8192 + np.arange(8), 4)]).astype(np.float32)
    ang = (inv[np.arange(128) % 32][:, None] * pos[None, :]).astype(np.float32).astype(np.float64)
    return np.stack([np.cos(ang), np.sin(ang)], axis=0).astype(np.float32)


def build(stage=99, phases=None):
    nc = bass.Bass("TRN2", target_bir_lowering=False)

    def din(name, shape):
        return nc.dram_tensor(name, list(shape), F32, kind="ExternalInput").ap()

    def dout(name, shape):
        return nc.dram_tensor(name, list(shape), F32, kind="ExternalOutput").ap()

    xp = din("xp", [2048, 1024]); xs = din("xs", [32, 1024]); st_in = din("st", [16, 128, 256])
    ck = din("ck", [4, 2048, 256]); cv = din("cv", [4, 2048, 256])
    small = din("small", [64, 128])
    knq = din("knq", [2, 64])
    ffn_in = din("ffn_w_in", [2, 2, 1024, 5376]); ffn_out = din("ffn_w_out", [2, 2, 2688, 1024])
    gla_in = din("gla_w_in", [1024, 3088]); gla_g2 = din("gla_w_gate2", [16, 512]); gla_out = din("gla_w_out", [1024, 1024])
    kv_w = din("kv_w", [1024, 512]); wq = din("attn_w_q", [1024, 768]); wo = din("attn_w_out", [768, 1024])
    consts = din("consts", [128, C_END]); rope = din("rope", [2, 128, 2080])
    y_p = dout("y_p", [2048, 1024]); y_s = dout("y_s", [32, 1024])
    stp_o = dout("stp", [4, 128, 256]); sts_o = dout("sts", [16, 128, 256])
    kp_o = dout("kp", [2048, 256]); vp_o = dout("vp", [2048, 256]); ks_o = dout("ks", [32, 256]); vs_o = dout("vs", [32, 256])

    S = Sched(nc)
    A = nc.alloc_sbuf_tensor
    TPM = 1056
    xT = A("xT", [128, 8, TPM], F32)
    hT = A("hT", [128, 8, TPM], BF16)
    NSLOT = 4
    wsl = A("wsl", [128, NSLOT, 4096], BF16)
    CF = A("CF", [128, C_END], F32)
    CB = A("CB", [128, C_END], BF16)
    G = A("G", [128, 64], F32)
    negb = A("negb", [128, 4], F32)
    KNQ = A("KNQ", [128, 2, 64], F32)
    wg2 = A("wg2", [16, 512], BF16)
    Sst = A("Sst", [128, 4, 256], F32)
    Sbf = A("Sbf", [128, 4, 256], BF16)
    KT = A("KT", [128, 2, 2048], BF16)
    KTs = A("KTs", [128, 2, 32], BF16)
    Vg = A("Vg", [128, 3, 16, 256], BF16)
    Vsn = A("Vsn", [32, 256], BF16)
    zer = A("zer", [128, 512], BF16)
    SCRW = 19100
    scr = A("scr", [128, SCRW], F32)
    PS = [nc.alloc_psum_tensor(f"ps{b}", [128, 512], F32) for b in range(8)]
    NB = 5
    LL = 5
    PBFS = [PS[6][:, :].bitcast(BF16), PS[7][:, :].bitcast(BF16)]
    bank_ctr = [0]

    bank_ring = [list(range(NB))]

    def newbank():
        ring = bank_ring[0]
        b = ring[bank_ctr[0] % len(ring)]
        bank_ctr[0] += 1
        return b

    class Carver:
        def __init__(self):
            self.off = 0

        def f32(self, shape):
            n = int(np.prod(shape[1:]))
            v = scr[:shape[0], self.off:self.off + n]
            self.off += n
            assert self.off <= SCRW, (self.off, SCRW)
            return v if len(shape) == 2 else v.rearrange(_pat(len(shape)), **_dims(shape))

        def bf(self, shape):
            n = int(np.prod(shape[1:]))
            w = (n + 1) // 2
            v = scr[:shape[0], self.off:self.off + w].bitcast(BF16)[:, 0:n]
            self.off += w
            assert self.off <= SCRW, (self.off, SCRW)
            return v if len(shape) == 2 else v.rearrange(_pat(len(shape)), **_dims(shape))

    def _pat(nd):
        names = "abcd"[:nd - 1]
        return "p (" + " ".join(names) + ") -> p " + " ".join(names)

    def _dims(shape):
        names = "abcd"[:len(shape) - 1]
        return {names[i]: int(shape[i + 1]) for i in range(len(shape) - 2)}

    SP, PL, ACT, DVE, PE = "sp", "pool", "act", "dve", "pe"
    S.op(SP, lambda e: e.dma_start(out=CF[:], in_=consts[:, :]), writes=["CF"], dma="c0")
    S.op(PL, lambda e: e.dma_start(out=CB[:], in_=consts[:, :]), writes=["CB"], dma="c1")
    S.op(PL, lambda e: e.dma_start(out=wg2[:], in_=gla_g2[:, :]), writes=["wg2"], dma="c2")
    S.op(SP, lambda e: e.dma_start(out=KNQ[:, 0, :], in_=knq[0, :].partition_broadcast(128)), writes=["KNQ0"], dma="c3")
    S.op(SP, lambda e: e.dma_start(out=KNQ[:, 1, :], in_=knq[1, :].partition_broadcast(128)), writes=["KNQ1"], dma="c3b")
    S.op(PL, lambda e: e.memset(zer[:], 0.0), writes=["zer"])
    S.op(PL, lambda e: e.memset(Sst[:], 0.0), writes=[("Sst", 0), ("Sst", 1)])
    S.op(PL, lambda e: e.memset(Sbf[:], 0.0), writes=[("Sbf", 0), ("Sbf", 1)])
    cv0 = Carver()
    smt = cv0.f32([64, 128])
    S.op(SP, lambda e: e.dma_start(out=smt, in_=small[:, :]), writes=["smt"], dma="c4")
    S.op(PE, lambda e: e.transpose(out=PS[0][:, 0:64], in_=smt, identity=CF[0:64, C_ID:C_ID + 64]),
         reads=["smt", "CF"], writes=[("ps", 0)])
    S.op(DVE, lambda e: e.tensor_copy(out=G[:], in_=PS[0][:, 0:64]), reads=[("ps", 0)], writes=["G"])
    S.op(DVE, lambda e: e.tensor_scalar(out=negb[:], in0=G[:, 56:60], scalar1=-1.0, scalar2=None, op0=ALU.mult),
         reads=["G"], writes=["negb"])
    ident = CF[:, C_ID:C_ID + 128]
    identb = CB[:, C_ID:C_ID + 128]
    onesb = CB[:, C_ONE:C_ONE + 128]
    S.barrier()

    steps = []

    def step(pieces, fn, hold=0):
        steps.append((pieces, fn, hold))

    rs_state = {}

    def _rs_init():
        if "load_idx" in rs_state:
            return
        rs_state["load_idx"] = [i for i, (p, _, _) in enumerate(steps) if p is not None]
        rs_state["holds"] = [steps[i][2] for i in rs_state["load_idx"]]
        rs_state["issued"] = 0

    def _issue_for(q):
        load_idx, holds = rs_state["load_idx"], rs_state["holds"]
        while rs_state["issued"] < len(load_idx):
            k = rs_state["issued"]
            if k > q + NSLOT - 1:
                break
            if k >= NSLOT and (k - NSLOT + holds[k - NSLOT]) >= q:
                break
            sl = (slot_base[0] + k) % NSLOT
            pieces = steps[load_idx[k]][0]
            for pi, (osl, src) in enumerate(pieces(wsl[:, sl, :])):
                S.op(PL, (lambda e, o=osl, s_=src: e.dma_start(out=o, in_=s_)),
                     writes=[("ws", sl)], dma=f"ws{sl}_{pi}")
            rs_state["issued"] += 1

    def prefetch_steps():
        _rs_init()
        _issue_for(0)

    def run_steps():
        _rs_init()
        q = 0
        for i, (p, fn, hold) in enumerate(steps):
            if p is not None:
                _issue_for(q)
                assert rs_state["issued"] > q
                sl = (slot_base[0] + q) % NSLOT
                fn(wsl[:, sl, :], ("ws", sl))
                q += 1
            else:
                fn(None, None)
        slot_base[0] = (slot_base[0] + q) % NSLOT
        steps.clear()
        rs_state.clear()

    slot_base = [0]

    evac_ctr = [0]

    def evac_eng():
        evac_ctr[0] += 1
        return ACT if evac_ctr[0] % 2 else DVE

    def copy_op(eng, out, in_, reads, writes):
        if eng == ACT:
            S.op(ACT, lambda e: e.copy(out=out, in_=in_), reads=reads, writes=writes)
        else:
            S.op(DVE, lambda e: e.tensor_copy(out=out, in_=in_), reads=reads, writes=writes)

    def mm_group(out, pairs, reads, writes, skip=False, start=True, stop=True):
        def fn(e):
            ins = None
            n = len(pairs)
            for i, (l, r) in enumerate(pairs):
                ins = e.matmul(out, lhsT=l, rhs=r, start=(start and i == 0), stop=(stop and i == n - 1),
                               skip_group_check=skip)
            return ins
        S.op(PE, fn, reads=reads, writes=writes)

    def subtiles(p):
        return [(0, 512), (512, 512), (1024, 32)] if p == 0 else [(0, 512), (512, 512)]

    def toktiles(p):
        tl = [(tt * 128, 128, "p", p * 1024 + tt * 128) for tt in range(8)]
        if p == 0:
            tl.append((1024, 32, "s", 0))
        return tl

    XOFF = 14800
    xin = [scr[:, XOFF + i * 1024:XOFF + (i + 1) * 1024] for i in range(4)]
    x_issued = {}

    def x_dma(p, ti):
        if (p, ti) in x_issued:
            return
        tl = toktiles(p)
        if ti >= len(tl):
            return
        x_issued[(p, ti)] = True
        c0, n, kind, r0 = tl[ti]
        sl = ti % 4
        src = xp[r0:r0 + n, :] if kind == "p" else xs[:, :]
        S.op(SP, (lambda e, sl=sl, n=n, src=src: e.dma_start(out=xin[sl][0:n, :], in_=src)),
             writes=[("xin", sl)], dma=f"xin{sl}")

    def x_prefetch(p):
        for ti in range(4):
            x_dma(p, ti)

    def load_x(p):
        for ti, (c0, n, kind, r0) in enumerate(toktiles(p)):
            sl = ti % 4
            x_dma(p, ti)
            for hb in range(2):
                b = newbank()
                pv = PS[b][:, :].rearrange("p (a t) -> p a t", a=4)

                def fn(e, sl=sl, n=n, hb=hb, pv=pv):
                    ins = None
                    for a in range(4):
                        fc = hb * 4 + a
                        ins = e.transpose(out=pv[:, a, 0:n], in_=xin[sl][0:n, fc * 128:(fc + 1) * 128], identity=ident[0:n, 0:n])
                    return ins
                S.op(PE, fn, reads=[("xin", sl), "CF"], writes=[("ps", b)])
                copy_op(evac_eng(), xT[:, hb * 4:hb * 4 + 4, c0:c0 + n], pv[:, :, 0:n], [("ps", b)], [("xT", c0)])
            x_dma(p, ti + 4)

    def store_y(p):
        cvx = Carver()
        yo = [cvx.f32([128, 1024]) for _ in range(4)]
        for ti, (c0, n, kind, r0) in enumerate(toktiles(p)):
            sl = ti % 4
            for hb in range(2):
                b = newbank()
                pv = PS[b][:, :].rearrange("p (a t) -> p a t", a=4)

                def fn(e, n=n, hb=hb, pv=pv, c0=c0):
                    ins = None
                    for a in range(4):
                        fc = hb * 4 + a
                        ins = e.transpose(out=pv[0:n, a, :], in_=xT[:, fc, c0:c0 + n], identity=ident)
                    return ins
                S.op(PE, fn, reads=[("xT", c0 // 128 * 128 if n == 128 else c0), "CF"], writes=[("ps", b)])
                copy_op(evac_eng(), yo[sl][0:n, hb * 512:(hb + 1) * 512].rearrange("p (a t) -> p a t", a=4), pv[0:n, :, :],
                        [("ps", b)], [("yo", sl, hb)])
            dst = y_p[r0:r0 + n, :] if kind == "p" else y_s[:, :]
            S.op(SP, (lambda e, sl=sl, n=n, dst=dst: e.dma_start(out=dst, in_=yo[sl][0:n, :])),
                 reads=[("yo", sl, 0), ("yo", sl, 1)], dma=f"yo{sl}")

    def xkeys(c0, n):
        return [("xT", c) for c in range(c0, c0 + n, 128)] if n >= 128 else [("xT", c0)]

    def hkeys(c0, n):
        return [("hT", c) for c in range(c0, c0 + n, 128)] if n >= 128 else [("hT", c0)]

    def norm_to_hT(p, gcol, cvn, piece=512):
        sqb = cvn.bf([128, 8, piece])
        rs = cvn.f32([128, piece])
        pieces = []
        for (c0, n) in subtiles(p):
            for o in range(0, n, piece):
                pieces.append((c0 + o, min(piece, n - o)))

        def emit():
            for (c0, n) in pieces:
                S.op(ACT, (lambda e, c0=c0, n=n: e.activation(out=sqb[:, :, 0:n], in_=xT[:, :, c0:c0 + n], func=AF.Square)),
                     reads=xkeys(c0, n), writes=["sqb"])
                b = newbank()
                mm_group(PS[b][:, 0:n], [(onesb, sqb[:, fc, 0:n]) for fc in range(8)], reads=["sqb", "CB"], writes=[("ps", b)])
                S.op(ACT, (lambda e, b=b, n=n: e.activation(out=rs[:, 0:n], in_=PS[b][:, 0:n], func=AF.Ln, scale=1.0 / 1024, bias=EPS)),
                     reads=[("ps", b)], writes=["rs"])
                S.op(ACT, (lambda e, n=n: e.activation(out=rs[:, 0:n], in_=rs[:, 0:n], func=AF.Exp, scale=-0.5)), reads=["rs"], writes=["rs"])
                for fc in range(8):
                    S.op(DVE, (lambda e, fc=fc, c0=c0, n=n: e.scalar_tensor_tensor(
                        out=hT[:, fc, c0:c0 + n], in0=xT[:, fc, c0:c0 + n], scalar=G[:, gcol + fc:gcol + fc + 1], in1=rs[:, 0:n],
                        op0=ALU.mult, op1=ALU.mult)), reads=xkeys(c0, n) + ["rs", "G"], writes=hkeys(c0, n))
        return emit

    def ffn(p, l, f):
        cvf = Carver()
        emit_norm = norm_to_hT(p, (l * 3 + (0 if f == 0 else 2)) * 8, cvf)
        act = cvf.bf([128, 21, TPM])
        sg = [cvf.f32([128, 512]) for _ in range(2)]
        w_in = ffn_in[l, f].rearrange("(kt p) c -> p kt c", p=128)
        w_out = ffn_out[l, f].rearrange("(kt p) c -> p kt c", p=128)
        subs = subtiles(p)
        for j in range(21):
            def pieces(slot, j=j):
                v = slot[:, 0:2048].rearrange("p (kt c) -> p kt c", kt=8)
                return [(v[:, :, 0:128], w_in[:, :, j * 128:(j + 1) * 128]),
                        (v[:, :, 128:256], w_in[:, :, 2688 + j * 128:2688 + (j + 1) * 128])]

            def fn(slot, key, j=j):
                v = slot[:, 0:2048].rearrange("p (kt c) -> p kt c", kt=8)
                for si, (c0, n) in enumerate(subs):
                    bg, bu = newbank(), newbank()
                    mm_group(PS[bg][:, 0:n], [(v[:, kt, 0:128], hT[:, kt, c0:c0 + n]) for kt in range(8)],
                             reads=[key] + hkeys(c0, n), writes=[("ps", bg)])
                    mm_group(PS[bu][:, 0:n], [(v[:, kt, 128:256], hT[:, kt, c0:c0 + n]) for kt in range(8)],
                             reads=[key] + hkeys(c0, n), writes=[("ps", bu)])
                    sl = (j * 3 + si) % 2
                    S.op(ACT, (lambda e, bg=bg, n=n, sl=sl: e.activation(out=sg[sl][:, 0:n], in_=PS[bg][:, 0:n], func=AF.Silu)),
                         reads=[("ps", bg)], writes=[("sg", sl)])
                    S.op(DVE, (lambda e, bu=bu, n=n, sl=sl, j=j, c0=c0: e.tensor_tensor(
                        out=act[:, j, c0:c0 + n], in0=sg[sl][:, 0:n], in1=PS[bu][:, 0:n], op=ALU.mult)),
                        reads=[("sg", sl), ("ps", bu)], writes=[("act", j, c0)])
            step(pieces, fn)
        for m in range(8):
            def pieces(slot, m=m):
                v = slot[:, 0:2688].rearrange("p (kt c) -> p kt c", kt=21)
                return [(v, w_out[:, :, m * 128:(m + 1) * 128])]

            def fn(slot, key, m=m):
                v = slot[:, 0:2688].rearrange("p (kt c) -> p kt c", kt=21)
                for (c0, n) in subs:
                    b = newbank()
                    mm_group(PS[b][:, 0:n], [(v[:, kt, :], act[:, kt, c0:c0 + n]) for kt in range(21)],
                             reads=[key] + [("act", kt, c0) for kt in range(21)], writes=[("ps", b)])
                    S.op(DVE, (lambda e, b=b, n=n, m=m, c0=c0: e.scalar_tensor_tensor(
                        out=xT[:, m, c0:c0 + n], in0=PS[b][:, 0:n], scalar=0.5, in1=xT[:, m, c0:c0 + n],
                        op0=ALU.mult, op1=ALU.add)), reads=[("ps", b)] + xkeys(c0, n), writes=xkeys(c0, n))
            step(pieces, fn)
        prefetch_steps()
        emit_norm()
        run_steps()
        S.barrier()

    QSC = float(128 ** -0.5)

    def gla(p):
        cvn = Carver()
        emit_norm = norm_to_hT(p, 1 * 8, cvn, piece=256)
        NOFF = cvn.off
        gin = gla_in.rearrange("(kt p) c -> p kt c", p=128)
        gout = gla_out.rearrange("(kt p) c -> p kt c", p=128)
        tiles = [(0, 512, "p"), (512, 512, "p")]
        if p == 0:
            tiles.append((1024, 32, "s"))

        def w8(slot, ncol):
            return slot[:, 0:8 * ncol].rearrange("p (kt c) -> p kt c", kt=8)

        def ld(col0, ncol, src=None):
            src = gin if src is None else src
            return lambda slot: [(w8(slot, ncol), src[:, :, col0:col0 + ncol])]

        for (c0, n, kind) in tiles:
            ntt = n // 128 if kind == "p" else 1
            ntok = 128 if kind == "p" else 32
            hk = hkeys(c0, n)
            T = n
            cv = Carver()
            cv.off = NOFF
            oT = cv.f32([128, 8, T])
            qT = cv.bf([128, 4, T]); kT = cv.bf([128, 4, T]); kdT = cv.bf([128, 4, T])
            EQ = cv.f32([128, 4, T])
            tA = cv.f32([128, 4, T]); tB = cv.f32([128, 4, T])
            EK = tB
            glr = cv.bf([16, T])
            vv = cv.bf([128, ntt, 1024])
            ktok = cv.bf([128, ntt, 4, 128])
            Am = cv.bf([128, ntt, 4, 128])
            if kind == "s":
                kms = cv.bf([32, 4, 128])
                S0 = [cv.f32([128, 4, 256]) for _ in range(4)]
                S0b = [cv.bf([128, 4, 256]) for _ in range(4)]
                Snew = cv.f32([128, 4, 256])

                def prefetch_states(S0=S0, S0b=S0b):
                    for s_ in range(4):
                        S.op(SP, (lambda e, s_=s_: e.dma_start(out=S0[s_][:, :, :], in_=st_in[s_ * 4:(s_ + 1) * 4].rearrange("h p d -> p h d"))),
                             writes=[("S0", s_)], dma=f"s0_{s_}")
                        S.op(PL, (lambda e, s_=s_: e.dma_start(out=S0b[s_][:, :, :], in_=st_in[s_ * 4:(s_ + 1) * 4].rearrange("h p d -> p h d"))),
                             writes=[("S0b", s_)], dma=f"s0b_{s_}")
            else:
                prefetch_states = None

            def fn_g(slot, key, c0=c0, n=n, hk=hk, kind=kind, glr=glr, tA=tA, tB=tB, EQ=EQ, EK=EK, prefetch_states=prefetch_states):
                if prefetch_states is not None:
                    prefetch_states()
                w = w8(slot, 16)
                b = newbank()
                mm_group(PS[b][0:16, 0:n], [(w[:, kt, :], hT[:, kt, c0:c0 + n]) for kt in range(8)], reads=[key] + hk, writes=[("ps", b)])
                S.op(ACT, lambda e: e.copy(out=glr[0:16, 0:n], in_=PS[b][0:16, 0:n]), reads=[("ps", b)], writes=["glr"])
                for h in range(4):
                    b2 = newbank()
                    mm_group(PS[b2][:, 0:n], [(wg2[0:16, h * 128:(h + 1) * 128], glr[0:16, 0:n])], reads=["glr", "wg2"], writes=[("ps", b2)])
                    S.op(ACT, (lambda e, b2=b2, h=h: e.activation(out=tA[:, h, 0:n], in_=PS[b2][:, 0:n], func=AF.Exp, bias=negb[:, h:h + 1], scale=-1.0)),
                         reads=[("ps", b2), "negb"], writes=[("tA", h)])
                S.op(ACT, lambda e: e.activation(out=tA[:, :, 0:n], in_=tA[:, :, 0:n], func=AF.Ln, bias=1.0),
                     reads=[("tA", h) for h in range(4)], writes=[("tA", h) for h in range(4)])
                for h in range(4):
                    if kind == "p":
                        for q2 in range(n // 256):
                            cs2 = slice(q2 * 256, (q2 + 1) * 256)
                            S.op(DVE, (lambda e, h=h, cs2=cs2: e.tensor_tensor_scan(out=tB[:, h, cs2], data0=CF[:, C_SCP:C_SCP + 256], data1=tA[:, h, cs2], initial=0.0, op0=ALU.mult, op1=ALU.add)),
                                 reads=[("tA", h), "CF"], writes=[("tB", h)])
                    else:
                        S.op(DVE, (lambda e, h=h: e.tensor_tensor_scan(out=tB[:, h, 0:n], data0=CF[:, C_SCS:C_SCS + 32], data1=tA[:, h, 0:n], initial=0.0, op0=ALU.mult, op1=ALU.add)),
                             reads=[("tA", h), "CF"], writes=[("tB", h)])
                S.op(ACT, lambda e: e.activation(out=EQ[:, :, 0:n], in_=tB[:, :, 0:n], func=AF.Exp, scale=-1.0 / 16),
                     reads=[("tB", h) for h in range(4)], writes=["EQ"])
                S.op(ACT, lambda e: e.activation(out=EK[:, :, 0:n], in_=tB[:, :, 0:n], func=AF.Exp, scale=1.0 / 16),
                     reads=["EQ"], writes=["EK"] + [("tB", h) for h in range(4)])
            step(ld(2048, 16), fn_g)

            for half in range(2):
                def fn_v(slot, key, half=half, c0=c0, hk=hk, ntt=ntt, ntok=ntok, vv=vv):
                    w = w8(slot, 512)
                    for tt in range(ntt):
                        b = newbank()
                        mm_group(PS[b][0:ntok, :], [(hT[:, kt, c0 + tt * 128:c0 + tt * 128 + ntok], w[:, kt, :]) for kt in range(8)], reads=[key] + hk, writes=[("ps", b)])
                        copy_op(evac_eng(), vv[0:ntok, tt, half * 512:(half + 1) * 512], PS[b][0:ntok, :], [("ps", b)], [("vv", tt, half)])
                step(ld(1024 + half * 512, 512), fn_v)

            def fn_q(slot, key, c0=c0, n=n, hk=hk, qT=qT, EQ=EQ):
                w = w8(slot, 512)
                for h in range(4):
                    b = newbank()
                    mm_group(PS[b][:, 0:n], [(w[:, kt, h * 128:(h + 1) * 128], hT[:, kt, c0:c0 + n]) for kt in range(8)], reads=[key] + hk, writes=[("ps", b)])
                    S.op(DVE, (lambda e, b=b, h=h: e.scalar_tensor_tensor(out=qT[:, h, 0:n], in0=PS[b][:, 0:n], scalar=QSC, in1=EQ[:, h, 0:n], op0=ALU.mult, op1=ALU.mult)),
                         reads=[("ps", b), "EQ"], writes=[("qT", h)])
            step(ld(0, 512), fn_q)

            def fn_k(slot, key, c0=c0, n=n, hk=hk, ntt=ntt, ntok=ntok, kind=kind, kT=kT, kdT=kdT, EK=EK, EQ=EQ, ktok=ktok):
                w = w8(slot, 512)
                for h in range(4):
                    b = newbank()
                    mm_group(PS[b][:, 0:n], [(w[:, kt, h * 128:(h + 1) * 128], hT[:, kt, c0:c0 + n]) for kt in range(8)], reads=[key] + hk, writes=[("ps", b)])
                    S.op(DVE, (lambda e, b=b, h=h: e.tensor_tensor(out=kT[:, h, 0:n], in0=PS[b][:, 0:n], in1=EK[:, h, 0:n], op=ALU.mult)),
                         reads=[("ps", b), "EK"], writes=[("kT", h)])
                segw = 128 if kind == "p" else 8
                nseg = n // segw
                S.op(DVE, lambda e: e.tensor_tensor(out=kdT[:, :, 0:n].rearrange("p h (s w) -> p h s w", w=segw),
                                                    in0=kT[:, :, 0:n].rearrange("p h (s w) -> p h s w", w=segw),
                                                    in1=EQ[:, :, segw - 1:n:segw].unsqueeze(3).to_broadcast([128, 4, nseg, segw]), op=ALU.mult),
                     reads=[("kT", h) for h in range(4)] + ["EQ"], writes=[("kdT", h) for h in range(4)])
                for tt in range(ntt):
                    pv = PBFS[tt % 2][:, 0:512].rearrange("p (h d) -> p h d", h=4)

                    def fnt(e, tt=tt, pv=pv):
                        ins = None
                        for h in range(4):
                            ins = e.transpose(out=pv[0:ntok, h, :], in_=kdT[:, h, tt * 128:tt * 128 + ntok], identity=identb)
                        return ins
                    S.op(PE, fnt, reads=[("kdT", h) for h in range(4)] + ["CB"], writes=[("psbf", tt % 2)])
                    copy_op(evac_eng(), ktok[0:ntok, tt, :, :], pv[0:ntok, :, :], [("psbf", tt % 2)], [("ktok", tt)])
            step(ld(512, 512), fn_k)

            def fn_rec(slot, key, c0=c0, n=n, kind=kind, ntt=ntt, qT=qT, kT=kT, vv=vv, ktok=ktok, Am=Am, oT=oT, EQ=EQ):
                qk_keys = [("kT", h) for h in range(4)] + [("qT", h) for h in range(4)]
                if kind == "p":
                    for tt in range(ntt):
                        cs = slice(tt * 128, (tt + 1) * 128)
                        ba = newbank()
                        pa = PS[ba][:, :].rearrange("p (h t) -> p h t", h=4)

                        def fa(e, cs=cs, pa=pa):
                            ins = None
                            for h in range(4):
                                ins = e.matmul(pa[:, h, :], lhsT=kT[:, h, cs], rhs=qT[:, h, cs], start=True, stop=True)
                            return ins
                        S.op(PE, fa, reads=qk_keys, writes=[("ps", ba)])
                        S.op(DVE, (lambda e, pa=pa, tt=tt: e.tensor_tensor(out=Am[:, tt, :, :], in0=pa, in1=CB[:, C_LE:C_LE + 128].unsqueeze(1).to_broadcast([128, 4, 128]), op=ALU.mult)),
                             reads=[("ps", ba), "CB"], writes=[("Am", tt)])
                    for tt in range(ntt):
                        cs = slice(tt * 128, (tt + 1) * 128)
                        bks = []
                        for hp in range(2):
                            bk = newbank()
                            pk = PS[bk][:, :].rearrange("p (a d) -> p a d", a=2)
                            bks.append((bk, pk))

                            def fk(e, hp=hp, pk=pk, tt=tt):
                                ins = None
                                for hh in range(2):
                                    h = hp * 2 + hh
                                    ins = e.matmul(pk[:, hh, :], lhsT=ktok[:, tt, h, :], rhs=vv[:, tt, h * 256:(h + 1) * 256], start=True, stop=True)
                                return ins
                            S.op(PE, fk, reads=[("ktok", tt), ("vv", tt, 0), ("vv", tt, 1)], writes=[("ps", bk)])
                        for hp in range(2):
                            bo = newbank()
                            po = PS[bo][:, :].rearrange("p (a t) -> p a t", a=4)

                            def fo(e, hp=hp, po=po, tt=tt, cs=cs):
                                ins = None
                                for hh in range(2):
                                    h = hp * 2 + hh
                                    for half in range(2):
                                        o_ = po[:, hh * 2 + half, :]
                                        e.matmul(o_, lhsT=vv[:, tt, h * 256 + half * 128:h * 256 + (half + 1) * 128], rhs=Am[:, tt, h, :], start=True, stop=False)
                                        ins = e.matmul(o_, lhsT=Sbf[:, h, half * 128:(half + 1) * 128], rhs=qT[:, h, cs], start=False, stop=True)
                                return ins
                            S.op(PE, fo, reads=[("Am", tt), ("Sbf", hp), ("vv", tt, 0), ("vv", tt, 1)] + [("qT", h) for h in range(4)], writes=[("ps", bo)])
                            copy_op(ACT, oT[:, hp * 4:(hp + 1) * 4, cs], po, [("ps", bo)], [("oT", hp)])
                        col = tt * 128 + 127
                        for hp in range(2):
                            bk, pk = bks[hp]
                            for hh in range(2):
                                h = hp * 2 + hh
                                S.op(DVE, (lambda e, h=h, hh=hh, pk=pk, col=col: e.scalar_tensor_tensor(out=Sst[:, h, :], in0=Sst[:, h, :], scalar=EQ[:, h, col:col + 1], in1=pk[:, hh, :], op0=ALU.mult, op1=ALU.add)),
                                     reads=[("ps", bk), "EQ"], writes=[("Sst", hp)])
                            S.op(ACT, (lambda e, hp=hp: e.copy(out=Sbf[:, hp * 2:hp * 2 + 2, :], in_=Sst[:, hp * 2:hp * 2 + 2, :])), reads=[("Sst", hp)], writes=[("Sbf", hp)])
                else:
                    ba = newbank()
                    pa = PS[ba][:, :].rearrange("p (h t) -> p h t", h=4)

                    def fa(e, pa=pa):
                        ins = None
                        for h in range(4):
                            ins = e.matmul(pa[0:32, h, 0:32], lhsT=kT[:, h, 0:32], rhs=qT[:, h, 0:32], start=True, stop=True)
                        return ins
                    S.op(PE, fa, reads=qk_keys, writes=[("ps", ba)])
                    S.op(DVE, (lambda e, pa=pa: e.tensor_tensor(out=Am[0:32, 0, :, 0:32], in0=pa[0:32, :, 0:32], in1=CB[0:32, C_MS:C_MS + 32].unsqueeze(1).to_broadcast([32, 4, 32]), op=ALU.mult)),
                         reads=[("ps", ba), "CB"], writes=[("Am", 0)])
                    bo = LL
                    po = PS[bo][:, 0:256].rearrange("p (a t) -> p a t", a=8)
                    for s_ in range(4):
                        sl = s_
                        c8 = slice(s_ * 8, s_ * 8 + 8)

                        def fo(e, s_=s_, sl=sl, c8=c8):
                            ins = None
                            for h in range(4):
                                for half in range(2):
                                    o_ = po[:, h * 2 + half, c8]
                                    e.matmul(o_, lhsT=vv[0:32, 0, h * 256 + half * 128:h * 256 + (half + 1) * 128], rhs=Am[0:32, 0, h, c8], start=True, stop=False, skip_group_check=True)
                                    ins = e.matmul(o_, lhsT=S0b[sl][:, h, half * 128:(half + 1) * 128], rhs=qT[:, h, c8], start=False, stop=True, skip_group_check=True)
                            return ins
                        S.op(PE, fo, reads=[("Am", 0), ("S0b", sl), ("vv", 0, 0), ("vv", 0, 1)] + [("qT", h) for h in range(4)], writes=[("ps", bo)])
                        S.op(DVE, (lambda e, s_=s_: e.tensor_scalar(out=kms[:, :, :], in0=ktok[0:32, 0, :, :], scalar1=CF[0:32, C_SQ + s_:C_SQ + s_ + 1], scalar2=None, op0=ALU.mult)),
                             reads=[("ktok", 0), "CF"], writes=["kms"])
                        col = s_ * 8 + 7
                        for hp in range(2):
                            bk = newbank()
                            pk = PS[bk][:, :].rearrange("p (a d) -> p a d", a=2)

                            def fk(e, hp=hp, pk=pk):
                                ins = None
                                for hh in range(2):
                                    h = hp * 2 + hh
                                    ins = e.matmul(pk[:, hh, :], lhsT=kms[0:32, h, :], rhs=vv[0:32, 0, h * 256:(h + 1) * 256], start=True, stop=True)
                                return ins
                            S.op(PE, fk, reads=["kms", ("vv", 0, 0), ("vv", 0, 1)], writes=[("ps", bk)])
                            for hh in range(2):
                                h = hp * 2 + hh
                                S.op(DVE, (lambda e, h=h, hh=hh, pk=pk, sl=sl, col=col: e.scalar_tensor_tensor(out=Snew[:, h, :], in0=S0[sl][:, h, :], scalar=EQ[:, h, col:col + 1], in1=pk[:, hh, :], op0=ALU.mult, op1=ALU.add)),
                                     reads=[("ps", bk), ("S0", sl), "EQ"], writes=[("Snew", hp)])
                        S.op(SP, (lambda e, s_=s_: e.dma_start(out=sts_o[s_ * 4:(s_ + 1) * 4].rearrange("h p d -> p h d"), in_=Snew[:, :, :])),
                             reads=[("Snew", 0), ("Snew", 1)], dma="sts")
                    copy_op(ACT, oT[:, :, 0:32], po, [("ps", bo)], [("oT", 0), ("oT", 1)])
                S.barrier()
            step(None, fn_rec)

            cvB = Carver()
            cvB.off = NOFF
            oT_B = cvB.f32([128, 8, T])
            sqo = [cvB.bf([128, 2, T]) for _ in range(2)]
            RS = cvB.f32([128, 4, T])
            sr = [cvB.f32([128, T]) for _ in range(2)]; t1 = [cvB.f32([128, T]) for _ in range(2)]
            uT = cvB.bf([128, 8, T])

            for half in range(2):
                def fn_r(slot, key, half=half, c0=c0, n=n, hk=hk, oT=oT_B, sqo=sqo, RS=RS, sr=sr, t1=t1, uT=uT):
                    w = w8(slot, 512)
                    if half == 0:
                        for h in range(4):
                            S.op(ACT, (lambda e, h=h: e.activation(out=sqo[h % 2][:, :, 0:n], in_=oT[:, 2 * h:2 * h + 2, 0:n], func=AF.Square)),
                                 reads=[("oT", h // 2)], writes=[("sqo", h % 2)])
                            b = newbank()
                            mm_group(PS[b][:, 0:n], [(onesb, sqo[h % 2][:, 0, 0:n]), (onesb, sqo[h % 2][:, 1, 0:n])], reads=[("sqo", h % 2), "CB"], writes=[("ps", b)])
                            S.op(ACT, (lambda e, b=b, h=h: e.activation(out=RS[:, h, 0:n], in_=PS[b][:, 0:n], func=AF.Ln, scale=1.0 / 256, bias=EPS)),
                                 reads=[("ps", b)], writes=[("RS", h)])
                            S.op(ACT, (lambda e, h=h: e.activation(out=RS[:, h, 0:n], in_=RS[:, h, 0:n], func=AF.Exp, scale=-0.5)), reads=[("RS", h)], writes=[("RS", h)])
                    for cc in range(4):
                        c = half * 4 + cc
                        h = c // 2
                        b = newbank()
                        mm_group(PS[b][:, 0:n], [(w[:, kt, cc * 128:(cc + 1) * 128], hT[:, kt, c0:c0 + n]) for kt in range(8)], reads=[key] + hk, writes=[("ps", b)])
                        sl = c % 2
                        S.op(ACT, (lambda e, b=b, sl=sl: e.activation(out=sr[sl][:, 0:n], in_=PS[b][:, 0:n], func=AF.Silu)), reads=[("ps", b)], writes=[("sr", sl)])
                        S.op(DVE, (lambda e, c=c, h=h, sl=sl: e.tensor_tensor(out=t1[sl][:, 0:n], in0=oT[:, c, 0:n], in1=RS[:, h, 0:n], op=ALU.mult)),
                             reads=[("oT", c // 4), ("RS", h)], writes=[("t1", sl)])
                        S.op(DVE, (lambda e, c=c, sl=sl: e.scalar_tensor_tensor(out=uT[:, c, 0:n], in0=t1[sl][:, 0:n], scalar=G[:, 60 + (c % 2):61 + (c % 2)], in1=sr[sl][:, 0:n], op0=ALU.mult, op1=ALU.mult)),
                             reads=[("t1", sl), ("sr", sl), "G"], writes=[("uT", c)])
                step(ld(2064 + half * 512, 512), fn_r)

            for half in range(2):
                def fn_o(slot, key, half=half, c0=c0, n=n, uT=uT):
                    w = w8(slot, 512)
                    for mm in range(4):
                        m = half * 4 + mm
                        b = newbank()
                        mm_group(PS[b][:, 0:n], [(w[:, kt, mm * 128:(mm + 1) * 128], uT[:, kt, 0:n]) for kt in range(8)], reads=[key] + [("uT", c) for c in range(8)], writes=[("ps", b)])
                        S.op(DVE, (lambda e, b=b, m=m: e.tensor_tensor(out=xT[:, m, c0:c0 + n], in0=xT[:, m, c0:c0 + n], in1=PS[b][:, 0:n], op=ALU.add)),
                             reads=[("ps", b)] + xkeys(c0, n), writes=xkeys(c0, n))
                    if half == 1:
                        S.barrier()
                step(ld(half * 512, 512, gout), fn_o)
        prefetch_steps()
        emit_norm()
        run_steps()
        if p == 1:
            S.op(SP, lambda e: e.dma_start(out=stp_o.rearrange("h p d -> p h d"), in_=Sst[:, :, :]), reads=[("Sst", 0), ("Sst", 1)], dma="stp")
        S.barrier()

    def w8g(slot, ncol, kt=8):
        return slot[:, 0:kt * ncol].rearrange("p (kt c) -> p kt c", kt=kt)

    def rope_ops(src3, dst3, cst, n, nh, ta, tb, rk, wk):
        cos = cst[0:n, 0:32].unsqueeze(1).to_broadcast([n, nh, 32])
        sin = cst[0:n, 32:64].unsqueeze(1).to_broadcast([n, nh, 32])
        a = src3[:, :, 0:32]; b_ = src3[:, :, 32:64]
        S.op(DVE, lambda e: e.tensor_tensor(out=ta[0:n], in0=a, in1=cos, op=ALU.mult), reads=rk, writes=["ta"])
        S.op(DVE, lambda e: e.tensor_tensor(out=tb[0:n], in0=b_, in1=sin, op=ALU.mult), reads=rk, writes=["tb"])
        S.op(DVE, lambda e: e.tensor_tensor(out=dst3[:, :, 0:32], in0=ta[0:n], in1=tb[0:n], op=ALU.subtract), reads=["ta", "tb"], writes=wk)
        S.op(DVE, lambda e: e.tensor_tensor(out=ta[0:n], in0=a, in1=sin, op=ALU.mult), reads=rk, writes=["ta"])
        S.op(DVE, lambda e: e.tensor_tensor(out=tb[0:n], in0=b_, in1=cos, op=ALU.mult), reads=rk, writes=["tb"])
        S.op(DVE, lambda e: e.tensor_tensor(out=dst3[:, :, 32:64], in0=ta[0:n], in1=tb[0:n], op=ALU.add), reads=["ta", "tb"], writes=wk)

    def head_norm(src, dst, n, nh, gain, sq, ss, rk, wk):
        W_ = nh * 64
        S.op(DVE, lambda e: e.tensor_tensor(out=sq[0:n, 0:W_], in0=src[0:n, 0:W_], in1=src[0:n, 0:W_], op=ALU.mult), reads=rk, writes=["sq"])
        S.op(DVE, lambda e: e.tensor_reduce(out=ss[0:n, 0:nh], in_=sq[0:n, 0:W_].rearrange("p (h d) -> p h d", h=nh), axis=AX.X, op=ALU.add), reads=["sq"], writes=["ss"])
        S.op(ACT, lambda e: e.activation(out=ss[0:n, 0:nh], in_=ss[0:n, 0:nh], func=AF.Sqrt, scale=1.0 / 64, bias=EPS), reads=["ss"], writes=["ss"])
        S.op(DVE, lambda e: e.reciprocal(out=ss[0:n, 0:nh], in_=ss[0:n, 0:nh]), reads=["ss"], writes=["ss"])
        s3 = src[0:n, 0:W_].rearrange("p (h d) -> p h d", h=nh)
        d3 = dst[0:n, 0:W_].rearrange("p (h d) -> p h d", h=nh)
        S.op(DVE, lambda e: e.tensor_tensor(out=d3, in0=s3, in1=ss[0:n, 0:nh].unsqueeze(2).to_broadcast([n, nh, 64]), op=ALU.mult), reads=rk + ["ss"], writes=wk)
        S.op(DVE, lambda e: e.tensor_tensor(out=d3, in0=d3, in1=gain[0:n, :].unsqueeze(1).to_broadcast([n, nh, 64]), op=ALU.mult), reads=wk + ["KNQ0", "KNQ1"], writes=wk)

    def fm_norm_rope(ps_b, n, gcol, tab, bufs, idx, writer):
        sqb, rsb, qn32, qnb, tt, tt2 = bufs
        sl = idx % NFM
        S.op(ACT, lambda e: e.activation(out=sqb[sl][:, 0:n], in_=PS[ps_b][:, 0:n], func=AF.Square), reads=[("ps", ps_b)], writes=[("f_sqb", sl)])
        S.op(ACT, lambda e: e.activation(out=qnb[sl][:, 0:n], in_=PS[ps_b][:, 0:n], func=AF.Copy, scale=G[:, gcol:gcol + 1]), reads=[("ps", ps_b), "G"], writes=[("f_qnb", sl)])
        S.op(DVE, lambda e: e.scalar_tensor_tensor(out=tt[sl][:, 0:n], in0=PS[ps_b][:, 0:n], scalar=G[:, gcol:gcol + 1], in1=tab[:, 0, 0:n], op0=ALU.mult, op1=ALU.mult),
             reads=[("ps", ps_b), "f_tab", "G"], writes=[("f_t", sl)])
        b2 = newbank()
        mm_group(PS[b2][:, 0:n], [(CB[:, C_BD:C_BD + 128], sqb[sl][:, 0:n])], reads=[("f_sqb", sl), "CB"], writes=[("ps", b2)])
        b3 = newbank()
        mm_group(PS[b3][:, 0:n], [(CB[:, C_RM:C_RM + 128], qnb[sl][:, 0:n])], reads=[("f_qnb", sl), "CB"], writes=[("ps", b3)])
        S.op(ACT, lambda e: e.activation(out=rsb[sl][:, 0:n], in_=PS[b2][:, 0:n], func=AF.Ln, scale=1.0 / 64, bias=EPS), reads=[("ps", b2)], writes=[("f_rs", sl)])
        S.op(ACT, lambda e: e.activation(out=rsb[sl][:, 0:n], in_=rsb[sl][:, 0:n], func=AF.Exp, scale=-0.5), reads=[("f_rs", sl)], writes=[("f_rs", sl)])
        S.op(DVE, lambda e: e.tensor_tensor(out=tt2[sl][:, 0:n], in0=PS[b3][:, 0:n], in1=tab[:, 1, 0:n], op=ALU.mult), reads=[("ps", b3), "f_tab"], writes=[("f_t2", sl)])
        S.op(DVE, lambda e: e.tensor_tensor(out=tt[sl][:, 0:n], in0=tt[sl][:, 0:n], in1=tt2[sl][:, 0:n], op=ALU.add), reads=[("f_t", sl), ("f_t2", sl)], writes=[("f_t", sl)])
        writer(tt[sl], rsb[sl], [("f_t", sl), ("f_rs", sl)])

    NFM = 3

    def fm_bufs(cv):
        return ([cv.bf([128, 512]) for _ in range(NFM)], [cv.f32([128, 512]) for _ in range(NFM)], None,
                [cv.bf([128, 512]) for _ in range(NFM)], [cv.f32([128, 512]) for _ in range(NFM)], [cv.f32([128, 512]) for _ in range(NFM)])

    def tabcols(p, c0, n):
        return (p * 1024 + c0) if c0 < 1024 else 2048


    def kvproj(p):
        cvk = Carver()
        emit_norm = norm_to_hT(p, 48, cvk, piece=256)
        bufs = fm_bufs(cvk)
        tab = cvk.f32([128, 2, 512])
        kfm = [cvk.f32([128, 512]) for _ in range(2)]
        vf = [cvk.f32([128, 256]) for _ in range(2)]
        ko2 = [cvk.f32([128, 4, 256]) for _ in range(2)]
        kvw = kv_w.rearrange("(kt p) c -> p kt c", p=128)
        allh = hkeys(0, 1024)
        ropev = rope.rearrange("t p n -> p t n")
        cnt = [0]

        def fn(slot, key):
            w = w8g(slot, 512)
            for (c0, n) in subtiles(p):
                tc0 = tabcols(p, c0, n)
                S.op(SP, (lambda e, tc0=tc0, n=n: e.dma_start(out=tab[:, :, 0:n], in_=ropev[:, :, tc0:tc0 + n])), writes=["f_tab"], dma="ftab")
                for kc in range(2):
                    b = newbank()
                    mm_group(PS[b][:, 0:n], [(w[:, kt, kc * 128:(kc + 1) * 128], hT[:, kt, c0:c0 + n]) for kt in range(8)], reads=[key] + hkeys(c0, n), writes=[("ps", b)])
                    ci = cnt[0]
                    cnt[0] += 1

                    def writer(t_, t2_, rk, kc=kc, c0=c0, n=n, ci=ci):
                        S.op(DVE, lambda e: e.tensor_tensor(out=kfm[ci % 2][:, 0:n], in0=t_[:, 0:n], in1=t2_[:, 0:n], op=ALU.mult), reads=rk, writes=[("kfm", ci % 2)])
                        if c0 < 1024:
                            S.op(ACT, lambda e: e.copy(out=KT[:, kc, p * 1024 + c0:p * 1024 + c0 + n], in_=kfm[ci % 2][:, 0:n]), reads=[("kfm", ci % 2)], writes=[("KT", p)])
                        else:
                            S.op(ACT, lambda e: e.copy(out=KTs[:, kc, 0:n], in_=kfm[ci % 2][:, 0:n]), reads=[("kfm", ci % 2)], writes=["KTs"])
                        ntl = (n + 127) // 128
                        for t4 in range(0, ntl, 4):
                            bt = newbank()
                            nt4 = min(4, ntl - t4)
                            pv = PS[bt][:, :].rearrange("p (a t) -> p a t", a=4)
                            rows = min(128, n)

                            def ft(e, t4=t4, nt4=nt4, pv=pv, rows=rows):
                                ins = None
                                for a in range(nt4):
                                    ins = e.transpose(out=pv[0:rows, a, :], in_=kfm[ci % 2][:, (t4 + a) * 128:(t4 + a) * 128 + rows], identity=ident)
                                return ins
                            S.op(PE, ft, reads=[("kfm", ci % 2), "CF"], writes=[("ps", bt)])
                            sb = (c0 // 512) % 2
                            for a in range(nt4):
                                S.op(DVE if a % 2 else ACT, (lambda e, a=a, sb=sb, pv=pv, rows=rows, kc=kc, t4=t4: (e.tensor_copy if a % 2 else e.copy)(out=ko2[sb][0:rows, t4 + a, kc * 128:(kc + 1) * 128], in_=pv[0:rows, a, :])),
                                     reads=[("ps", bt)], writes=[("ko2", sb, kc)])
                            if kc == 1:
                                if c0 < 1024:
                                    r0 = p * 1024 + c0
                                    dstk = kp_o[r0:r0 + 512, :].rearrange("(t q) c -> q t c", q=128)
                                    S.op(SP, (lambda e, sb=sb, dstk=dstk: e.dma_start(out=dstk, in_=ko2[sb][:, 0:4, :])), reads=[("ko2", sb, 0), ("ko2", sb, 1)], dma=f"ko{sb}")
                                else:
                                    S.op(SP, (lambda e, sb=sb: e.dma_start(out=ks_o[:, :], in_=ko2[sb][0:32, 0, :])), reads=[("ko2", sb, 0), ("ko2", sb, 1)], dma=f"ko{sb}")
                    fm_norm_rope(b, n, 62, tab, bufs, ci, writer)
            for ti, (c0, n, kind, r0) in enumerate(toktiles(p)):
                sl = ti % 2
                b = newbank()
                mm_group(PS[b][0:n, 0:256], [(hT[:, kt, c0:c0 + n], w[:, kt, 256:512]) for kt in range(8)], reads=[key] + hkeys(c0, n), writes=[("ps", b)])
                S.op(ACT, (lambda e, sl=sl, n=n, b=b: e.copy(out=vf[sl][0:n, :], in_=PS[b][0:n, 0:256])), reads=[("ps", b)], writes=[("vf", sl)])
                dstv = vp_o[r0:r0 + n, :] if kind == "p" else vs_o[:, :]
                S.op(SP, (lambda e, sl=sl, n=n, dstv=dstv: e.dma_start(out=dstv, in_=vf[sl][0:n, :])), reads=[("vf", sl)],
                     writes=([("vdram", r0 // 128)] if kind == "p" else []), dma=f"vf{sl}")
                if kind == "p":
                    u = r0 // 128
                    S.op(DVE, (lambda e, u=u, b=b: e.tensor_copy(out=Vg[:, 0, u, :], in_=PS[b][:, 0:256])), reads=[("ps", b)], writes=[("Vg", 0, u)])
                else:
                    S.op(DVE, (lambda e, b=b: e.tensor_copy(out=Vsn[0:32, :], in_=PS[b][0:32, 0:256])), reads=[("ps", b)], writes=["Vsn"])
            for r in range(4):
                for bl in range(2):
                    b_ = 2 * p + bl
                    u = r * 4 + b_
                    row0 = r + 512 * b_
                    tiles_ = [("vdram", t_) for t_ in range(4 * b_, 4 * b_ + 4)]
                    S.op(PL, (lambda e, u=u, row0=row0: e.dma_start(out=Vg[:, 1, u, :], in_=vp_o[sl_(row0, 128, 4), :])),
                         reads=tiles_, writes=[("Vg", 1, u)], dma="vg1")
            for r in range(16):
                row0 = r + 1024 * p
                rows = slice(64 * p, 64 * p + 64)
                tiles_ = [("vdram", t_) for t_ in range(8 * p, 8 * p + 8)]
                S.op(PL, (lambda e, r=r, row0=row0, rows=rows: e.dma_start(out=Vg[rows, 2, r, :], in_=vp_o[sl_(row0, 64, 16), :])),
                     reads=tiles_, writes=[("Vg", 2, r)], dma="vg2")
        step(lambda slot: [(w8g(slot, 512), kvw[:, :, :])], fn)
        prefetch_steps()
        emit_norm()
        run_steps()
        S.barrier()

    def attn(p):
        cvn = Carver()
        emit_norm = norm_to_hT(p, 32, cvn)
        cva = Carver()
        TP = TPM if p == 0 else 1024
        QT = cva.bf([128, 6, TPM])
        On = cva.f32([128, 6, TPM])
        Zacc = cva.f32([128, 2, TPM])
        off_mark = cva.off
        bufs = fm_bufs(cva)
        tab = cva.f32([128, 2, 512])
        ropev = rope.rearrange("t p n -> p t n")
        cvb = Carver()
        cvb.off = off_mark
        Pe = [cvb.bf([128, 4, 128]) for _ in range(4)]
        KcH = [cvb.bf([128, 8, 256]) for _ in range(2)]
        VcH = [cvb.bf([128, 8, 256]) for _ in range(2)]
        KTc = [cvb.bf([128, 2, 128]) for _ in range(4)]
        wqv = wq.rearrange("(kt p) c -> p kt c", p=128)
        wov = wo.rearrange("(kt p) c -> p kt c", p=128)
        held = {}

        def fnA(slot, key):
            held["wA"] = w8g(slot, 512); held["kA"] = key

        def fnB(slot, key):
            wA, kA = held["wA"], held["kA"]
            wB = w8g(slot, 256)
            ci = 0
            for (c0, n) in subtiles(p):
                tc0 = tabcols(p, c0, n)
                S.op(SP, (lambda e, tc0=tc0, n=n: e.dma_start(out=tab[:, :, 0:n], in_=ropev[:, :, tc0:tc0 + n])), writes=["f_tab"], dma="ftab")
                for c in range(6):
                    wsrc, wkey, cc = (wA, kA, c) if c < 4 else (wB, key, c - 4)
                    b = newbank()
                    mm_group(PS[b][:, 0:n], [(wsrc[:, kt, cc * 128:(cc + 1) * 128], hT[:, kt, c0:c0 + n]) for kt in range(8)], reads=[wkey] + hkeys(c0, n), writes=[("ps", b)])

                    def writer(t_, t2_, rk, c=c, c0=c0, n=n):
                        S.op(DVE, lambda e: e.tensor_tensor(out=QT[:, c, c0:c0 + n], in0=t_[:, 0:n], in1=t2_[:, 0:n], op=ALU.mult), reads=rk, writes=["QT"])
                    fm_norm_rope(b, n, 63, tab, bufs, ci, writer)
                    ci += 1
        step(lambda slot: [(w8g(slot, 512), wqv[:, :, 0:512])], fnA, hold=1)
        step(lambda slot: [(w8g(slot, 256), wqv[:, :, 512:768])], fnB)

        ACCB = [5, 7]
        NPE = 4

        def acc_views(u):
            bnk = ACCB[u % 2]
            return (bnk, PS[bnk][:, 0:256].rearrange("p (a t) -> p a t", a=2), PS[bnk][:, 256:512].rearrange("p (a t) -> p a t", a=2))

        pe_ctr = [0]

        def stage1(B):
            r0_, r1_ = B["rows"]
            g, qsl, nq, qdims, ktile = B["g"], B["qsl"], B["nq"], B["qdims"], B["ktile"]
            bsx = [newbank(), newbank()]
            sl = pe_ctr[0] % NPE
            pe_ctr[0] += 1
            B["sl"] = sl
            nqq = nq if qdims is None else 24
            for hp in range(2):
                bs = bsx[hp]
                if qdims is None:
                    psS = PS[bs][:, 0:256].rearrange("p (h t) -> p h t", h=2)
                else:
                    psS = PS[bs][:, 0:48].rearrange("p (h g t) -> p h g t", h=2, g=3)

                def fs(e, hp=hp, psS=psS):
                    ins = None
                    for jh in range(2):
                        j = jh * 2 + hp
                        if qdims is None:
                            o_ = psS[r0_:r1_, jh, 0:nq]
                            rhs = QT[hp * 64:hp * 64 + 64, (g * 4 + j) // 2, qsl]
                        else:
                            o_ = psS[r0_:r1_, jh, :, :]
                            rhs = QT[hp * 64:hp * 64 + 64, jh:6:2, qsl]
                        ins = e.matmul(o_, lhsT=ktile(j), rhs=rhs, start=True, stop=True)
                    return ins
                S.op(PE, fs, reads=["QT"] + B["keys"], writes=[("ps", bs)])
            for hp in range(2):
                bs = bsx[hp]
                if qdims is None:
                    sview = PS[bs][r0_:r1_, 0:256].rearrange("p (h t) -> p h t", h=2)[:, :, 0:nq]
                else:
                    sview = PS[bs][r0_:r1_, 0:48].rearrange("p (h t) -> p h t", h=2)
                pview = Pe[sl][r0_:r1_, hp:4:2, 0:nqq]
                S.op(ACT, (lambda e, pview=pview, sview=sview: e.activation(out=pview, in_=sview, func=AF.Exp, scale=0.125)),
                     reads=[("ps", bs)], writes=[("Pe", sl, hp)])
            mask = B["mask"]
            if mask is not None:
                pall = Pe[sl][r0_:r1_, :, 0:nqq]
                S.op(DVE, lambda e: e.tensor_tensor(out=pall, in0=pall, in1=mask.unsqueeze(1).to_broadcast([r1_ - r0_, 4, nqq]), op=ALU.mult),
                     reads=["CB"], writes=[("Pe", sl, 0), ("Pe", sl, 1)])

        def stage2(B):
            r0_, r1_ = B["rows"]
            sl = B["sl"]
            nqq = B["nq"] if B["qdims"] is None else 24
            bnk, accO, accD = acc_views(B["u"])
            Vt = B["Vt"]
            if B["first"]:
                mm_group(PS[bnk][:, :], [(zer[:, 0:128], zer[:, 0:512])], reads=["zer"], writes=[("ps", bnk)])

            def fp(e):
                ins = None
                for j in range(4):
                    hp = j % 2
                    rhs = Pe[sl][r0_:r1_, j, 0:nqq]
                    e.matmul(accO[hp * 64:hp * 64 + 64, j // 2, 0:nqq], lhsT=Vt(j), rhs=rhs, start=False, stop=False, skip_group_check=True)
                    ins = e.matmul(accD[hp * 64:hp * 64 + 64, j // 2, 0:nqq], lhsT=CB[r0_:r1_, C_ONE:C_ONE + 64], rhs=rhs, start=False, stop=False, skip_group_check=True)
                return ins
            S.op(PE, fp, reads=[("Pe", sl, 0), ("Pe", sl, 1), "CB"] + B["keys"], writes=[("ps", bnk)])
            if B["last"]:
                B["evac"](bnk, accO, accD)

        def fn_units(slot, key):
            blocks = []
            units = []
            for b in range(8):
                Bk = 8 * p + b
                kbs = []
                if Bk >= 1:
                    kbs.append((128 * (Bk - 1), 128, 1, (0, Bk - 1), (0, 128), "ge"))
                kbs.append((128 * Bk, 128, 1, (0, Bk), (0, 128), "le"))
                units.append((0, (128 * b, 128, 1), kbs))
            for r in range(4):
                for bl in range(2):
                    b = 2 * p + bl
                    kbs = []
                    if b >= 1:
                        kbs.append((r + 512 * (b - 1), 128, 4, (1, r * 4 + b - 1), (0, 128), "ge"))
                    kbs.append((r + 512 * b, 128, 4, (1, r * 4 + b), (0, 128), "le"))
                    units.append((1, (r + 512 * bl, 128, 4), kbs))
            for r in range(16):
                if p == 0:
                    kbs = [(r, 64, 16, (2, r), (0, 64), "le")]
                else:
                    kbs = [(r, 128, 16, (2, r), (0, 128), "le64")]
                units.append((2, (r, 64, 16), kbs))
            ucount = [0]
            for (g, (q0, nq, qst), kbs) in units:
                qsl = sl_(q0, nq, qst)
                u = ucount[0]
                ucount[0] += 1

                def evac(bnk, accO, accD, g=g, qsl=qsl, nq=nq):
                    S.op(ACT, lambda e: e.copy(out=On[:, 2 * g:2 * g + 2, qsl], in_=accO[:, :, 0:nq]), reads=[("ps", bnk)], writes=["On"])
                    S.op(DVE, lambda e: e.tensor_tensor(out=Zacc[:, :, qsl], in0=Zacc[:, :, qsl], in1=accD[:, :, 0:nq], op=ALU.add), reads=[("ps", bnk), "Zacc"], writes=["Zacc"])
                for bi, (k0, nk, kst, (vg, vu), rows, mk) in enumerate(kbs):
                    ksl = sl_(k0, nk, kst)
                    r0_, r1_ = rows
                    if mk is None:
                        mask = None
                    elif mk == "le":
                        mask = CB[r0_:r1_, C_LE:C_LE + nq]
                    elif mk == "ge":
                        mask = CB[r0_:r1_, C_GE:C_GE + nq]
                    else:
                        mask = CB[:, C_LE + 64:C_LE + 128]
                    blocks.append(dict(g=g, qsl=qsl, nq=nq, qdims=None, rows=rows, mask=mask, u=u,
                                       ktile=(lambda j, ksl=ksl: KT[(j % 2) * 64:(j % 2) * 64 + 64, j // 2, ksl]),
                                       Vt=(lambda j, vg=vg, vu=vu, r0_=r0_, r1_=r1_: Vg[r0_:r1_, vg, vu, j * 64:(j + 1) * 64]),
                                       keys=[("KT", 0), ("KT", 1), ("Vg", vg, vu)],
                                       first=(bi == 0), last=(bi == len(kbs) - 1), evac=evac))
            if p == 0:
                for s_ in range(4):
                    qsl = slice(1024 + 8 * s_, 1024 + 8 * s_ + 8)
                    u = ucount[0]
                    ucount[0] += 1

                    def evac(bnk, accO, accD, qsl=qsl):
                        for jj in range(2):
                            S.op(ACT, (lambda e, jj=jj: e.copy(out=On[:, jj:6:2, qsl], in_=accO[:, jj, 0:24].rearrange("p (g t) -> p g t", g=3))),
                                 reads=[("ps", bnk)], writes=["On"])
                        S.op(DVE, lambda e: e.tensor_reduce(out=Zacc[:, :, qsl], in_=accD[:, :, 0:24].rearrange("p a (g t) -> p a t g", g=3), axis=AX.X, op=ALU.add),
                             reads=[("ps", bnk)], writes=["Zacc"])
                    for rt in range(16):
                        sl = rt % 4
                        hh, ri = rt // 8, rt % 8

                        def pre(sl=sl, hh=hh, ri=ri, s_=s_):
                            if ri == 0:
                                S.op(PL, lambda e: e.dma_start(out=KcH[hh][:, :, :], in_=ck[s_, hh * 1024:(hh + 1) * 1024, :].rearrange("(t p) c -> p t c", p=128)),
                                     writes=[("KcH", hh)], dma=f"kc{hh}")
                                S.op(PL, lambda e: e.dma_start(out=VcH[hh][:, :, :], in_=cv[s_, hh * 1024:(hh + 1) * 1024, :].rearrange("(t p) c -> p t c", p=128)),
                                     writes=[("VcH", hh)], dma=f"vc{hh}")
                            bt = newbank()
                            pv = PS[bt][:, 0:128].bitcast(BF16).rearrange("p (a t) -> p a t", a=2)

                            def ft(e):
                                ins = None
                                for kc in range(2):
                                    ins = e.transpose(out=pv[:, kc, :], in_=KcH[hh][:, ri, kc * 128:(kc + 1) * 128], identity=identb)
                                return ins
                            S.op(PE, ft, reads=[("KcH", hh), "CB"], writes=[("ps", bt)])
                            S.op(DVE, lambda e: e.tensor_copy(out=KTc[sl][:, :, :], in_=pv), reads=[("ps", bt)], writes=[("KTc", sl)])
                        blocks.append(dict(g=0, qsl=qsl, nq=8, qdims=3, rows=(0, 128), mask=CB[:, C_SM + rt * 24:C_SM + rt * 24 + 24], u=u,
                                           ktile=(lambda j, sl=sl: KTc[sl][(j % 2) * 64:(j % 2) * 64 + 64, j // 2, :]),
                                           Vt=(lambda j, hh=hh, ri=ri: VcH[hh][:, ri, j * 64:(j + 1) * 64]),
                                           keys=[("KTc", sl), ("VcH", hh)], first=(rt == 0), last=False, evac=None, pre=pre))
                    blocks.append(dict(g=0, qsl=qsl, nq=8, qdims=3, rows=(0, 32), mask=CB[0:32, C_MN + s_ * 24:C_MN + s_ * 24 + 24], u=u,
                                       ktile=(lambda j: KTs[(j % 2) * 64:(j % 2) * 64 + 64, j // 2, :]),
                                       Vt=(lambda j: Vsn[0:32, j * 64:(j + 1) * 64]),
                                       keys=["KTs", "Vsn"], first=False, last=True, evac=evac))
            D0, D = 2, 3
            nb = len(blocks)
            for i in range(nb + D0 + D):
                if i < nb and blocks[i].get("pre") is not None:
                    blocks[i]["pre"]()
                if 0 <= i - D0 < nb:
                    stage1(blocks[i - D0])
                if 0 <= i - D0 - D < nb:
                    stage2(blocks[i - D0 - D])
            for si, (c0, n) in enumerate(subtiles(p)):
                S.op(ACT, (lambda e, c0=c0, n=n: e.activation(out=Zacc[:, :, c0:c0 + n], in_=Zacc[:, :, c0:c0 + n], func=AF.Ln)), reads=["Zacc"], writes=[("Zr", si)])
                S.op(ACT, (lambda e, c0=c0, n=n: e.activation(out=Zacc[:, :, c0:c0 + n], in_=Zacc[:, :, c0:c0 + n], func=AF.Exp, scale=-1.0)), reads=[("Zr", si)], writes=[("Zr", si)])
                for c in range(6):
                    S.op(DVE, (lambda e, c=c, c0=c0, n=n: e.tensor_tensor(out=QT[:, c, c0:c0 + n], in0=On[:, c, c0:c0 + n], in1=Zacc[:, c % 2, c0:c0 + n], op=ALU.mult)),
                         reads=["On", ("Zr", si)], writes=["QT", ("QTn", si)])
        def fn_units_ring(slot, key):
            S.barrier()
            bank_ring[0] = [0, 1, 2, 3, 4, 6]
            fn_units(slot, key)
            bank_ring[0] = list(range(NB))
        step(None, fn_units_ring)

        for half in range(2):
            def fn_o(slot, key, half=half):
                w = w8g(slot, 512, kt=6)
                for mm in range(4):
                    m = half * 4 + mm
                    for (c0, n) in subtiles(p):
                        b = newbank()
                        mm_group(PS[b][:, 0:n], [(w[:, kt, mm * 128:(mm + 1) * 128], QT[:, kt, c0:c0 + n]) for kt in range(6)], reads=[key, ("QTn", c0 // 512)], writes=[("ps", b)])
                        S.op(DVE, (lambda e, b=b, m=m, c0=c0, n=n: e.tensor_tensor(out=xT[:, m, c0:c0 + n], in0=xT[:, m, c0:c0 + n], in1=PS[b][:, 0:n], op=ALU.add)),
                             reads=[("ps", b)] + xkeys(c0, n), writes=xkeys(c0, n))
            step((lambda slot, half=half: [(w8g(slot, 512, kt=6), wov[:, :, half * 512:(half + 1) * 512])]), fn_o)
        prefetch_steps()
        emit_norm()
        S.barrier(skip=("ws",))
        S.op(DVE, lambda e: e.memset(Zacc[:, :, :], 0.0), writes=["Zacc"])
        run_steps()
        S.barrier()

    x_prefetch(0)
    for p in range(2):
        load_x(p)
        S.barrier()
        ph = phases if phases is not None else ["f1", "gla", "f2", "kv", "f3", "att", "f4"][:stage]
        if "f1" in ph:
            ffn(p, 0, 0)
        if "gla" in ph:
            gla(p)
        if "f2" in ph:
            ffn(p, 0, 1)
        if "kv" in ph:
            kvproj(p)
        if "f3" in ph:
            ffn(p, 1, 0)
        if "att" in ph:
            attn(p)
        if p == 0:
            x_prefetch(1)
        if "f4" in ph:
            ffn(p, 1, 1)
        store_y(p)
        if p == 1:
            S.barrier()
    S.final_wait(SP)
    S.emit()
    return nc


_CACHE = {}


def make_in_maps(inp, ncores=8):
    f = lambda a: np.ascontiguousarray(np.asarray(a, dtype=np.float32))
    small = np.concatenate([f(inp["norm_gains"]).reshape(48, 128), f(inp["kv_norm"]).reshape(8, 128),
                            f(inp["gla_b_gate"]).reshape(4, 128), f(inp["gla_out_norm"]).reshape(2, 128),
                            np.tile(f(inp["k_norm"]).reshape(64), 2)[None, :], np.tile(f(inp["q_norm"]).reshape(64), 2)[None, :]], axis=0)
    knq = np.stack([f(inp["k_norm"]).reshape(64), f(inp["q_norm"]).reshape(64)], axis=0)
    shared = {
        "small": f(small), "knq": f(knq),
        "ffn_w_in": f(inp["ffn_w_in"]), "ffn_w_out": f(inp["ffn_w_out"]),
        "gla_w_in": f(inp["gla_w_in"])[0], "gla_w_gate2": f(inp["gla_w_gate2"])[0], "gla_w_out": f(inp["gla_w_out"])[0],
        "kv_w": f(inp["kv_w"]), "attn_w_q": f(inp["attn_w_q"])[0], "attn_w_out": f(inp["attn_w_out"])[0],
        "consts": build_consts(), "rope": build_rope(),
    }
    maps = []
    for i in range(ncores):
        m = dict(shared)
        m["xp"] = f(inp["x_prompt"][i])
        m["xs"] = f(inp["x_sample"][4 * i:4 * i + 4]).reshape(32, 1024)
        m["st"] = f(inp["state_gla"][0, 4 * i:4 * i + 4]).reshape(16, 128, 256)
        m["ck"] = f(inp["cache_k_win"][4 * i:4 * i + 4]).reshape(4, 2048, 256)
        m["cv"] = f(inp["cache_v_win"][4 * i:4 * i + 4]).reshape(4, 2048, 256)
        maps.append(m)
    return maps


def assemble(results, ncores=8):
    y_p = np.stack([r["y_p"] for r in results], 0)
    y_s = np.concatenate([r["y_s"].reshape(4, 8, 1024) for r in results], 0)
    stp = np.stack([r["stp"] for r in results], 0)[None]
    sts = np.concatenate([r["sts"].reshape(4, 4, 128, 256) for r in results], 0)[None]
    kp = np.stack([r["kp"].reshape(2048, 4, 64) for r in results], 0)
    vp = np.stack([r["vp"].reshape(2048, 4, 64) for r in results], 0)
    ks = np.concatenate([r["ks"].reshape(4, 8, 4, 64) for r in results], 0)
    vs = np.concatenate([r["vs"].reshape(4, 8, 4, 64) for r in results], 0)
    return tuple(np.ascontiguousarray(a.astype(np.float32)) for a in (y_p, y_s, stp, sts, kp, vp, ks, vs))


def kernel(**inputs):
    if "nc" not in _CACHE:
        _CACHE["nc"] = build()
    nc = _CACHE["nc"]
    maps = make_in_maps(inputs, 8)
    res = run_bass_kernel_spmd(nc, maps, core_ids=list(range(8)))
    return assemble(res.results, 8)
```
